# Optimizing a Trainium2 kernel written in Bass

```python
import math
import jax, jax.numpy as jnp
from jax import lax
import numpy as np

D_MODEL = 1024
BATCH = 16
SEQ = 4096
DEPTH = 2

GRID_W = 64
CTX_LEN = 256
CONV_WIDTH = 512
CONV_K = 31
N_HEADS = 8
QK_DIM = 64
V_DIM = 2 * QK_DIM
ATTN_WIDTH = N_HEADS * V_DIM
QK_COLS = N_HEADS * 2 * QK_DIM
D_FF = 4 * D_MODEL
ROPE_BASE = 10000.0
N_FREQ_AXIS = QK_DIM // 4
Q_BLOCK = 128
EPS = 1e-6

CONV_OFF = 0
Q_OFF = CONV_OFF + 2 * CONV_WIDTH
K_OFF = Q_OFF + QK_COLS
V_OFF = K_OFF + QK_COLS
GATE_OFF = V_OFF + ATTN_WIDTH
IN_COLS = GATE_OFF + 2 * D_MODEL

kernel_name = "hybrid_conformer_diffattn_dit_block"


def rmsnorm(x, g):
    xf = x.astype(jnp.float32)
    y = xf * lax.rsqrt(jnp.mean(xf * xf, axis=-1, keepdims=True) + EPS)
    return (y * g.astype(jnp.float32)).astype(x.dtype)


def layernorm(x, g, b):
    xf = x.astype(jnp.float32)
    mu = jnp.mean(xf, axis=-1, keepdims=True)
    var = jnp.mean(jnp.square(xf - mu), axis=-1, keepdims=True)
    y = (xf - mu) * lax.rsqrt(var + EPS)
    return (y * g.astype(jnp.float32) + b.astype(jnp.float32)).astype(x.dtype)


def modulate(h, shift, scale):
    return h * (1 + scale) + shift


def axial_rope(n_tokens):
    rows = n_tokens // GRID_W
    row = jnp.repeat(jnp.arange(rows, dtype=jnp.float32), GRID_W)
    col = jnp.tile(jnp.arange(GRID_W, dtype=jnp.float32), rows)
    inv = jnp.power(ROPE_BASE, -jnp.arange(N_FREQ_AXIS, dtype=jnp.float32) / N_FREQ_AXIS)
    ang = jnp.concatenate([row[:, None] * inv, col[:, None] * inv], axis=-1)
    return jnp.cos(ang)[:, None, None, :], jnp.sin(ang)[:, None, None, :]


def apply_rope(t, cos, sin):
    cos = cos.astype(t.dtype)
    sin = sin.astype(t.dtype)
    half = QK_DIM // 2
    t1, t2 = t[..., :half], t[..., half:]
    return jnp.concatenate([t1 * cos - t2 * sin, t2 * cos + t1 * sin], axis=-1)


def diff_attention(q, keys, vals, lam, sub_g, lam_init):
    B, L = q.shape[:2]
    nb = L // Q_BLOCK
    scale = QK_DIM ** -0.5
    qb = q.reshape(B, nb, Q_BLOCK, N_HEADS, 2, QK_DIM).transpose(1, 0, 2, 3, 4, 5)

    def block(qi):
        s = jnp.einsum('bqhcd,bkhcd->bhcqk', qi, keys,
                       preferred_element_type=jnp.float32) * scale
        p = jax.nn.softmax(s, axis=-1)
        w = p[:, :, 0] - lam * p[:, :, 1]
        return jnp.einsum('bhqk,bkhv->bqhv', w.astype(vals.dtype), vals)

    out = lax.map(block, qb)
    out = out.transpose(1, 0, 2, 3, 4).reshape(B, L, N_HEADS, V_DIM)
    out = rmsnorm(out, sub_g) * (1.0 - lam_init)
    return out.reshape(B, L, ATTN_WIDTH)


def conformer_conv(u, w_dw, b_dw, ln_g, ln_b):
    a, g = u[..., :CONV_WIDTH], u[..., CONV_WIDTH:]
    z = a * jax.nn.sigmoid(g)
    pad = CONV_K // 2
    z = lax.conv_general_dilated(z, w_dw[:, None, :].astype(z.dtype), window_strides=(1,),
                                 padding=[(pad, pad)], dimension_numbers=('NWC', 'WIO', 'NWC'),
                                 feature_group_count=CONV_WIDTH) + b_dw
    return jax.nn.silu(layernorm(z, ln_g, ln_b))


def merge_branches(conv_in, attn, gate_logits, w_dw, b_dw, ln_g, ln_b,
                   w_conv_out, w_attn_out, b_gate, w_out):
    y_conv = conformer_conv(conv_in, w_dw, b_dw, ln_g, ln_b) @ w_conv_out
    y_attn = attn @ w_attn_out
    g = jax.nn.sigmoid(gate_logits + b_gate)
    g_conv, g_attn = g[..., :D_MODEL], g[..., D_MODEL:]
    return (g_conv * y_conv + g_attn * y_attn) @ w_out


def sqrelu_mlp(h, w1, w2):
    return jnp.square(jax.nn.relu(h @ w1)) @ w2


def setup_inputs(seed: int = 0) -> dict:
    key = jax.random.key(seed)
    ks = jax.random.split(key, 26)
    f = jnp.float32
    nrm = lambda k, shape, s: (jax.random.normal(k, shape, f) * s).astype(f)
    return {
        "x": nrm(ks[0], (BATCH, SEQ, D_MODEL), 1.0),
        "c": nrm(ks[1], (BATCH, D_MODEL), 1.0),
        "ctx": nrm(ks[2], (BATCH, CTX_LEN, D_MODEL), 1.0),
        "c_ctx": nrm(ks[3], (D_MODEL,), 1.0),
        "w_ada": nrm(ks[4], (DEPTH, D_MODEL, 6 * D_MODEL), 0.5 * D_MODEL ** -0.5),
        "b_ada": nrm(ks[5], (DEPTH, 6 * D_MODEL), 0.02),
        "norm1_g": 1.0 + nrm(ks[6], (DEPTH, D_MODEL), 0.02),
        "w_in": nrm(ks[7], (DEPTH, D_MODEL, IN_COLS), D_MODEL ** -0.5),
        "b_gate": nrm(ks[8], (DEPTH, 2 * D_MODEL), 0.02),
        "q_norm_g": 1.0 + nrm(ks[9], (DEPTH, QK_DIM), 0.02),
        "k_norm_g": 1.0 + nrm(ks[10], (DEPTH, QK_DIM), 0.02),
        "lam_q": nrm(ks[11], (DEPTH, 2, QK_DIM), 0.1),
        "lam_k": nrm(ks[12], (DEPTH, 2, QK_DIM), 0.1),
        "attn_norm_g": 1.0 + nrm(ks[13], (DEPTH, V_DIM), 0.02),
        "w_dw": nrm(ks[14], (DEPTH, CONV_K, CONV_WIDTH), CONV_K ** -0.5),
        "b_dw": nrm(ks[15], (DEPTH, CONV_WIDTH), 0.02),
        "conv_ln_g": 1.0 + nrm(ks[16], (DEPTH, CONV_WIDTH), 0.02),
        "conv_ln_b": nrm(ks[17], (DEPTH, CONV_WIDTH), 0.02),
        "w_conv_out": nrm(ks[18], (DEPTH, CONV_WIDTH, D_MODEL), CONV_WIDTH ** -0.5),
        "w_attn_out": nrm(ks[19], (DEPTH, ATTN_WIDTH, D_MODEL), ATTN_WIDTH ** -0.5),
        "w_out": nrm(ks[20], (DEPTH, D_MODEL, D_MODEL), D_MODEL ** -0.5),
        "norm2_g": 1.0 + nrm(ks[21], (DEPTH, D_MODEL), 0.02),
        "w_mlp1": nrm(ks[22], (DEPTH, D_MODEL, D_FF), D_MODEL ** -0.5),
        "w_mlp2": nrm(ks[23], (DEPTH, D_FF, D_MODEL), D_FF ** -0.5),
    }


def reference(x, c, ctx, c_ctx, w_ada, b_ada, norm1_g, w_in, b_gate, q_norm_g, k_norm_g,
              lam_q, lam_k, attn_norm_g, w_dw, b_dw, conv_ln_g, conv_ln_b, w_conv_out,
              w_attn_out, w_out, norm2_g, w_mlp1, w_mlp2):
    B, S, _ = x.shape
    C = ctx.shape[1]
    cos, sin = axial_rope(S)
    silu_c = jax.nn.silu(c)
    silu_cc = jax.nn.silu(c_ctx)
    h_ctx = ctx

    for i in range(DEPTH):
        last = i == DEPTH - 1
        lam_init = 0.8 - 0.6 * math.exp(-0.3 * i)
        lq = lam_q[i].astype(jnp.float32)
        lk = lam_k[i].astype(jnp.float32)
        lam = jnp.exp(jnp.sum(lq[0] * lk[0])) - jnp.exp(jnp.sum(lq[1] * lk[1])) + lam_init

        mx = (silu_c @ w_ada[i] + b_ada[i])[:, None, :]
        mc = silu_cc @ w_ada[i] + b_ada[i]
        sx1, ax1, gx1, sx2, ax2, gx2 = jnp.split(mx, 6, axis=-1)
        sc1, ac1, gc1, sc2, ac2, gc2 = jnp.split(mc, 6, axis=-1)

        hc = modulate(rmsnorm(h_ctx, norm1_g[i]), sc1, ac1)
        if last:
            kv_c = hc @ w_in[i][:, K_OFF:GATE_OFF]
            k_c, v_c = kv_c[..., :QK_COLS], kv_c[..., QK_COLS:]
        else:
            pc = hc @ w_in[i]
            conv_c = pc[..., CONV_OFF:Q_OFF]
            q_c = pc[..., Q_OFF:K_OFF]
            k_c = pc[..., K_OFF:V_OFF]
            v_c = pc[..., V_OFF:GATE_OFF]
            g_c = pc[..., GATE_OFF:]
        k_c = rmsnorm(k_c.reshape(B, C, N_HEADS, 2, QK_DIM), k_norm_g[i])
        v_c = v_c.reshape(B, C, N_HEADS, V_DIM)

        hx = modulate(rmsnorm(x, norm1_g[i]), sx1, ax1)
        px = hx @ w_in[i]
        conv_x = px[..., CONV_OFF:Q_OFF]
        q_x = px[..., Q_OFF:K_OFF].reshape(B, S, N_HEADS, 2, QK_DIM)
        k_x = px[..., K_OFF:V_OFF].reshape(B, S, N_HEADS, 2, QK_DIM)
        v_x = px[..., V_OFF:GATE_OFF].reshape(B, S, N_HEADS, V_DIM)
        g_x = px[..., GATE_OFF:]
        q_x = apply_rope(rmsnorm(q_x, q_norm_g[i]), cos, sin)
        k_x = apply_rope(rmsnorm(k_x, k_norm_g[i]), cos, sin)
        keys = jnp.concatenate([k_x, k_c], axis=1)
        vals = jnp.concatenate([v_x, v_c], axis=1)
        attn_x = diff_attention(q_x, keys, vals, lam, attn_norm_g[i], lam_init)
        y_x = merge_branches(conv_x, attn_x, g_x, w_dw[i], b_dw[i], conv_ln_g[i], conv_ln_b[i],
                             w_conv_out[i], w_attn_out[i], b_gate[i], w_out[i])
        x = x + gx1 * y_x

        if not last:
            q_c = rmsnorm(q_c.reshape(B, C, N_HEADS, 2, QK_DIM), q_norm_g[i])
            attn_c = diff_attention(q_c, k_c, v_c, lam, attn_norm_g[i], lam_init)
            y_c = merge_branches(conv_c, attn_c, g_c, w_dw[i], b_dw[i], conv_ln_g[i],
                                 conv_ln_b[i], w_conv_out[i], w_attn_out[i], b_gate[i], w_out[i])
            h_ctx = h_ctx + gc1 * y_c

        x = x + gx2 * sqrelu_mlp(modulate(rmsnorm(x, norm2_g[i]), sx2, ax2), w_mlp1[i], w_mlp2[i])
        if not last:
            h_ctx = h_ctx + gc2 * sqrelu_mlp(modulate(rmsnorm(h_ctx, norm2_g[i]), sc2, ac2),
                                             w_mlp1[i], w_mlp2[i])
    return x
```

```python
import math
from contextlib import ExitStack

import numpy as np
import ml_dtypes
import concourse.bass as bass
import concourse.mybir as mybir
from concourse.bass_utils import run_bass_kernel_spmd

F32 = mybir.dt.float32
BF16 = mybir.dt.bfloat16
ALU = mybir.AluOpType
AF = mybir.ActivationFunctionType
AX = mybir.AxisListType

D = 1024
KD = 8
NH = 8
IN_COLS = 6144
DFF = 4096
CW = 512
CK = 31
Q_OFF, K_OFF, V_OFF, GATE_OFF = 1024, 2048, 3072, 4096
EPS = 1e-6
N_CORES = 8


class Buf:
    def __init__(self, name, acc=False):
        self.name = name
        self.w = {}
        self.r = {}
        self.acc = acc


def _merge(dst, src):
    for k, (s, v) in src.items():
        if k not in dst or dst[k][1] < v:
            dst[k] = (s, v)


class _Rec:
    def __init__(self):
        self.call = None

    def __getattr__(self, name):
        def _f(*a, **k):
            assert self.call is None
            self.call = (name, a, k)
            return self
        return _f


def _record(fn):
    r = _Rec()
    fn(r)
    assert r.call is not None
    return r.call


class Queue:
    def __init__(self, name, sem, is_pe=False):
        self.name, self.sem, self.is_pe = name, sem, is_pe
        self.cnt = 0
        self.seen = {}
        self.ops = []
        self.dma_sems = []
        self.dma_cnt = []
        self.dma_rr = 0
        self.pending_noinc = False

    def _wait(self, tok):
        for sid, (sem, v) in tok.items():
            if self.is_pe and sem is self.sem:
                continue
            if self.seen.get(sid, 0) < v:
                self.seen[sid] = v
                self.ops.append(lambda e, sem=sem, v=v: e.wait_ge(sem, v))

    def _deps(self, reads, writes):
        for b in reads:
            self._wait(b.w)
        for b in writes:
            if not b.acc:
                self._wait(b.w)
            self._wait(b.r)

    def _commit(self, tok, reads, writes):
        for b in reads:
            _merge(b.r, tok)
        for b in writes:
            if b.acc:
                _merge(b.w, tok)
            else:
                b.w = dict(tok)
                b.r = {}

    def op(self, fn, reads=(), writes=()):
        self._deps(reads, writes)
        self.cnt += 1
        sem = self.sem
        name, a, k = _record(fn)
        self.ops.append(lambda e, name=name, a=a, k=k, sem=sem: getattr(e, name)(*a, **k).then_inc(sem, 1))
        tok = {id(sem): (sem, self.cnt)}
        self._commit(tok, reads, writes)
        self.pending_noinc = False
        return tok

    def op_noinc(self, fn, reads=(), writes=()):
        assert self.is_pe
        self._deps(reads, writes)
        name, a, k = _record(fn)
        self.ops.append(lambda e, name=name, a=a, k=k: getattr(e, name)(*a, **k))
        tok = {id(self.sem): (self.sem, self.cnt + 1)}
        self._commit(tok, reads, writes)
        self.pending_noinc = True
        return tok

    def dma(self, out, in_, reads=(), writes=()):
        self._deps(reads, writes)
        i = self.dma_rr
        self.dma_rr = (self.dma_rr + 1) % len(self.dma_sems)
        sem = self.dma_sems[i]
        prev = self.dma_cnt[i]
        if prev:
            self._wait({id(sem): (sem, prev)})
        self.dma_cnt[i] = prev + 16
        self.ops.append(lambda e, out=out, in_=in_, sem=sem:
                        e.dma_start(out=out, in_=in_).then_inc(sem, 16))
        tok = {id(sem): (sem, prev + 16)}
        self._commit(tok, reads, writes)
        return tok

    def all_tokens(self):
        t = {}
        if self.cnt:
            t[id(self.sem)] = (self.sem, self.cnt)
        for s, c in zip(self.dma_sems, self.dma_cnt):
            if c:
                t[id(s)] = (s, c)
        return t


class FW:
    def __init__(self, nc, stack, n_sync_dma=16, n_pool_dma=6):
        self.nc = nc
        mk = lambda n: stack.enter_context(nc.semaphore(n))
        self.pe = Queue("pe", mk("s_pe"), is_pe=True)
        self.act = Queue("act", mk("s_act"))
        self.dve = Queue("dve", mk("s_dve"))
        self.pool = Queue("pool", mk("s_pool"))
        self.sp = Queue("sp", mk("s_sp"))
        self.queues = [self.pe, self.act, self.dve, self.pool, self.sp]
        for q, n in ((self.sp, n_sync_dma), (self.pool, n_pool_dma)):
            q.dma_sems = [mk(f"d_{q.name}{i}") for i in range(n)]
            q.dma_cnt = [0] * n

    def barrier(self):
        assert not self.pe.pending_noinc
        tok = {}
        for q in self.queues:
            _merge(tok, q.all_tokens())
        for q in self.queues:
            q._wait(tok)

    def finish(self):
        self.barrier()
        nc = self.nc
        with nc.Block() as block:
            @block.tensor
            def _(e):
                for f in self.pe.ops:
                    f(e)

            @block.scalar
            def _(e):
                for f in self.act.ops:
                    f(e)

            @block.vector
            def _(e):
                for f in self.dve.ops:
                    f(e)

            @block.gpsimd
            def _(e):
                for f in self.pool.ops:
                    f(e)

            @block.sync
            def _(e):
                for f in self.sp.ops:
                    f(e)


class Arena:
    def __init__(self, U, nbytes):
        self.U, self.nbytes, self.off = U, nbytes, 0

    def mark(self):
        return self.off

    def reset(self, m):
        self.off = m

    def alloc(self, shape, dtype):
        n = 1
        for s in shape[1:]:
            n *= s
        esz = 4 if dtype == F32 else 2
        nb = (n * esz + 63) // 64 * 64
        assert self.off + nb <= self.nbytes, f"SBUF arena overflow {self.off}+{nb}>{self.nbytes}"
        ap = self.U[0:shape[0], self.off // 2:(self.off + n * esz) // 2]
        self.off += nb
        if dtype == F32:
            ap = ap.bitcast(F32)
        if len(shape) == 3:
            ap = ap.rearrange("p (a b) -> p a b", a=shape[1])
        elif len(shape) == 4:
            ap = ap.rearrange("p (a b c) -> p a b c", a=shape[1], b=shape[2])
        return ap


def build_program(S, C, NB, DEPTH):
    TT = S + C
    NKB = TT // 128
    NLT = S // 512
    assert S % 512 == 0 and C % 128 == 0 and C <= 512
    nc = bass.Bass("TRN2", target_bir_lowering=False)

    def din(name, shape, dt=F32):
        return nc.dram_tensor(name, list(shape), dt, kind="ExternalInput").ap()

    def dint(name, shape, dt):
        return nc.dram_tensor(name, list(shape), dt, kind="Internal").ap()

    x_d = din("x", [NB, S, D])
    c_d = din("c", [NB, D])
    ctx_d = din("ctx", [NB, C, D])
    cctx_d = din("c_ctx", [D])
    w_ada_d = din("w_ada", [DEPTH, D, 6 * D])
    b_ada_d = din("b_ada", [DEPTH, 6 * D])
    n1g_d = din("norm1_g", [DEPTH, D])
    w_in_d = din("w_in", [DEPTH, D, IN_COLS])
    bgate_d = din("b_gate", [DEPTH, 2 * D])
    qng_d = din("q_norm_g", [DEPTH, 64])
    kng_d = din("k_norm_g", [DEPTH, 64])
    lamq_d = din("lam_q", [DEPTH, 128])
    lamk_d = din("lam_k", [DEPTH, 128])
    ang_d = din("attn_norm_g", [DEPTH, 128])
    wdw_d = din("w_dw", [DEPTH, CK, CW])
    bdw_d = din("b_dw", [DEPTH, CW])
    lng_d = din("conv_ln_g", [DEPTH, CW])
    lnb_d = din("conv_ln_b", [DEPTH, CW])
    wco_d = din("w_conv_out", [DEPTH, CW, D])
    wao_d = din("w_attn_out", [DEPTH, D, D])
    wo_d = din("w_out", [DEPTH, D, D])
    n2g_d = din("norm2_g", [DEPTH, D])
    w1_d = din("w_mlp1", [DEPTH, D, DFF])
    w2_d = din("w_mlp2", [DEPTH, DFF, D])
    ident_d = din("ident", [128, 128])
    cos_d = din("rope_cos", [S, 32])
    sin_d = din("rope_sin", [S, 32])
    out_d = nc.dram_tensor("out", [NB, S, D], F32, kind="ExternalOutput").ap()

    win_b = dint("win_b", [DEPTH, D, IN_COLS], BF16)
    wco_b = dint("wco_b", [DEPTH, CW, D], BF16)
    wao_b = dint("wao_b", [DEPTH, D, D], BF16)
    wo_b = dint("wo_b", [DEPTH, D, D], BF16)
    w1_b = dint("w1_b", [DEPTH, D, DFF], BF16)
    w2_b = dint("w2_b", [DEPTH, DFF, D], BF16)
    hctx_d = dint("hctx", [NB, C, D], F32)
    qT_d = dint("qT_s", [DEPTH, NB, NH, 128, TT], BF16)
    kT_d = dint("kT_s", [DEPTH, NB, NH, 128, TT], BF16)
    v_d = dint("v_s", [DEPTH, NB, TT, D], BF16)
    zT_d = dint("zT_s", [DEPTH, NB, 4, 128, TT], BF16)
    aT_d = dint("aT_s", [DEPTH, NB, NH, 128, TT], BF16)

    with ExitStack() as st:
        fw = FW(nc, st)
        pe, act, dve, pool, sp = fw.pe, fw.act, fw.dve, fw.pool, fw.sp
        SB_BYTES = 206 * 1024
        U = st.enter_context(nc.sbuf_tensor("U", [128, SB_BYTES // 2], BF16))
        ar = Arena(U, SB_BYTES)
        P = st.enter_context(nc.psum_tensor("P", [128, 8, 512], F32))
        PB = [Buf(f"ps{i}") for i in range(8)]
        ps_rr = [0]

        def ps_next():
            i = ps_rr[0]
            ps_rr[0] = (i + 1) % 8
            return i

        def psf(i):
            return P[:, i, :]

        def psb(i):
            return P[:, i, :].bitcast(BF16)

        ident_f = ar.alloc([128, 128], F32); IDF = Buf("identf")
        ident_b = ar.alloc([128, 128], BF16); IDB = Buf("identb")
        ones_f = ar.alloc([128, 128], F32); ONF = Buf("onesf")
        ones_b = ar.alloc([128, 128], BF16); ONB = Buf("onesb")
        epsc = ar.alloc([128, 1], F32); EPSB = Buf("eps")
        negc = ar.alloc([128, 2], F32); NEGC = Buf("negc")
        pstage = ar.alloc([128, 128], F32); PST = Buf("pstage")
        PT1 = ar.alloc([128, 92], F32); PT1B = Buf("pt1")
        PT2 = ar.alloc([128, 124], F32); PT2B = Buf("pt2")
        NS3 = NB + 1
        scT = ar.alloc([128, 8, 4], F32); SCT = Buf("scT")
        modT = ar.alloc([128, 48, NS3], F32); MODT = Buf("modT")
        AB = ar.alloc([128, NS3, 4, 8], F32); ABB = Buf("AB")
        gq_bc = ar.alloc([128, 64], F32); GQ = Buf("gq")
        gk_bc = ar.alloc([128, 64], F32); GK = Buf("gk")
        lamw = ar.alloc([128, 4, 128], F32); LAMW = Buf("lamw")
        lamc = ar.alloc([128, 8], F32); LAMC = Buf("lamc")
        Gt = [ar.alloc([128, 128], F32) for _ in range(2)]; GTB = [Buf("gt0"), Buf("gt1")]
        gbc = [ar.alloc([128, 1024], F32) for _ in range(2)]; GBC = [Buf("gbc0"), Buf("gbc1")]
        NSLOT = 8
        wslots = [(ar.alloc([128, 2048], BF16), Buf(f"wslot{i}")) for i in range(NSLOT)]
        common_mark = ar.mark()

        class WStream:
            PF = 4

            def __init__(self):
                self.units = []
                self.issued = 0
                self.nxt = 0

            def extend(self, srcs):
                self.units += srcs

            def _view(self, j):
                slot, buf = wslots[j % NSLOT]
                src = self.units[j]
                a, b = src.shape[1], src.shape[2]
                if src.dtype == F32:
                    v = slot[:, 0:2 * a * b].bitcast(F32)
                else:
                    v = slot[:, 0:a * b]
                return v.rearrange("p (a b) -> p a b", a=a), buf

            def _issue(self, n):
                while self.issued < min(n, len(self.units)):
                    j = self.issued
                    v, buf = self._view(j)
                    sp.dma(v, self.units[j], writes=[buf])
                    self.issued += 1

            def get(self):
                j = self.nxt
                self.nxt += 1
                assert j < len(self.units)
                self._issue(j + 1 + self.PF)
                return self._view(j)

            def done(self):
                assert self.nxt == len(self.units) == self.issued, (self.nxt, len(self.units), self.issued)

        ws = WStream()

        sp.dma(ident_f, ident_d, writes=[IDF])
        dve.op(lambda e: e.tensor_copy(out=ident_b, in_=ident_f), reads=[IDF], writes=[IDB])
        dve.op(lambda e: e.memset(ones_f, 1.0), writes=[ONF])
        dve.op(lambda e: e.memset(ones_b, 1.0), writes=[ONB])
        dve.op(lambda e: e.memset(epsc, EPS), writes=[EPSB])
        dve.op(lambda e: e.memset(negc[:, 0:1], -1.0), writes=[NEGC])
        dve.op(lambda e: e.memset(negc[:, 1:2], -0.5), writes=[NEGC])

        WCAST_IN = [Buf(f"wcast_in{l}", acc=True) for l in range(DEPTH)]
        WCAST_REST = [Buf(f"wcast_rest{l}", acc=True) for l in range(DEPTH)]
        for l in range(DEPTH):
            for (src, dst, rows) in ((w_in_d, win_b, D), (wco_d, wco_b, CW), (wao_d, wao_b, D),
                                     (wo_d, wo_b, D), (w1_d, w1_b, D), (w2_d, w2_b, DFF)):
                wbuf = WCAST_IN[l] if src is w_in_d else WCAST_REST[l]
                for r0 in range(0, rows, 128):
                    pool.dma(dst[l, r0:r0 + 128, :], src[l, r0:r0 + 128, :], writes=[wbuf])

        m0 = ar.mark()
        crow = ar.alloc([NS3, 1024], F32); CROW = Buf("crow")
        srow = ar.alloc([NS3, 1024], F32); SROW = Buf("srow")
        sp.dma(crow[0:NB, :], c_d, writes=[CROW])
        sp.dma(crow[NB:NB + 1, :], cctx_d.rearrange("(o d) -> o d", o=1), writes=[CROW])
        act.op(lambda e: e.activation(out=srow, in_=crow, func=AF.Silu), reads=[CROW], writes=[SROW])
        bi = ps_next()
        for k in range(8):
            pe.op(lambda e, k=k: e.transpose(out=psf(bi)[:, k * 4:k * 4 + NS3], in_=srow[0:NS3, k * 128:(k + 1) * 128],
                                             identity=ident_f[0:NS3, 0:NS3]),
                  reads=[SROW, IDF], writes=[PB[bi]])
        dve.op(lambda e: e.tensor_copy(out=scT[:, :, 0:NS3],
                                       in_=psf(bi)[:, 0:32].rearrange("p (k s) -> p k s", s=4)[:, :, 0:NS3]),
               reads=[PB[bi]], writes=[SCT])
        fw.barrier()
        ar.reset(m0)

        LN_SCALE = 1.0 / CW

        def layer_setup(l):
            lam_init = 0.8 - 0.6 * math.exp(-0.3 * l)
            sp.dma(pstage[0:48, :], b_ada_d[l].rearrange("(r p) -> r p", p=128), writes=[PST])
            sp.dma(pstage[48:56, :], n1g_d[l].rearrange("(r p) -> r p", p=128), writes=[PST])
            sp.dma(pstage[56:64, :], n2g_d[l].rearrange("(r p) -> r p", p=128), writes=[PST])
            sp.dma(pstage[64:80, :], bgate_d[l].rearrange("(r p) -> r p", p=128), writes=[PST])
            sp.dma(pstage[80:84, :], lng_d[l].rearrange("(r p) -> r p", p=128), writes=[PST])
            sp.dma(pstage[84:88, :], lnb_d[l].rearrange("(r p) -> r p", p=128), writes=[PST])
            sp.dma(pstage[88:92, :], bdw_d[l].rearrange("(r p) -> r p", p=128), writes=[PST])
            b1 = ps_next()
            pe.op(lambda e: e.transpose(out=psf(b1)[:, 0:92], in_=pstage[0:92, :], identity=ident_f[0:92, 0:92]),
                  reads=[PST, IDF], writes=[PB[b1]])
            dve.op(lambda e: e.tensor_copy(out=PT1, in_=psf(b1)[:, 0:92]), reads=[PB[b1]], writes=[PT1B])
            sp.dma(pstage[0:124, :], wdw_d[l].rearrange("t (c p) -> (t c) p", p=128), reads=[], writes=[PST])
            b2 = ps_next()
            pe.op(lambda e: e.transpose(out=psf(b2)[:, 0:124], in_=pstage[0:124, :], identity=ident_f[0:124, 0:124]),
                  reads=[PST, IDF], writes=[PB[b2]])
            dve.op(lambda e: e.tensor_copy(out=PT2, in_=psf(b2)[:, 0:124]), reads=[PB[b2]], writes=[PT2B])
            sp.dma(gq_bc, qng_d[l].partition_broadcast(128), writes=[GQ])
            sp.dma(gk_bc, kng_d[l].partition_broadcast(128), writes=[GK])
            sp.dma(lamw[:, 0, :], lamq_d[l].partition_broadcast(128), writes=[LAMW])
            sp.dma(lamw[:, 1, :], lamk_d[l].partition_broadcast(128), writes=[LAMW])
            sp.dma(lamc[:, 3:4], ang_d[l].rearrange("(p o) -> p o", o=1), writes=[LAMC])
            dve.op(lambda e: e.tensor_tensor(out=lamw[:, 2, :], in0=lamw[:, 0, :], in1=lamw[:, 1, :], op=ALU.mult),
                   reads=[LAMW], writes=[LAMW])
            dve.op(lambda e: e.tensor_reduce(out=lamc[:, 0:2], in_=lamw[:, 2, :].rearrange("p (a b) -> p a b", a=2),
                                             axis=AX.X, op=ALU.add), reads=[LAMW, LAMC], writes=[LAMC])
            act.op(lambda e: e.activation(out=lamc[:, 4:6], in_=lamc[:, 0:2], func=AF.Exp), reads=[LAMC], writes=[LAMC])
            dve.op(lambda e: e.tensor_tensor(out=lamc[:, 6:7], in0=lamc[:, 4:5], in1=lamc[:, 5:6], op=ALU.subtract),
                   reads=[LAMC], writes=[LAMC])
            dve.op(lambda e: e.tensor_scalar(out=lamc[:, 2:3], in0=lamc[:, 6:7], scalar1=float(lam_init), scalar2=-1.0,
                                             op0=ALU.add, op1=ALU.mult), reads=[LAMC], writes=[LAMC])
            dve.op(lambda e: e.tensor_scalar(out=lamc[:, 7:8], in0=lamc[:, 3:4], scalar1=float(1.0 - lam_init),
                                             scalar2=None, op0=ALU.mult), reads=[LAMC], writes=[LAMC])
            ws.extend([w_ada_d[l][:, j * 128:(j + 1) * 128].rearrange("(k p) c -> p k c", p=128) for j in range(48)])
            bm = ps_next()
            mps = psf(bm)[:, 0:192].rearrange("p (j s) -> p j s", s=4)
            for j in range(48):
                wv, wb = ws.get()
                for k in range(8):
                    f = (lambda e, j=j, k=k, wv=wv: e.matmul(mps[:, j, 0:NS3], lhsT=wv[:, k, :], rhs=scT[:, k, 0:NS3],
                                                              start=(k == 0), stop=(k == 7)))
                    (pe.op if k == 7 else pe.op_noinc)(f, reads=[wb, SCT], writes=[PB[bm]])
            ws.done()
            dve.op(lambda e: e.tensor_tensor(out=modT, in0=mps[:, :, 0:NS3],
                                             in1=PT1[:, 0:48].unsqueeze(2).to_broadcast([128, 48, NS3]), op=ALU.add),
                   reads=[PB[bm], PT1B], writes=[MODT])
            for s in range(NS3):
                dve.op(lambda e, s=s: e.scalar_tensor_tensor(out=AB[:, s, 0, :], in0=modT[:, 8:16, s], scalar=1.0,
                                                             in1=PT1[:, 48:56], op0=ALU.add, op1=ALU.mult),
                       reads=[MODT, PT1B], writes=[ABB])
                dve.op(lambda e, s=s: e.tensor_copy(out=AB[:, s, 1, :], in_=modT[:, 0:8, s]), reads=[MODT], writes=[ABB])
                dve.op(lambda e, s=s: e.scalar_tensor_tensor(out=AB[:, s, 2, :], in0=modT[:, 32:40, s], scalar=1.0,
                                                             in1=PT1[:, 56:64], op0=ALU.add, op1=ALU.mult),
                       reads=[MODT, PT1B], writes=[ABB])
                dve.op(lambda e, s=s: e.tensor_copy(out=AB[:, s, 3, :], in_=modT[:, 24:32, s]), reads=[MODT], writes=[ABB])
            fw.barrier()

        def build_gbc(s):
            for gi, j0 in ((0, 16), (1, 40)):
                for half in range(2):
                    b = ps_next()
                    for kk in range(4):
                        k = half * 4 + kk
                        g = Gt[k % 2]
                        dve.op(lambda e, g=g, k=k: e.tensor_scalar(out=g, in0=ones_f, scalar1=modT[:, j0 + k, s:s + 1],
                                                                   scalar2=None, op0=ALU.mult),
                               reads=[ONF, MODT], writes=[GTB[k % 2]])
                        pe.op(lambda e, g=g, kk=kk, b=b: e.matmul(psf(b)[:, kk * 128:(kk + 1) * 128], lhsT=g, rhs=ident_f,
                                                                  start=True, stop=True),
                              reads=[GTB[k % 2], IDF], writes=[PB[b]])
                    dve.op(lambda e, b=b, gi=gi, half=half: e.tensor_copy(out=gbc[gi][:, half * 512:(half + 1) * 512],
                                                                          in_=psf(b)),
                           reads=[PB[b]], writes=[GBC[gi]])

        def norm_stats(xt, XT, nb, tmp):
            ss, sd, rstd, junk, xh, SSB, JB, XHB = tmp
            dve.op(lambda e: e.memset(ss, 0.0), reads=[], writes=[SSB])
            for n in range(nb):
                act.op(lambda e, n=n: e.activation(out=junk, in_=xt[:, n, :], func=AF.Square, accum_out=ss[:, n:n + 1]),
                       reads=[XT, SSB], writes=[JB, SSB])
            act.op(lambda e: e.activation(out=sd[:, 0:nb], in_=ss[:, 0:nb], func=AF.Sqrt, bias=epsc[:, 0:1], scale=1.0 / D),
                   reads=[SSB, EPSB], writes=[SSB])
            dve.op(lambda e: e.reciprocal(out=rstd[:, 0:nb], in_=sd[:, 0:nb]), reads=[SSB], writes=[SSB])

        def norm_T(xt, XT, nb, s, which, hT, HT, tmp):
            ss, sd, rstd, junk, xh, SSB, JB, XHB = tmp
            ia, ib = (0, 1) if which == 1 else (2, 3)
            xhs = [(xh, XHB), (junk, JB)]
            base = ((ps_rr[0] + 3) // 4 * 4) % 8
            ps_rr[0] = (base + 4) % 8
            for n in range(nb):
                xb, XB = xhs[n % 2]
                dve.op(lambda e, n=n, xb=xb: e.tensor_scalar(out=xb, in0=xt[:, n, :], scalar1=rstd[:, n:n + 1], scalar2=None,
                                                             op0=ALU.mult), reads=[XT, SSB], writes=[XB])
                b = base + n
                pv = psb(b).rearrange("p (k t) -> p k t", k=8)
                for k in range(8):
                    f = lambda e, k=k, pv=pv, xb=xb: e.transpose(out=pv[:, k, :], in_=xb[:, k * 128:(k + 1) * 128], identity=ident_b)
                    (pe.op if k == 7 else pe.op_noinc)(f, reads=[XB, IDB], writes=[PB[b]])
            pq = P[:, base:base + nb, :].bitcast(BF16).rearrange("p n (k t) -> p n k t", k=8)
            for k in range(8):
                act.op(lambda e, k=k: e.activation(out=hT[:, k, 0:nb * 128].rearrange("p (n t) -> p n t", n=nb), in_=pq[:, :, k, :],
                                                   func=AF.Identity, scale=AB[:, s, ia, k:k + 1], bias=AB[:, s, ib, k:k + 1]),
                       reads=PB[base:base + nb] + [ABB], writes=[HT])

        def norm_to_hT(xt, XT, nb, s, which, hT, HT, tmp):
            norm_stats(xt, XT, nb, tmp)
            norm_T(xt, XT, nb, s, which, hT, HT, tmp)

        def stream_tiles(si):
            tiles = [("lat", i * 512, 512, i * 512) for i in range(NLT)]
            for c0 in range(0, C, 512):
                tiles.append(("ctx", c0, min(512, C - c0), S + c0))
            return tiles

        def x_src(l, si, kind, t0, T):
            if kind == "lat":
                base = x_d if l == 0 else out_d
            else:
                base = ctx_d if l == 0 else hctx_d
            return base[si, t0:t0 + T, :].rearrange("(n p) d -> p n d", p=128)

        XRES = [Buf(f"xres{si}", acc=True) for si in range(NB)]
        HCTX = [Buf(f"hctx{si}", acc=True) for si in range(NB)]

        def pass1(l, si):
            m = ar.mark()
            xt1 = ar.alloc([128, 4, 1024], F32); XT1 = Buf("xt")
            hTs = [ar.alloc([128, 8, 512], BF16) for _ in range(2)]; HTS = [Buf("hT0"), Buf("hT1")]
            ssa = ar.alloc([128, 12], F32)
            tmp = (ssa[:, 0:4], ssa[:, 4:8], ssa[:, 8:12], ar.alloc([128, 1024], BF16), ar.alloc([128, 1024], BF16),
                   Buf("ss"), Buf("junk"), Buf("xh"))
            NBLK = S // 128
            tabs = [[ar.alloc([128, NBLK, 32], F32) for _ in range(4)] for _ in range(2)]; TABS = Buf("tabs")
            sig = [ar.alloc([128, 512], F32) for _ in range(2)]; SIG = [Buf("sig0"), Buf("sig1")]
            z_st = ar.alloc([128, 4, 512], BF16); ZST = Buf("zst")
            sq = [ar.alloc([128, 4, 512], BF16) for _ in range(2)]; SQ = [Buf("sq0"), Buf("sq1")]
            qn = [ar.alloc([128, 4, 512], F32) for _ in range(2)]; QN = [Buf("qn0"), Buf("qn1")]
            st8 = [ar.alloc([128, 96], F32) for _ in range(2)]; ST8 = [Buf("st80"), Buf("st81")]
            rt = [ar.alloc([128, 4, 8, 32], F32) for _ in range(4)]; RTA = Buf("rta"); RTB = Buf("rtb")
            qr = [ar.alloc([128, 4, 512], BF16) for _ in range(2)]; QR = [Buf("qr0"), Buf("qr1")]
            qT_st = ar.alloc([128, 8, 512], BF16); QTS = Buf("qTst")
            kT_st = ar.alloc([128, 8, 512], BF16); KTS = Buf("kTst")
            v_st = ar.alloc([128, 4, 1024], BF16); VST = Buf("vst")
            cos_t = qn[0].rearrange("p a b -> p (a b)")[:, 0:NBLK * 32].rearrange("p (n i) -> p n i", i=32)
            sin_t = qn[1].rearrange("p a b -> p (a b)")[:, 0:NBLK * 32].rearrange("p (n i) -> p n i", i=32)
            sp.dma(cos_t, cos_d.rearrange("(n p) i -> p n i", p=128), writes=[QN[0]])
            sp.dma(sin_t, sin_d.rearrange("(n p) i -> p n i", p=128), writes=[QN[1]])
            for ty, (gb, GB) in enumerate(((gq_bc, GQ), (gk_bc, GK))):
                g1 = gb[:, 0:32].unsqueeze(1).to_broadcast([128, NBLK, 32])
                g2 = gb[:, 32:64].unsqueeze(1).to_broadcast([128, NBLK, 32])
                for ti_, (src, SB_, gg) in enumerate(((cos_t, QN[0], g1), (sin_t, QN[1], g2), (cos_t, QN[0], g2), (sin_t, QN[1], g1))):
                    dve.op(lambda e, ty=ty, ti_=ti_, src=src, gg=gg: e.tensor_tensor(out=tabs[ty][ti_], in0=src, in1=gg, op=ALU.mult),
                           reads=[SB_, GB], writes=[TABS])
            tiles = stream_tiles(si)
            units = []
            for _ in tiles:
                units += [win_b[l][:, u * 256:(u + 1) * 256].rearrange("(k p) c -> p k c", p=128) for u in range(4)]
                for g in range(6):
                    c0 = Q_OFF + g * 512
                    units += [win_b[l][kh * 512:(kh + 1) * 512, c0:c0 + 512].rearrange("(k p) c -> p k c", p=128)
                              for kh in range(2)]
            sp._wait(WCAST_IN[l].w)
            ws.extend(units)
            src_buf = lambda kind: (XRES[si] if kind == "lat" else HCTX[si])
            uc = 0
            quad_rr = 0
            pend = []

            def load_x1(tj):
                kind_, t0_, T_, tok0_ = tiles[tj]
                rd = []
                if l > 0:
                    rd.append(src_buf(kind_))
                sp.dma(xt1[:, 0:T_ // 128, :], x_src(l, si, kind_, t0_, T_), reads=rd, writes=[XT1])

            def stats1(tj):
                norm_stats(xt1, XT1, tiles[tj][2] // 128, tmp)

            def normT1(tj):
                kind_, t0_, T_, tok0_ = tiles[tj]
                norm_T(xt1, XT1, T_ // 128, (si if kind_ == "lat" else NB), 1, hTs[tj % 2], HTS[tj % 2], tmp)

            def load_norm(tj):
                load_x1(tj); stats1(tj); normT1(tj)

            for ti, (kind, t0, T, tok0) in enumerate(tiles):
                nb = T // 128
                s = si if kind == "lat" else NB
                xt, XT = xt1, XT1
                hT, HT = hTs[ti % 2], HTS[ti % 2]
                if ti == 0:
                    load_norm(0)
                if ti + 1 < len(tiles):
                    load_x1(ti + 1)
                cu = [ws.get() for _ in range(4)]
                for j in range(4):
                    ba, bg = ps_next(), ps_next()
                    for (bb, u) in ((ba, cu[j // 2]), (bg, cu[2 + j // 2])):
                        wv, wb = u
                        for k in range(8):
                            f = lambda e, bb=bb, wv=wv, k=k, j=j: e.matmul(
                                psf(bb)[:, 0:T], lhsT=wv[:, k, (j % 2) * 128:(j % 2) * 128 + 128], rhs=hT[:, k, 0:T],
                                start=(k == 0), stop=(k == 7))
                            (pe.op if k == 7 else pe.op_noinc)(f, reads=[wb, HT], writes=[PB[bb]])
                    sg, SG = sig[j % 2], SIG[j % 2]
                    act.op(lambda e, bg=bg, sg=sg: e.activation(out=sg[:, 0:T], in_=psf(bg)[:, 0:T], func=AF.Sigmoid),
                           reads=[PB[bg]], writes=[SG])
                    dve.op(lambda e, ba=ba, sg=sg, j=j: e.tensor_tensor(out=z_st[:, j, 0:T], in0=psf(ba)[:, 0:T],
                                                                        in1=sg[:, 0:T], op=ALU.mult),
                           reads=[PB[ba], SG], writes=[ZST])
                sp.dma(zT_d[l, si][:, :, tok0:tok0 + T].rearrange("j p t -> p j t"), z_st[:, :, 0:T], reads=[ZST], writes=[])
                for g in range(6):
                    wu = [ws.get() for _ in range(2)]
                    typ = g // 2
                    q0b = (quad_rr % 2) * 4
                    quad_rr += 1
                    QB = PB[q0b:q0b + nb]
                    for n in range(nb):
                        b = q0b + n
                        for k in range(8):
                            wv, wb = wu[k // 4]
                            f = lambda e, b=b, wv=wv, k=k, n=n: e.matmul(
                                psf(b), lhsT=hT[:, k, n * 128:(n + 1) * 128], rhs=wv[:, k % 4, :],
                                start=(k == 0), stop=(k == 7))
                            (pe.op if k == 7 else pe.op_noinc)(f, reads=[wb, HT], writes=[PB[b]])
                    pq = P[:, q0b:q0b + nb, :]
                    if len(pend) == 2 or (typ == 2 and pend):
                        pend.pop(0)(4 - q0b)
                    if typ == 2:
                        act.op(lambda e, pq=pq, g=g: e.activation(out=v_st[:, 0:nb, (g - 4) * 512:(g - 3) * 512], in_=pq, func=AF.Copy),
                               reads=QB, writes=[VST])
                        if g == 4 and ti + 1 < len(tiles):
                            normT1(ti + 1)
                        if g == 5:
                            sp.dma(v_d[l, si][tok0:tok0 + T, :].rearrange("(n p) c -> p n c", p=128), v_st[:, 0:nb, :],
                                   reads=[VST], writes=[])
                        continue
                    u = uc % 2
                    uc += 1
                    ty = typ
                    act.op(lambda e, pq=pq, u=u: e.activation(out=sq[u][:, 0:nb, :], in_=pq, func=AF.Square),
                           reads=QB, writes=[SQ[u]])
                    dve.op(lambda e, u=u: e.tensor_reduce(out=st8[u][:, 0:nb * 8],
                                                          in_=sq[u][:, 0:nb, :].rearrange("p n (a b) -> p (n a) b", a=8),
                                                          axis=AX.X, op=ALU.add), reads=[SQ[u]], writes=[ST8[u]])
                    act.op(lambda e, u=u: e.activation(out=st8[u][:, 32:32 + nb * 8], in_=st8[u][:, 0:nb * 8], func=AF.Sqrt,
                                                       bias=epsc[:, 0:1], scale=1.0 / 64), reads=[ST8[u], EPSB], writes=[ST8[u]])
                    dve.op(lambda e, u=u: e.reciprocal(out=st8[u][:, 64:64 + nb * 8], in_=st8[u][:, 32:32 + nb * 8]),
                           reads=[ST8[u]], writes=[ST8[u]])
                    dve.op(lambda e, u=u, pq=pq: e.tensor_tensor(
                        out=qn[u][:, 0:nb, :].rearrange("p n (a b) -> p (n a) b", a=8),
                        in0=pq.rearrange("p n (a b) -> p (n a) b", a=8),
                        in1=st8[u][:, 64:64 + nb * 8].unsqueeze(2).to_broadcast([128, nb * 8, 64]), op=ALU.mult),
                        reads=QB + [ST8[u]], writes=[QN[u]])
                    if kind == "lat":
                        blk0 = t0 // 128
                        q5 = qn[u][:, 0:nb, :].rearrange("p n (a h i) -> p n a h i", a=8, h=2)
                        t1, t2 = q5[:, :, :, 0, :], q5[:, :, :, 1, :]
                        tb = [tabs[ty][i][:, blk0:blk0 + nb, :].unsqueeze(2).to_broadcast([128, nb, 8, 32]) for i in range(4)]
                        o5 = qr[u][:, 0:nb, :].rearrange("p n (a h i) -> p n a h i", a=8, h=2)
                        ra, rb, rc, rd_ = [r[:, 0:nb, :, :] for r in rt]
                        dve.op(lambda e, ra=ra, t1=t1, tb=tb: e.tensor_tensor(out=ra, in0=t1, in1=tb[0], op=ALU.mult),
                               reads=[QN[u], TABS], writes=[RTA])
                        dve.op(lambda e, rb=rb, t2=t2, tb=tb: e.tensor_tensor(out=rb, in0=t2, in1=tb[1], op=ALU.mult),
                               reads=[QN[u], TABS], writes=[RTA])
                        dve.op(lambda e, o5=o5, ra=ra, rb=rb: e.tensor_tensor(out=o5[:, :, :, 0, :], in0=ra, in1=rb, op=ALU.subtract),
                               reads=[RTA], writes=[QR[u]])
                        pool.op(lambda e, rc=rc, t2=t2, tb=tb: e.tensor_tensor(out=rc, in0=t2, in1=tb[2], op=ALU.mult),
                                reads=[QN[u], TABS], writes=[RTB])
                        pool.op(lambda e, rd_=rd_, t1=t1, tb=tb: e.tensor_tensor(out=rd_, in0=t1, in1=tb[3], op=ALU.mult),
                                reads=[QN[u], TABS], writes=[RTB])
                        pool.op(lambda e, o5=o5, rc=rc, rd_=rd_: e.tensor_tensor(out=o5[:, :, :, 1, :], in0=rc, in1=rd_, op=ALU.add),
                                reads=[RTB], writes=[QR[u]])
                    else:
                        gbcast, GB = (gq_bc, GQ) if typ == 0 else (gk_bc, GK)
                        pool.op(lambda e, u=u, gbcast=gbcast: e.tensor_tensor(
                            out=qr[u][:, 0:nb, :].rearrange("p n (a b) -> p (n a) b", a=8),
                            in0=qn[u][:, 0:nb, :].rearrange("p n (a b) -> p (n a) b", a=8),
                            in1=gbcast.unsqueeze(1).to_broadcast([128, nb * 8, 64]), op=ALU.mult),
                            reads=[QN[u], GB], writes=[QR[u]])
                    def emit_T(tb, u=u, typ=typ, g=g, nb=nb, T=T, tok0=tok0):
                      dstT, DSTB = (qT_st, QTS) if typ == 0 else (kT_st, KTS)
                      h0 = (g % 2) * 4
                      for n0 in range(0, nb, 2):
                          bt = tb + n0 // 2
                          pv = psb(bt).rearrange("p (n h t) -> p n h t", n=2, h=4)
                          for nn in range(2):
                              for hh in range(4):
                                  f = lambda e, hh=hh, pv=pv, u=u, nn=nn, n0=n0: e.transpose(
                                      out=pv[:, nn, hh, :], in_=qr[u][:, n0 + nn, hh * 128:(hh + 1) * 128], identity=ident_b)
                                  (pe.op if (nn == 1 and hh == 3) else pe.op_noinc)(f, reads=[QR[u], IDB], writes=[PB[bt]])
                          act.op(lambda e, pv=pv, dstT=dstT, h0=h0, n0=n0: e.activation(
                              out=dstT[:, h0:h0 + 4, n0 * 128:(n0 + 2) * 128].rearrange("p h (n t) -> p h n t", n=2),
                              in_=pv.rearrange("p n h t -> p h n t"), func=AF.Copy),
                              reads=[PB[bt]], writes=[DSTB])
                      if g == 1:
                          sp.dma(qT_d[l, si][:, :, tok0:tok0 + T].rearrange("h p t -> p h t"), qT_st[:, :, 0:T], reads=[QTS], writes=[])
                      if g == 3:
                          sp.dma(kT_d[l, si][:, :, tok0:tok0 + T].rearrange("h p t -> p h t"), kT_st[:, :, 0:T], reads=[KTS], writes=[])

                    pend.append(emit_T)
                    if g == 1 and ti + 1 < len(tiles):
                        stats1(ti + 1)
            ws.done()
            fw.barrier()
            ar.reset(m)

        def pass2(l, si, last):
            m = ar.mark()
            kTh = [ar.alloc([128, TT], BF16) for _ in range(2)]; KTH = [Buf("kth0"), Buf("kth1")]
            qTh = [ar.alloc([128, TT], BF16) for _ in range(2)]; QTH = [Buf("qth0"), Buf("qth1")]
            vh = [ar.alloc([128, NKB, 128], BF16) for _ in range(2)]; VH = [Buf("vh0"), Buf("vh1")]
            E = [ar.alloc([128, 2, 512], BF16) for _ in range(3)]; EB = [Buf(f"E{i}") for i in range(3)]
            r0 = ar.alloc([128, 512], F32); R0 = Buf("r0")
            r1 = ar.alloc([128, 512], F32); R1 = Buf("r1")
            t0_ = ar.alloc([128, 512], F32); T0 = Buf("t0")
            t1_ = ar.alloc([128, 512], F32); T1 = Buf("t1")
            osq = ar.alloc([128, 512], F32); OSQ = Buf("osq")
            acc0 = ar.alloc([128, 512], F32); ACC0 = Buf("acc0")
            o0c = ar.alloc([128, 512], F32); O0C = Buf("o0c")
            o1c = ar.alloc([128, 512], F32); O1C = Buf("o1c")
            s1c = ar.alloc([128, 512], F32); S1C = Buf("s1c")
            s0c = ar.alloc([128, 512], F32); S0C = Buf("s0c")
            o_all = ar.alloc([128, TT], F32); OALL = Buf("oall")
            ss_all = ar.alloc([128, TT], F32); SSALL = Buf("ssall")
            a_st = ar.alloc([128, TT], BF16); AST = Buf("ast")
            PS_S = [Buf("pss0"), Buf("pss1")]
            PO = [Buf("po0"), Buf("pso0"), Buf("po1"), Buf("pso1")]
            sslot = [0]
            qtiles = [(i * 512, 512, list(range(NKB))) for i in range(NLT)]
            if not last:
                qtiles.append((S, C, list(range(S // 128, NKB))))
            nq_tot = S + (0 if last else C)

            def load_head(h):
                sl = h % 2
                sp.dma(kTh[sl], kT_d[l, si, h], writes=[KTH[sl]])
                sp.dma(qTh[sl][:, 0:nq_tot], qT_d[l, si, h][:, 0:nq_tot], writes=[QTH[sl]])
                half = (NKB + 1) // 2
                for a, b in ((0, half), (half, NKB)):
                    sp.dma(vh[sl][:, a:b, :],
                           v_d[l, si][a * 128:b * 128, h * 128:(h + 1) * 128].rearrange("(kb p) c -> p kb c", p=128),
                           writes=[VH[sl]])

            load_head(0)
            for h in range(NH):
                sl = h % 2
                if h + 1 < NH:
                    load_head(h + 1)
                pend2 = {"s0": None, "a1": None, "a2": None, "b": None}
                for (q0, N, kbs) in qtiles:
                    nk = len(kbs)

                    def emit_qk(j):
                        sb = sslot[0] % 2
                        sslot[0] += 1
                        kb = kbs[j]
                        pe.op_noinc(lambda e, sb=sb, kb=kb: e.matmul(
                            P[:, 2 * sb, 0:N], lhsT=kTh[sl][0:64, kb * 128:(kb + 1) * 128], rhs=qTh[sl][0:64, q0:q0 + N],
                            start=True, stop=True), reads=[KTH[sl], QTH[sl]], writes=[PS_S[sb]])
                        pe.op(lambda e, sb=sb, kb=kb: e.matmul(
                            P[:, 2 * sb + 1, 0:N], lhsT=kTh[sl][64:128, kb * 128:(kb + 1) * 128], rhs=qTh[sl][64:128, q0:q0 + N],
                            start=True, stop=True), reads=[KTH[sl], QTH[sl]], writes=[PS_S[sb]])
                        return sb

                    sbs = {0: emit_qk(0)}
                    for j in range(nk):
                        if j + 1 < nk:
                            sbs[j + 1] = emit_qk(j + 1)
                        if j == 0 and pend2["s0"] is not None:
                            pend2["s0"](); pend2["s0"] = None
                        sb = sbs[j]
                        ei = j % 3
                        act.op(lambda e, sb=sb, ei=ei: e.activation(out=E[ei][:, :, 0:N], in_=P[:, 2 * sb:2 * sb + 2, 0:N],
                                                                    func=AF.Exp, scale=0.125),
                               reads=[PS_S[sb]], writes=[EB[ei]])
                        kb = kbs[j]
                        st_, sp_ = (j == 0), (j == nk - 1)
                        if j == 0:
                            dve.op(lambda e, ei=ei: e.tensor_copy(out=acc0[:, 0:N], in_=E[ei][:, 0, 0:N]),
                                   reads=[EB[ei]], writes=[ACC0])
                        else:
                            dve.op(lambda e, ei=ei: e.tensor_tensor(out=acc0[:, 0:N], in0=acc0[:, 0:N], in1=E[ei][:, 0, 0:N], op=ALU.add),
                                   reads=[EB[ei], ACC0], writes=[ACC0])
                        for c in range(2):
                            pe.op_noinc(lambda e, c=c, ei=ei, kb=kb, st_=st_, sp_=sp_: e.matmul(
                                P[:, 4 + 2 * c, 0:N], lhsT=vh[sl][:, kb, :], rhs=E[ei][:, c, 0:N], start=st_, stop=sp_),
                                reads=[VH[sl], EB[ei]], writes=[PO[2 * c]])
                        pe.op(lambda e, ei=ei, st_=st_, sp_=sp_: e.matmul(
                            P[:, 7, 0:N], lhsT=ones_b, rhs=E[ei][:, 1, 0:N], start=st_, stop=sp_),
                            reads=[ONB, EB[ei]], writes=[PO[3]])
                        for nm, jj in (("a1", 1), ("a2", 8), ("b", 12)):
                            if j == min(jj, nk - 1) and pend2[nm] is not None:
                                pend2[nm](); pend2[nm] = None
                    for nm in ("s0", "a1", "a2", "b"):
                        if pend2[nm] is not None:
                            pend2[nm](); pend2[nm] = None
                    act.op(lambda e: e.activation(out=o0c[:, 0:N], in_=P[:, 4, 0:N], func=AF.Copy), reads=[PO[0]], writes=[O0C])
                    dve.op(lambda e: e.tensor_copy(out=s1c[:, 0:N], in_=P[:, 7, 0:N]), reads=[PO[3]], writes=[S1C])
                    dve.op(lambda e: e.tensor_copy(out=o1c[:, 0:N], in_=P[:, 6, 0:N]), reads=[PO[2]], writes=[O1C])

                    def part_s0(N=N):
                        pe.op(lambda e: e.matmul(P[:, 5, 0:N], lhsT=ones_f, rhs=acc0[:, 0:N], start=True, stop=True),
                              reads=[ONF, ACC0], writes=[PO[1]])

                    def part_a1(N=N):
                        dve.op(lambda e: e.tensor_copy(out=s0c[:, 0:N], in_=P[:, 5, 0:N]), reads=[PO[1]], writes=[S0C])
                        pool.op(lambda e: e.tensor_tensor(out=r1[:, 0:N], in0=s1c[:, 0:N], in1=negc[:, 0:1].to_broadcast([128, N]),
                                                          op=ALU.pow), reads=[S1C, NEGC], writes=[R1])
                        pool.op(lambda e: e.tensor_tensor(out=r0[:, 0:N], in0=s0c[:, 0:N], in1=negc[:, 0:1].to_broadcast([128, N]),
                                                          op=ALU.pow), reads=[S0C, NEGC], writes=[R0])

                    def part_a2(q0=q0, N=N):
                        dve.op(lambda e: e.tensor_tensor(out=t1_[:, 0:N], in0=o1c[:, 0:N], in1=r1[:, 0:N], op=ALU.mult),
                               reads=[O1C, R1], writes=[T1])
                        dve.op(lambda e: e.tensor_tensor(out=t0_[:, 0:N], in0=o0c[:, 0:N], in1=r0[:, 0:N], op=ALU.mult),
                               reads=[O0C, R0], writes=[T0])
                        dve.op(lambda e: e.scalar_tensor_tensor(out=o_all[:, q0:q0 + N], in0=t1_[:, 0:N], scalar=lamc[:, 2:3],
                                                                in1=t0_[:, 0:N], op0=ALU.mult, op1=ALU.add),
                               reads=[T0, T1, LAMC], writes=[OALL])
                        dve.op(lambda e: e.tensor_tensor(out=osq[:, 0:N], in0=o_all[:, q0:q0 + N], in1=o_all[:, q0:q0 + N],
                                                         op=ALU.mult), reads=[OALL], writes=[OSQ])

                    def part_b(q0=q0, N=N):
                        pe.op(lambda e: e.matmul(P[:, 5, 0:N], lhsT=ones_f, rhs=osq[:, 0:N], start=True, stop=True),
                              reads=[ONF, OSQ], writes=[PO[1]])
                        dve.op(lambda e: e.tensor_scalar(out=ss_all[:, q0:q0 + N], in0=P[:, 5, 0:N], scalar1=1.0 / 128, scalar2=EPS,
                                                         op0=ALU.mult, op1=ALU.add), reads=[PO[1]], writes=[SSALL])
                        pool.op(lambda e: e.tensor_tensor(out=ss_all[:, q0:q0 + N], in0=ss_all[:, q0:q0 + N],
                                                          in1=negc[:, 1:2].to_broadcast([128, N]), op=ALU.pow),
                                reads=[SSALL, NEGC], writes=[SSALL])
                    pend2["s0"], pend2["a1"], pend2["a2"], pend2["b"] = part_s0, part_a1, part_a2, part_b
                for nm in ("s0", "a1", "a2", "b"):
                    if pend2[nm] is not None:
                        pend2[nm](); pend2[nm] = None
                dve.op(lambda e: e.scalar_tensor_tensor(out=a_st[:, 0:nq_tot], in0=o_all[:, 0:nq_tot], scalar=lamc[:, 7:8],
                                                        in1=ss_all[:, 0:nq_tot], op0=ALU.mult, op1=ALU.mult),
                       reads=[OALL, SSALL, LAMC], writes=[AST])
                sp.dma(aT_d[l, si, h][:, 0:nq_tot], a_st[:, 0:nq_tot], reads=[AST], writes=[])
            fw.barrier()
            ar.reset(m)

        def pass3(l, si, last):
            m = ar.mark()
            xts = [ar.alloc([128, 4, 1024], F32) for _ in range(2)]; XTS = [Buf("xt0"), Buf("xt1")]
            hTs = [ar.alloc([128, 8, 512], BF16) for _ in range(2)]; HTS = [Buf("hT0"), Buf("hT1")]
            ssa = ar.alloc([128, 12], F32)
            tmp = (ssa[:, 0:4], ssa[:, 4:8], ssa[:, 8:12], ar.alloc([128, 1024], BF16), ar.alloc([128, 1024], BF16),
                   Buf("ss"), Buf("junk"), Buf("xh"))
            zin = ar.alloc([128, 4, 512 + 32], BF16); ZIN = Buf("zin")
            cacc = ar.alloc([128, 4, 512], F32); CACC = [Buf(f"cacc{j}") for j in range(4)]
            csq = ar.alloc([128, 4, 512], F32); CSQ = Buf("csq")
            mean = ar.alloc([128, 512], F32); MEAN = Buf("mean")
            var = ar.alloc([128, 512], F32); VAR = Buf("var")
            rstd = ar.alloc([128, 512], F32); RSTD = Buf("rstd")
            sconv = ar.alloc([128, 4, 512], BF16); SCONV = Buf("sconv")
            attn_t = ar.alloc([128, 8, 512], BF16); ATT = Buf("attn_t")
            gs = [ar.alloc([128, 512], F32) for _ in range(2)]; GS = [Buf("gs0"), Buf("gs1")]
            tt = [ar.alloc([128, 512], F32) for _ in range(2)]; TTB = [Buf("tt0"), Buf("tt1")]
            merged = ar.alloc([128, 8, 512], BF16); MERGED = Buf("merged")
            rtmp = [ar.alloc([128, 512], F32) for _ in range(2)]; RTMP = [Buf("rtmp0"), Buf("rtmp1")]
            aT = ar.alloc([128, 32, 512], BF16); AT = Buf("aT")
            rl = [ar.alloc([128, 512], BF16) for _ in range(2)]; RL = [Buf("rl0"), Buf("rl1")]
            o_st = [ar.alloc([128, 512], F32) for _ in range(2)]; OST = [Buf("ost0"), Buf("ost1")]
            tiles = [t for t in stream_tiles(si) if not (last and t[0] == "ctx")]
            units = []
            for _ in tiles:
                for dp in range(4):
                    units.append(wco_b[l][:, dp * 256:(dp + 1) * 256].rearrange("(k p) c -> p k c", p=128))
                    units.append(wao_b[l][:, dp * 256:(dp + 1) * 256].rearrange("(k p) c -> p k c", p=128))
                    units.append(win_b[l][:, GATE_OFF + dp * 256:GATE_OFF + (dp + 1) * 256].rearrange("(k p) c -> p k c", p=128))
                    units.append(win_b[l][:, GATE_OFF + D + dp * 256:GATE_OFF + D + (dp + 1) * 256].rearrange("(k p) c -> p k c", p=128))
                for hf in range(2):
                    for kh in range(2):
                        units.append(wo_b[l][kh * 512:(kh + 1) * 512, hf * 512:(hf + 1) * 512].rearrange("(k p) c -> p k c", p=128))
                for fu in range(16):
                    units.append(w1_b[l][:, fu * 256:(fu + 1) * 256].rearrange("(k p) c -> p k c", p=128))
                for hf in range(2):
                    for f4 in range(8):
                        units.append(w2_b[l][f4 * 512:(f4 + 1) * 512, hf * 512:(hf + 1) * 512].rearrange("(k p) c -> p k c", p=128))
            sp._wait(WCAST_IN[l].w)
            sp._wait(WCAST_REST[l].w)
            ws.extend(units)
            def conv_stage(tile):
                kind, t0, T, tok0 = tile
                lo, hi = (0, S) if kind == "lat" else (S, TT)
                a = max(lo, tok0 - 15)
                b = min(hi, tok0 + T + 15)
                pool.op(lambda e: e.memset(zin, 0.0), reads=[], writes=[ZIN])
                sp.dma(zin[:, :, a - (tok0 - 15):b - (tok0 - 15)], zT_d[l, si][:, :, a:b].rearrange("j p t -> p j t"),
                       reads=[], writes=[ZIN])
                for tap in range(CK):
                    for j in range(4):
                        wcol = PT2[:, tap * 4 + j:tap * 4 + j + 1]
                        if tap == 0:
                            dve.op(lambda e, j=j, wcol=wcol: e.tensor_scalar(out=cacc[:, j, 0:T], in0=zin[:, j, 0:T], scalar1=wcol,
                                                                             scalar2=PT1[:, 88 + j:89 + j], op0=ALU.mult, op1=ALU.add),
                                   reads=[ZIN, PT2B, PT1B], writes=[CACC[j]])
                        else:
                            dve.op(lambda e, j=j, wcol=wcol, tap=tap: e.scalar_tensor_tensor(
                                out=cacc[:, j, 0:T], in0=zin[:, j, tap:tap + T], scalar=wcol, in1=cacc[:, j, 0:T],
                                op0=ALU.mult, op1=ALU.add), reads=[ZIN, PT2B, CACC[j]], writes=[CACC[j]])

            def front_pre(ti):
                kind, t0, T, tok0 = tiles[ti]
                nb = T // 128
                norm_stats(xts[ti % 2], XTS[ti % 2], nb, tmp)
                for j in range(4):
                    act.op(lambda e, j=j: e.activation(out=csq[:, j, 0:T], in_=cacc[:, j, 0:T], func=AF.Square),
                           reads=[CACC[j]], writes=[CSQ])

            def front_stage(ti):
                kind, t0, T, tok0 = tiles[ti]
                nb = T // 128
                s = si if kind == "lat" else NB
                xt, XT = xts[ti % 2], XTS[ti % 2]
                hT, HT = hTs[0], HTS[0]
                sp.dma(attn_t[:, :, 0:T], aT_d[l, si][:, :, tok0:tok0 + T].rearrange("h p t -> p h t"), reads=[], writes=[ATT])
                norm_T(xt, XT, nb, s, 1, hT, HT, tmp)
                bm_, bq_ = ps_next(), ps_next()
                for j in range(4):
                    f = lambda e, j=j: e.matmul(psf(bm_)[:, 0:T], lhsT=ones_f, rhs=cacc[:, j, 0:T], start=(j == 0), stop=(j == 3))
                    (pe.op if j == 3 else pe.op_noinc)(f, reads=[ONF, CACC[j]], writes=[PB[bm_]])
                for j in range(4):
                    f = lambda e, j=j: e.matmul(psf(bq_)[:, 0:T], lhsT=ones_f, rhs=csq[:, j, 0:T], start=(j == 0), stop=(j == 3))
                    (pe.op if j == 3 else pe.op_noinc)(f, reads=[ONF, CSQ], writes=[PB[bq_]])
                dve.op(lambda e: e.tensor_scalar(out=mean[:, 0:T], in0=psf(bm_)[:, 0:T], scalar1=LN_SCALE, scalar2=None, op0=ALU.mult),
                       reads=[PB[bm_]], writes=[MEAN])
                dve.op(lambda e: e.tensor_tensor(out=var[:, 0:T], in0=mean[:, 0:T], in1=mean[:, 0:T], op=ALU.mult),
                       reads=[MEAN], writes=[VAR])
                dve.op(lambda e: e.scalar_tensor_tensor(out=var[:, 0:T], in0=psf(bq_)[:, 0:T], scalar=LN_SCALE, in1=var[:, 0:T],
                                                        op0=ALU.mult, op1=ALU.subtract), reads=[PB[bq_], VAR], writes=[VAR])
                act.op(lambda e: e.activation(out=rstd[:, 0:T], in_=var[:, 0:T], func=AF.Sqrt, bias=epsc[:, 0:1], scale=1.0),
                       reads=[VAR, EPSB], writes=[RSTD])
                dve.op(lambda e: e.reciprocal(out=rstd[:, 0:T], in_=rstd[:, 0:T]), reads=[RSTD], writes=[RSTD])
                for j in range(4):
                    dve.op(lambda e, j=j: e.tensor_tensor(out=cacc[:, j, 0:T], in0=cacc[:, j, 0:T], in1=mean[:, 0:T], op=ALU.subtract),
                           reads=[CACC[j], MEAN], writes=[CACC[j]])
                    dve.op(lambda e, j=j: e.tensor_tensor(out=cacc[:, j, 0:T], in0=cacc[:, j, 0:T], in1=rstd[:, 0:T], op=ALU.mult),
                           reads=[CACC[j], RSTD], writes=[CACC[j]])
                    act.op(lambda e, j=j: e.activation(out=sconv[:, j, 0:T], in_=cacc[:, j, 0:T], func=AF.Silu,
                                                       scale=PT1[:, 80 + j:81 + j], bias=PT1[:, 84 + j:85 + j]),
                           reads=[CACC[j], PT1B], writes=[SCONV])

            def load_x(ti):
                kind, t0, T, tok0 = tiles[ti]
                rd = []
                if l > 0:
                    rd.append(XRES[si] if kind == "lat" else HCTX[si])
                sp.dma(xts[ti % 2][:, 0:T // 128, :], x_src(l, si, kind, t0, T), reads=rd, writes=[XTS[ti % 2]])

            load_x(0)
            conv_stage(tiles[0])
            front_pre(0)
            front_stage(0)
            cur_stream = None
            for ti, (kind, t0, T, tok0) in enumerate(tiles):
                nb = T // 128
                s = si if kind == "lat" else NB
                if cur_stream != s:
                    build_gbc(s)
                    cur_stream = s
                xt, XT = xts[ti % 2], XTS[ti % 2]
                hT, HT = hTs[0], HTS[0]
                h2T, H2T = hTs[1], HTS[1]
                if ti + 1 < len(tiles):
                    load_x(ti + 1)
                for dp in range(4):
                    wco_u = ws.get()
                    wao_u = ws.get()
                    wgc_u = ws.get()
                    wga_u = ws.get()
                    for d2 in range(2):
                        dc = dp * 2 + d2
                        byc, bya, bgc, bga = ps_next(), ps_next(), ps_next(), ps_next()
                        cc = d2 * 128
                        for k in range(4):
                            f = lambda e, k=k, cc=cc, wv=wco_u[0]: e.matmul(psf(byc)[:, 0:T], lhsT=wv[:, k, cc:cc + 128],
                                                                            rhs=sconv[:, k, 0:T], start=(k == 0), stop=(k == 3))
                            (pe.op if k == 3 else pe.op_noinc)(f, reads=[wco_u[1], SCONV], writes=[PB[byc]])
                        for (bb, wu, rhs_t, RB) in ((bya, wao_u, attn_t, ATT), (bgc, wgc_u, hT, HT), (bga, wga_u, hT, HT)):
                            for k in range(8):
                                f = lambda e, k=k, bb=bb, wv=wu[0], rhs_t=rhs_t, d2=d2: e.matmul(
                                    psf(bb)[:, 0:T], lhsT=wv[:, k, d2 * 128:(d2 + 1) * 128], rhs=rhs_t[:, k, 0:T],
                                    start=(k == 0), stop=(k == 7))
                                (pe.op if k == 7 else pe.op_noinc)(f, reads=[wu[1], RB], writes=[PB[bb]])
                        act.op(lambda e, dc=dc, bgc=bgc: e.activation(out=gs[0][:, 0:T], in_=psf(bgc)[:, 0:T], func=AF.Sigmoid,
                                                                      bias=PT1[:, 64 + dc:65 + dc], scale=1.0),
                               reads=[PB[bgc], PT1B], writes=[GS[0]])
                        act.op(lambda e, dc=dc, bga=bga: e.activation(out=gs[1][:, 0:T], in_=psf(bga)[:, 0:T], func=AF.Sigmoid,
                                                                      bias=PT1[:, 72 + dc:73 + dc], scale=1.0),
                               reads=[PB[bga], PT1B], writes=[GS[1]])
                        dve.op(lambda e, byc=byc: e.tensor_tensor(out=tt[0][:, 0:T], in0=psf(byc)[:, 0:T], in1=gs[0][:, 0:T], op=ALU.mult),
                               reads=[PB[byc], GS[0]], writes=[TTB[0]])
                        dve.op(lambda e, bya=bya: e.tensor_tensor(out=tt[1][:, 0:T], in0=psf(bya)[:, 0:T], in1=gs[1][:, 0:T], op=ALU.mult),
                               reads=[PB[bya], GS[1]], writes=[TTB[1]])
                        dve.op(lambda e, dc=dc: e.tensor_tensor(out=merged[:, dc, 0:T], in0=tt[0][:, 0:T], in1=tt[1][:, 0:T], op=ALU.add),
                               reads=[TTB[0], TTB[1]], writes=[MERGED])
                for hf in range(2):
                    wu = [ws.get() for _ in range(2)]
                    for n in range(nb):
                        b = ps_next()
                        for k in range(8):
                            wv, wb = wu[k // 4]
                            f = lambda e, b=b, k=k, n=n, wv=wv: e.matmul(psf(b), lhsT=merged[:, k, n * 128:(n + 1) * 128],
                                                                         rhs=wv[:, k % 4, :], start=(k == 0), stop=(k == 7))
                            (pe.op if k == 7 else pe.op_noinc)(f, reads=[wb, MERGED], writes=[PB[b]])
                        ri = n % 2
                        dve.op(lambda e, b=b, hf=hf, ri=ri: e.tensor_tensor(out=rtmp[ri], in0=psf(b), in1=gbc[0][:, hf * 512:(hf + 1) * 512],
                                                                            op=ALU.mult), reads=[PB[b], GBC[0]], writes=[RTMP[ri]])
                        dve.op(lambda e, n=n, hf=hf, ri=ri: e.tensor_tensor(out=xt[:, n, hf * 512:(hf + 1) * 512],
                                                                            in0=xt[:, n, hf * 512:(hf + 1) * 512], in1=rtmp[ri], op=ALU.add),
                               reads=[XT, RTMP[ri]], writes=[XT])
                norm_to_hT(xt, XT, nb, s, 2, h2T, H2T, tmp)
                if ti + 1 < len(tiles):
                    conv_stage(tiles[ti + 1])
                for fu in range(16):
                    wv, wb = ws.get()
                    for fc in range(2):
                        fi = fu * 2 + fc
                        b = ps_next()
                        for k in range(8):
                            f = lambda e, b=b, k=k, fc=fc, wv=wv: e.matmul(psf(b)[:, 0:T], lhsT=wv[:, k, fc * 128:(fc + 1) * 128],
                                                                           rhs=h2T[:, k, 0:T], start=(k == 0), stop=(k == 7))
                            (pe.op if k == 7 else pe.op_noinc)(f, reads=[wb, H2T], writes=[PB[b]])
                        ri = fi % 2
                        act.op(lambda e, b=b, ri=ri: e.activation(out=rl[ri][:, 0:T], in_=psf(b)[:, 0:T], func=AF.Relu),
                               reads=[PB[b]], writes=[RL[ri]])
                        act.op(lambda e, fi=fi, ri=ri: e.activation(out=aT[:, fi, 0:T], in_=rl[ri][:, 0:T], func=AF.Square),
                               reads=[RL[ri]], writes=[AT])
                dst_base, DSTB = (out_d, XRES[si]) if kind == "lat" else (hctx_d, HCTX[si])
                if ti + 1 < len(tiles):
                    front_pre(ti + 1)
                for hf in range(2):
                    banks = [ps_next() for _ in range(nb)]
                    for f4 in range(8):
                        wv, wb = ws.get()
                        for n in range(nb):
                            for f_ in range(4):
                                fi = f4 * 4 + f_
                                f = lambda e, n=n, fi=fi, f_=f_, wv=wv: e.matmul(psf(banks[n]), lhsT=aT[:, fi, n * 128:(n + 1) * 128],
                                                                                 rhs=wv[:, f_, :], start=(fi == 0), stop=(fi == 31))
                                (pe.op if (fi == 31 or f_ == 3) else pe.op_noinc)(f, reads=[wb, AT], writes=[PB[banks[n]]])
                    for n in range(nb):
                        ri = n % 2
                        dve.op(lambda e, n=n, hf=hf, ri=ri: e.tensor_tensor(out=rtmp[ri], in0=psf(banks[n]),
                                                                            in1=gbc[1][:, hf * 512:(hf + 1) * 512], op=ALU.mult),
                               reads=[PB[banks[n]], GBC[1]], writes=[RTMP[ri]])
                        dve.op(lambda e, n=n, hf=hf, ri=ri: e.tensor_tensor(out=o_st[ri], in0=xt[:, n, hf * 512:(hf + 1) * 512],
                                                                            in1=rtmp[ri], op=ALU.add),
                               reads=[XT, RTMP[ri]], writes=[OST[ri]])
                        sp.dma(dst_base[si, t0 + n * 128:t0 + (n + 1) * 128, hf * 512:(hf + 1) * 512], o_st[ri],
                               reads=[OST[ri]], writes=[DSTB])
                    if hf == 0 and ti + 1 < len(tiles):
                        front_stage(ti + 1)
            ws.done()
            fw.barrier()
            ar.reset(m)

        for l in range(DEPTH):
            last = (l == DEPTH - 1)
            layer_setup(l)
            for si in range(NB):
                pass1(l, si)
                pass2(l, si, last)
                pass3(l, si, last)
        fw.finish()
    return nc


def rope_tables(S):
    f = np.float32
    rows = S // 64
    row = np.repeat(np.arange(rows, dtype=f), 64)
    col = np.tile(np.arange(64, dtype=f), rows)
    inv = np.power(f(10000.0), -np.arange(16, dtype=f) / f(16)).astype(f)
    ang = np.concatenate([row[:, None] * inv, col[:, None] * inv], axis=-1).astype(f)
    return np.cos(ang).astype(f), np.sin(ang).astype(f)


def make_in_maps(inputs, n_cores, NB):
    S = inputs["x"].shape[1]
    cos, sin = rope_tables(S)
    ident = np.eye(128, dtype=np.float32)
    maps = []
    for ci in range(n_cores):
        sl = slice(ci * NB, (ci + 1) * NB)
        m = {}
        for k, v in inputs.items():
            v = np.asarray(v)
            if k in ("x", "c", "ctx"):
                m[k] = np.ascontiguousarray(v[sl])
            elif k in ("lam_q", "lam_k"):
                m[k] = np.ascontiguousarray(v.reshape(v.shape[0], 128))
            else:
                m[k] = np.ascontiguousarray(v)
        m["ident"] = ident
        m["rope_cos"] = cos
        m["rope_sin"] = sin
        maps.append(m)
    return maps


def kernel(**inputs):
    x = np.asarray(inputs["x"])
    B, S, _ = x.shape
    C = np.asarray(inputs["ctx"]).shape[1]
    DEPTH = np.asarray(inputs["w_in"]).shape[0]
    n_cores = N_CORES if B % N_CORES == 0 else 1
    NB = B // n_cores
    nc = build_program(S, C, NB, DEPTH)
    maps = make_in_maps(inputs, n_cores, NB)
    res = run_bass_kernel_spmd(nc, maps, core_ids=list(range(n_cores)))
    out = np.concatenate([np.asarray(r["out"]) for r in res.results], axis=0)
    return out.astype(np.float32)
```

```python
import math
from contextlib import ExitStack

import numpy as np
import ml_dtypes
import concourse.bass as bass
import concourse.mybir as mybir
from concourse.bass_utils import run_bass_kernel_spmd

F32 = mybir.dt.float32
BF16 = mybir.dt.bfloat16
ALU = mybir.AluOpType
AF = mybir.ActivationFunctionType
AX = mybir.AxisListType

D = 1024
KD = 8
NH = 8
IN_COLS = 6144
DFF = 4096
CW = 512
CK = 31
Q_OFF, K_OFF, V_OFF, GATE_OFF = 1024, 2048, 3072, 4096
EPS = 1e-6
N_CORES = 8


class Buf:
    def __init__(self, name, acc=False):
        self.name = name
        self.w = {}
        self.r = {}
        self.acc = acc


def _merge(dst, src):
    for k, (s, v) in src.items():
        if k not in dst or dst[k][1] < v:
            dst[k] = (s, v)


class _Rec:
    def __init__(self):
        self.call = None

    def __getattr__(self, name):
        def _f(*a, **k):
            assert self.call is None
            self.call = (name, a, k)
            return self
        return _f


def _record(fn):
    r = _Rec()
    fn(r)
    assert r.call is not None
    return r.call


class Queue:
    def __init__(self, name, sem, is_pe=False):
        self.name, self.sem, self.is_pe = name, sem, is_pe
        self.cnt = 0
        self.seen = {}
        self.ops = []
        self.dma_sems = []
        self.dma_cnt = []
        self.dma_rr = 0
        self.pending_noinc = False

    def _wait(self, tok):
        for sid, (sem, v) in tok.items():
            if self.is_pe and sem is self.sem:
                continue
            if self.seen.get(sid, 0) < v:
                self.seen[sid] = v
                self.ops.append(lambda e, sem=sem, v=v: e.wait_ge(sem, v))

    def _deps(self, reads, writes):
        for b in reads:
            self._wait(b.w)
        for b in writes:
            if not b.acc:
                self._wait(b.w)
            self._wait(b.r)

    def _commit(self, tok, reads, writes):
        for b in reads:
            _merge(b.r, tok)
        for b in writes:
            if b.acc:
                _merge(b.w, tok)
            else:
                b.w = dict(tok)
                b.r = {}

    def op(self, fn, reads=(), writes=()):
        self._deps(reads, writes)
        self.cnt += 1
        sem = self.sem
        name, a, k = _record(fn)
        self.ops.append(lambda e, name=name, a=a, k=k, sem=sem: getattr(e, name)(*a, **k).then_inc(sem, 1))
        tok = {id(sem): (sem, self.cnt)}
        self._commit(tok, reads, writes)
        self.pending_noinc = False
        return tok

    def op_noinc(self, fn, reads=(), writes=()):
        assert self.is_pe
        self._deps(reads, writes)
        name, a, k = _record(fn)
        self.ops.append(lambda e, name=name, a=a, k=k: getattr(e, name)(*a, **k))
        tok = {id(self.sem): (self.sem, self.cnt + 1)}
        self._commit(tok, reads, writes)
        self.pending_noinc = True
        return tok

    def dma(self, out, in_, reads=(), writes=()):
        self._deps(reads, writes)
        i = self.dma_rr
        self.dma_rr = (self.dma_rr + 1) % len(self.dma_sems)
        sem = self.dma_sems[i]
        prev = self.dma_cnt[i]
        if prev:
            self._wait({id(sem): (sem, prev)})
        self.dma_cnt[i] = prev + 16
        self.ops.append(lambda e, out=out, in_=in_, sem=sem:
                        e.dma_start(out=out, in_=in_).then_inc(sem, 16))
        tok = {id(sem): (sem, prev + 16)}
        self._commit(tok, reads, writes)
        return tok

    def all_tokens(self):
        t = {}
        if self.cnt:
            t[id(self.sem)] = (self.sem, self.cnt)
        for s, c in zip(self.dma_sems, self.dma_cnt):
            if c:
                t[id(s)] = (s, c)
        return t


class FW:
    def __init__(self, nc, stack, n_sync_dma=16, n_pool_dma=6):
        self.nc = nc
        mk = lambda n: stack.enter_context(nc.semaphore(n))
        self.pe = Queue("pe", mk("s_pe"), is_pe=True)
        self.act = Queue("act", mk("s_act"))
        self.dve = Queue("dve", mk("s_dve"))
        self.pool = Queue("pool", mk("s_pool"))
        self.sp = Queue("sp", mk("s_sp"))
        self.queues = [self.pe, self.act, self.dve, self.pool, self.sp]
        for q, n in ((self.sp, n_sync_dma), (self.pool, n_pool_dma)):
            q.dma_sems = [mk(f"d_{q.name}{i}") for i in range(n)]
            q.dma_cnt = [0] * n

    def barrier(self):
        assert not self.pe.pending_noinc
        tok = {}
        for q in self.queues:
            _merge(tok, q.all_tokens())
        for q in self.queues:
            q._wait(tok)

    def finish(self):
        self.barrier()
        nc = self.nc
        with nc.Block() as block:
            @block.tensor
            def _(e):
                for f in self.pe.ops:
                    f(e)

            @block.scalar
            def _(e):
                for f in self.act.ops:
                    f(e)

            @block.vector
            def _(e):
                for f in self.dve.ops:
                    f(e)

            @block.gpsimd
            def _(e):
                for f in self.pool.ops:
                    f(e)

            @block.sync
            def _(e):
                for f in self.sp.ops:
                    f(e)


class Arena:
    def __init__(self, U, nbytes):
        self.U, self.nbytes, self.off = U, nbytes, 0

    def mark(self):
        return self.off

    def reset(self, m):
        self.off = m

    def alloc(self, shape, dtype):
        n = 1
        for s in shape[1:]:
            n *= s
        esz = 4 if dtype == F32 else 2
        nb = (n * esz + 63) // 64 * 64
        assert self.off + nb <= self.nbytes, f"SBUF arena overflow {self.off}+{nb}>{self.nbytes}"
        ap = self.U[0:shape[0], self.off // 2:(self.off + n * esz) // 2]
        self.off += nb
        if dtype == F32:
            ap = ap.bitcast(F32)
        if len(shape) == 3:
            ap = ap.rearrange("p (a b) -> p a b", a=shape[1])
        elif len(shape) == 4:
            ap = ap.rearrange("p (a b c) -> p a b c", a=shape[1], b=shape[2])
        return ap


def build_program(S, C, NB, DEPTH):
    TT = S + C
    NKB = TT // 128
    NLT = S // 512
    assert S % 512 == 0 and C % 128 == 0 and C <= 512
    nc = bass.Bass("TRN2", target_bir_lowering=False)

    def din(name, shape, dt=F32):
        return nc.dram_tensor(name, list(shape), dt, kind="ExternalInput").ap()

    def dint(name, shape, dt):
        return nc.dram_tensor(name, list(shape), dt, kind="Internal").ap()

    x_d = din("x", [NB, S, D])
    c_d = din("c", [NB, D])
    ctx_d = din("ctx", [NB, C, D])
    cctx_d = din("c_ctx", [D])
    w_ada_d = din("w_ada", [DEPTH, D, 6 * D])
    b_ada_d = din("b_ada", [DEPTH, 6 * D])
    n1g_d = din("norm1_g", [DEPTH, D])
    w_in_d = din("w_in", [DEPTH, D, IN_COLS])
    bgate_d = din("b_gate", [DEPTH, 2 * D])
    qng_d = din("q_norm_g", [DEPTH, 64])
    kng_d = din("k_norm_g", [DEPTH, 64])
    lamq_d = din("lam_q", [DEPTH, 128])
    lamk_d = din("lam_k", [DEPTH, 128])
    ang_d = din("attn_norm_g", [DEPTH, 128])
    wdw_d = din("w_dw", [DEPTH, CK, CW])
    bdw_d = din("b_dw", [DEPTH, CW])
    lng_d = din("conv_ln_g", [DEPTH, CW])
    lnb_d = din("conv_ln_b", [DEPTH, CW])
    wco_d = din("w_conv_out", [DEPTH, CW, D])
    wao_d = din("w_attn_out", [DEPTH, D, D])
    wo_d = din("w_out", [DEPTH, D, D])
    n2g_d = din("norm2_g", [DEPTH, D])
    w1_d = din("w_mlp1", [DEPTH, D, DFF])
    w2_d = din("w_mlp2", [DEPTH, DFF, D])
    ident_d = din("ident", [128, 128])
    cos_d = din("rope_cos", [S, 32])
    sin_d = din("rope_sin", [S, 32])
    out_d = nc.dram_tensor("out", [NB, S, D], F32, kind="ExternalOutput").ap()

    win_b = dint("win_b", [DEPTH, D, IN_COLS], BF16)
    wco_b = dint("wco_b", [DEPTH, CW, D], BF16)
    wao_b = dint("wao_b", [DEPTH, D, D], BF16)
    wo_b = dint("wo_b", [DEPTH, D, D], BF16)
    w1_b = dint("w1_b", [DEPTH, D, DFF], BF16)
    w2_b = dint("w2_b", [DEPTH, DFF, D], BF16)
    hctx_d = dint("hctx", [NB, C, D], F32)
    qT_d = dint("qT_s", [DEPTH, NB, NH, 128, TT], BF16)
    kT_d = dint("kT_s", [DEPTH, NB, NH, 128, TT], BF16)
    v_d = dint("v_s", [DEPTH, NB, TT, D], BF16)
    zT_d = dint("zT_s", [DEPTH, NB, 4, 128, TT], BF16)
    aT_d = dint("aT_s", [DEPTH, NB, NH, 128, TT], BF16)

    with ExitStack() as st:
        fw = FW(nc, st)
        pe, act, dve, pool, sp = fw.pe, fw.act, fw.dve, fw.pool, fw.sp
        SB_BYTES = 206 * 1024
        U = st.enter_context(nc.sbuf_tensor("U", [128, SB_BYTES // 2], BF16))
        ar = Arena(U, SB_BYTES)
        P = st.enter_context(nc.psum_tensor("P", [128, 8, 512], F32))
        PB = [Buf(f"ps{i}") for i in range(8)]
        ps_rr = [0]

        def ps_next():
            i = ps_rr[0]
            ps_rr[0] = (i + 1) % 8
            return i

        def psf(i):
            return P[:, i, :]

        def psb(i):
            return P[:, i, :].bitcast(BF16)

        ident_f = ar.alloc([128, 128], F32); IDF = Buf("identf")
        ident_b = ar.alloc([128, 128], BF16); IDB = Buf("identb")
        ones_f = ar.alloc([128, 128], F32); ONF = Buf("onesf")
        ones_b = ar.alloc([128, 128], BF16); ONB = Buf("onesb")
        epsc = ar.alloc([128, 1], F32); EPSB = Buf("eps")
        pstage = ar.alloc([128, 128], F32); PST = Buf("pstage")
        PT1 = ar.alloc([128, 92], F32); PT1B = Buf("pt1")
        PT2 = ar.alloc([128, 124], F32); PT2B = Buf("pt2")
        NS3 = NB + 1
        scT = ar.alloc([128, 8, 4], F32); SCT = Buf("scT")
        modT = ar.alloc([128, 48, NS3], F32); MODT = Buf("modT")
        AB = ar.alloc([128, NS3, 4, 8], F32); ABB = Buf("AB")
        gq_bc = ar.alloc([128, 64], F32); GQ = Buf("gq")
        gk_bc = ar.alloc([128, 64], F32); GK = Buf("gk")
        lamw = ar.alloc([128, 4, 128], F32); LAMW = Buf("lamw")
        lamc = ar.alloc([128, 8], F32); LAMC = Buf("lamc")
        Gt = [ar.alloc([128, 128], F32) for _ in range(2)]; GTB = [Buf("gt0"), Buf("gt1")]
        gbc = [ar.alloc([128, 1024], F32) for _ in range(2)]; GBC = [Buf("gbc0"), Buf("gbc1")]
        NSLOT = 8
        wslots = [(ar.alloc([128, 2048], BF16), Buf(f"wslot{i}")) for i in range(NSLOT)]
        common_mark = ar.mark()

        class WStream:
            PF = 4

            def __init__(self):
                self.units = []
                self.issued = 0
                self.nxt = 0

            def extend(self, srcs):
                self.units += srcs

            def _view(self, j):
                slot, buf = wslots[j % NSLOT]
                src = self.units[j]
                a, b = src.shape[1], src.shape[2]
                if src.dtype == F32:
                    v = slot[:, 0:2 * a * b].bitcast(F32)
                else:
                    v = slot[:, 0:a * b]
                return v.rearrange("p (a b) -> p a b", a=a), buf

            def _issue(self, n):
                while self.issued < min(n, len(self.units)):
                    j = self.issued
                    v, buf = self._view(j)
                    sp.dma(v, self.units[j], writes=[buf])
                    self.issued += 1

            def get(self):
                j = self.nxt
                self.nxt += 1
                assert j < len(self.units)
                self._issue(j + 1 + self.PF)
                return self._view(j)

            def done(self):
                assert self.nxt == len(self.units) == self.issued, (self.nxt, len(self.units), self.issued)

        ws = WStream()

        sp.dma(ident_f, ident_d, writes=[IDF])
        dve.op(lambda e: e.tensor_copy(out=ident_b, in_=ident_f), reads=[IDF], writes=[IDB])
        dve.op(lambda e: e.memset(ones_f, 1.0), writes=[ONF])
        dve.op(lambda e: e.memset(ones_b, 1.0), writes=[ONB])
        dve.op(lambda e: e.memset(epsc, EPS), writes=[EPSB])

        WCAST_IN = [Buf(f"wcast_in{l}", acc=True) for l in range(DEPTH)]
        WCAST_REST = [Buf(f"wcast_rest{l}", acc=True) for l in range(DEPTH)]
        for l in range(DEPTH):
            for (src, dst, rows) in ((w_in_d, win_b, D), (wco_d, wco_b, CW), (wao_d, wao_b, D),
                                     (wo_d, wo_b, D), (w1_d, w1_b, D), (w2_d, w2_b, DFF)):
                wbuf = WCAST_IN[l] if src is w_in_d else WCAST_REST[l]
                for r0 in range(0, rows, 128):
                    pool.dma(dst[l, r0:r0 + 128, :], src[l, r0:r0 + 128, :], writes=[wbuf])

        m0 = ar.mark()
        crow = ar.alloc([NS3, 1024], F32); CROW = Buf("crow")
        srow = ar.alloc([NS3, 1024], F32); SROW = Buf("srow")
        sp.dma(crow[0:NB, :], c_d, writes=[CROW])
        sp.dma(crow[NB:NB + 1, :], cctx_d.rearrange("(o d) -> o d", o=1), writes=[CROW])
        act.op(lambda e: e.activation(out=srow, in_=crow, func=AF.Silu), reads=[CROW], writes=[SROW])
        bi = ps_next()
        for k in range(8):
            pe.op(lambda e, k=k: e.transpose(out=psf(bi)[:, k * 4:k * 4 + NS3], in_=srow[0:NS3, k * 128:(k + 1) * 128],
                                             identity=ident_f[0:NS3, 0:NS3]),
                  reads=[SROW, IDF], writes=[PB[bi]])
        dve.op(lambda e: e.tensor_copy(out=scT[:, :, 0:NS3],
                                       in_=psf(bi)[:, 0:32].rearrange("p (k s) -> p k s", s=4)[:, :, 0:NS3]),
               reads=[PB[bi]], writes=[SCT])
        fw.barrier()
        ar.reset(m0)

        LN_SCALE = 1.0 / CW

        def layer_setup(l):
            lam_init = 0.8 - 0.6 * math.exp(-0.3 * l)
            sp.dma(pstage[0:48, :], b_ada_d[l].rearrange("(r p) -> r p", p=128), writes=[PST])
            sp.dma(pstage[48:56, :], n1g_d[l].rearrange("(r p) -> r p", p=128), writes=[PST])
            sp.dma(pstage[56:64, :], n2g_d[l].rearrange("(r p) -> r p", p=128), writes=[PST])
            sp.dma(pstage[64:80, :], bgate_d[l].rearrange("(r p) -> r p", p=128), writes=[PST])
            sp.dma(pstage[80:84, :], lng_d[l].rearrange("(r p) -> r p", p=128), writes=[PST])
            sp.dma(pstage[84:88, :], lnb_d[l].rearrange("(r p) -> r p", p=128), writes=[PST])
            sp.dma(pstage[88:92, :], bdw_d[l].rearrange("(r p) -> r p", p=128), writes=[PST])
            b1 = ps_next()
            pe.op(lambda e: e.transpose(out=psf(b1)[:, 0:92], in_=pstage[0:92, :], identity=ident_f[0:92, 0:92]),
                  reads=[PST, IDF], writes=[PB[b1]])
            dve.op(lambda e: e.tensor_copy(out=PT1, in_=psf(b1)[:, 0:92]), reads=[PB[b1]], writes=[PT1B])
            sp.dma(pstage[0:124, :], wdw_d[l].rearrange("t (c p) -> (t c) p", p=128), reads=[], writes=[PST])
            b2 = ps_next()
            pe.op(lambda e: e.transpose(out=psf(b2)[:, 0:124], in_=pstage[0:124, :], identity=ident_f[0:124, 0:124]),
                  reads=[PST, IDF], writes=[PB[b2]])
            dve.op(lambda e: e.tensor_copy(out=PT2, in_=psf(b2)[:, 0:124]), reads=[PB[b2]], writes=[PT2B])
            sp.dma(gq_bc, qng_d[l].partition_broadcast(128), writes=[GQ])
            sp.dma(gk_bc, kng_d[l].partition_broadcast(128), writes=[GK])
            sp.dma(lamw[:, 0, :], lamq_d[l].partition_broadcast(128), writes=[LAMW])
            sp.dma(lamw[:, 1, :], lamk_d[l].partition_broadcast(128), writes=[LAMW])
            sp.dma(lamc[:, 3:4], ang_d[l].rearrange("(p o) -> p o", o=1), writes=[LAMC])
            dve.op(lambda e: e.tensor_tensor(out=lamw[:, 2, :], in0=lamw[:, 0, :], in1=lamw[:, 1, :], op=ALU.mult),
                   reads=[LAMW], writes=[LAMW])
            dve.op(lambda e: e.tensor_reduce(out=lamc[:, 0:2], in_=lamw[:, 2, :].rearrange("p (a b) -> p a b", a=2),
                                             axis=AX.X, op=ALU.add), reads=[LAMW, LAMC], writes=[LAMC])
            act.op(lambda e: e.activation(out=lamc[:, 4:6], in_=lamc[:, 0:2], func=AF.Exp), reads=[LAMC], writes=[LAMC])
            dve.op(lambda e: e.tensor_tensor(out=lamc[:, 6:7], in0=lamc[:, 4:5], in1=lamc[:, 5:6], op=ALU.subtract),
                   reads=[LAMC], writes=[LAMC])
            dve.op(lambda e: e.tensor_scalar(out=lamc[:, 2:3], in0=lamc[:, 6:7], scalar1=float(lam_init), scalar2=-1.0,
                                             op0=ALU.add, op1=ALU.mult), reads=[LAMC], writes=[LAMC])
            dve.op(lambda e: e.tensor_scalar(out=lamc[:, 7:8], in0=lamc[:, 3:4], scalar1=float(1.0 - lam_init),
                                             scalar2=None, op0=ALU.mult), reads=[LAMC], writes=[LAMC])
            ws.extend([w_ada_d[l][:, j * 128:(j + 1) * 128].rearrange("(k p) c -> p k c", p=128) for j in range(48)])
            bm = ps_next()
            mps = psf(bm)[:, 0:192].rearrange("p (j s) -> p j s", s=4)
            for j in range(48):
                wv, wb = ws.get()
                for k in range(8):
                    f = (lambda e, j=j, k=k, wv=wv: e.matmul(mps[:, j, 0:NS3], lhsT=wv[:, k, :], rhs=scT[:, k, 0:NS3],
                                                              start=(k == 0), stop=(k == 7)))
                    (pe.op if k == 7 else pe.op_noinc)(f, reads=[wb, SCT], writes=[PB[bm]])
            ws.done()
            dve.op(lambda e: e.tensor_tensor(out=modT, in0=mps[:, :, 0:NS3],
                                             in1=PT1[:, 0:48].unsqueeze(2).to_broadcast([128, 48, NS3]), op=ALU.add),
                   reads=[PB[bm], PT1B], writes=[MODT])
            for s in range(NS3):
                dve.op(lambda e, s=s: e.scalar_tensor_tensor(out=AB[:, s, 0, :], in0=modT[:, 8:16, s], scalar=1.0,
                                                             in1=PT1[:, 48:56], op0=ALU.add, op1=ALU.mult),
                       reads=[MODT, PT1B], writes=[ABB])
                dve.op(lambda e, s=s: e.tensor_copy(out=AB[:, s, 1, :], in_=modT[:, 0:8, s]), reads=[MODT], writes=[ABB])
                dve.op(lambda e, s=s: e.scalar_tensor_tensor(out=AB[:, s, 2, :], in0=modT[:, 32:40, s], scalar=1.0,
                                                             in1=PT1[:, 56:64], op0=ALU.add, op1=ALU.mult),
                       reads=[MODT, PT1B], writes=[ABB])
                dve.op(lambda e, s=s: e.tensor_copy(out=AB[:, s, 3, :], in_=modT[:, 24:32, s]), reads=[MODT], writes=[ABB])
            fw.barrier()

        def build_gbc(s):
            for gi, j0 in ((0, 16), (1, 40)):
                for half in range(2):
                    b = ps_next()
                    for kk in range(4):
                        k = half * 4 + kk
                        g = Gt[k % 2]
                        dve.op(lambda e, g=g, k=k: e.tensor_scalar(out=g, in0=ones_f, scalar1=modT[:, j0 + k, s:s + 1],
                                                                   scalar2=None, op0=ALU.mult),
                               reads=[ONF, MODT], writes=[GTB[k % 2]])
                        pe.op(lambda e, g=g, kk=kk, b=b: e.matmul(psf(b)[:, kk * 128:(kk + 1) * 128], lhsT=g, rhs=ident_f,
                                                                  start=True, stop=True),
                              reads=[GTB[k % 2], IDF], writes=[PB[b]])
                    dve.op(lambda e, b=b, gi=gi, half=half: e.tensor_copy(out=gbc[gi][:, half * 512:(half + 1) * 512],
                                                                          in_=psf(b)),
                           reads=[PB[b]], writes=[GBC[gi]])

        def norm_stats(xt, XT, nb, tmp):
            ss, sd, rstd, junk, xh, SSB, JB, XHB = tmp
            dve.op(lambda e: e.memset(ss, 0.0), reads=[], writes=[SSB])
            for n in range(nb):
                act.op(lambda e, n=n: e.activation(out=junk, in_=xt[:, n, :], func=AF.Square, accum_out=ss[:, n:n + 1]),
                       reads=[XT, SSB], writes=[JB, SSB])
            act.op(lambda e: e.activation(out=sd[:, 0:nb], in_=ss[:, 0:nb], func=AF.Sqrt, bias=epsc[:, 0:1], scale=1.0 / D),
                   reads=[SSB, EPSB], writes=[SSB])
            dve.op(lambda e: e.reciprocal(out=rstd[:, 0:nb], in_=sd[:, 0:nb]), reads=[SSB], writes=[SSB])

        def norm_T(xt, XT, nb, s, which, hT, HT, tmp):
            ss, sd, rstd, junk, xh, SSB, JB, XHB = tmp
            ia, ib = (0, 1) if which == 1 else (2, 3)
            xhs = [(xh, XHB), (junk, JB)]
            base = ((ps_rr[0] + 3) // 4 * 4) % 8
            ps_rr[0] = (base + 4) % 8
            for n in range(nb):
                xb, XB = xhs[n % 2]
                dve.op(lambda e, n=n, xb=xb: e.tensor_scalar(out=xb, in0=xt[:, n, :], scalar1=rstd[:, n:n + 1], scalar2=None,
                                                             op0=ALU.mult), reads=[XT, SSB], writes=[XB])
                b = base + n
                pv = psb(b).rearrange("p (k t) -> p k t", k=8)
                for k in range(8):
                    f = lambda e, k=k, pv=pv, xb=xb: e.transpose(out=pv[:, k, :], in_=xb[:, k * 128:(k + 1) * 128], identity=ident_b)
                    (pe.op if k == 7 else pe.op_noinc)(f, reads=[XB, IDB], writes=[PB[b]])
            pq = P[:, base:base + nb, :].bitcast(BF16).rearrange("p n (k t) -> p n k t", k=8)
            for k in range(8):
                act.op(lambda e, k=k: e.activation(out=hT[:, k, 0:nb * 128].rearrange("p (n t) -> p n t", n=nb), in_=pq[:, :, k, :],
                                                   func=AF.Identity, scale=AB[:, s, ia, k:k + 1], bias=AB[:, s, ib, k:k + 1]),
                       reads=PB[base:base + nb] + [ABB], writes=[HT])

        def norm_to_hT(xt, XT, nb, s, which, hT, HT, tmp):
            norm_stats(xt, XT, nb, tmp)
            norm_T(xt, XT, nb, s, which, hT, HT, tmp)

        def stream_tiles(si):
            tiles = [("lat", i * 512, 512, i * 512) for i in range(NLT)]
            for c0 in range(0, C, 512):
                tiles.append(("ctx", c0, min(512, C - c0), S + c0))
            return tiles

        def x_src(l, si, kind, t0, T):
            if kind == "lat":
                base = x_d if l == 0 else out_d
            else:
                base = ctx_d if l == 0 else hctx_d
            return base[si, t0:t0 + T, :].rearrange("(n p) d -> p n d", p=128)

        XRES = [Buf(f"xres{si}", acc=True) for si in range(NB)]
        HCTX = [Buf(f"hctx{si}", acc=True) for si in range(NB)]

        def pass1(l, si):
            m = ar.mark()
            xt1 = ar.alloc([128, 4, 1024], F32); XT1 = Buf("xt")
            hTs = [ar.alloc([128, 8, 512], BF16) for _ in range(2)]; HTS = [Buf("hT0"), Buf("hT1")]
            ssa = ar.alloc([128, 12], F32)
            tmp = (ssa[:, 0:4], ssa[:, 4:8], ssa[:, 8:12], ar.alloc([128, 1024], BF16), ar.alloc([128, 1024], BF16),
                   Buf("ss"), Buf("junk"), Buf("xh"))
            NBLK = S // 128
            tabs = [[ar.alloc([128, NBLK, 32], F32) for _ in range(4)] for _ in range(2)]; TABS = Buf("tabs")
            sig = [ar.alloc([128, 512], F32) for _ in range(2)]; SIG = [Buf("sig0"), Buf("sig1")]
            z_st = ar.alloc([128, 4, 512], BF16); ZST = Buf("zst")
            sq = [ar.alloc([128, 4, 512], BF16) for _ in range(2)]; SQ = [Buf("sq0"), Buf("sq1")]
            qn = [ar.alloc([128, 4, 512], F32) for _ in range(2)]; QN = [Buf("qn0"), Buf("qn1")]
            st8 = [ar.alloc([128, 96], F32) for _ in range(2)]; ST8 = [Buf("st80"), Buf("st81")]
            rt = [ar.alloc([128, 4, 8, 32], F32) for _ in range(4)]; RTA = Buf("rta"); RTB = Buf("rtb")
            qr = [ar.alloc([128, 4, 512], BF16) for _ in range(2)]; QR = [Buf("qr0"), Buf("qr1")]
            qT_st = ar.alloc([128, 8, 512], BF16); QTS = Buf("qTst")
            kT_st = ar.alloc([128, 8, 512], BF16); KTS = Buf("kTst")
            v_st = ar.alloc([128, 4, 1024], BF16); VST = Buf("vst")
            cos_t = qn[0].rearrange("p a b -> p (a b)")[:, 0:NBLK * 32].rearrange("p (n i) -> p n i", i=32)
            sin_t = qn[1].rearrange("p a b -> p (a b)")[:, 0:NBLK * 32].rearrange("p (n i) -> p n i", i=32)
            sp.dma(cos_t, cos_d.rearrange("(n p) i -> p n i", p=128), writes=[QN[0]])
            sp.dma(sin_t, sin_d.rearrange("(n p) i -> p n i", p=128), writes=[QN[1]])
            for ty, (gb, GB) in enumerate(((gq_bc, GQ), (gk_bc, GK))):
                g1 = gb[:, 0:32].unsqueeze(1).to_broadcast([128, NBLK, 32])
                g2 = gb[:, 32:64].unsqueeze(1).to_broadcast([128, NBLK, 32])
                for ti_, (src, SB_, gg) in enumerate(((cos_t, QN[0], g1), (sin_t, QN[1], g2), (cos_t, QN[0], g2), (sin_t, QN[1], g1))):
                    dve.op(lambda e, ty=ty, ti_=ti_, src=src, gg=gg: e.tensor_tensor(out=tabs[ty][ti_], in0=src, in1=gg, op=ALU.mult),
                           reads=[SB_, GB], writes=[TABS])
            tiles = stream_tiles(si)
            units = []
            for _ in tiles:
                units += [win_b[l][:, u * 256:(u + 1) * 256].rearrange("(k p) c -> p k c", p=128) for u in range(4)]
                for g in range(6):
                    c0 = Q_OFF + g * 512
                    units += [win_b[l][kh * 512:(kh + 1) * 512, c0:c0 + 512].rearrange("(k p) c -> p k c", p=128)
                              for kh in range(2)]
            sp._wait(WCAST_IN[l].w)
            ws.extend(units)
            src_buf = lambda kind: (XRES[si] if kind == "lat" else HCTX[si])
            uc = 0
            quad_rr = 0
            pend = []

            def load_x1(tj):
                kind_, t0_, T_, tok0_ = tiles[tj]
                rd = []
                if l > 0:
                    rd.append(src_buf(kind_))
                sp.dma(xt1[:, 0:T_ // 128, :], x_src(l, si, kind_, t0_, T_), reads=rd, writes=[XT1])

            def stats1(tj):
                norm_stats(xt1, XT1, tiles[tj][2] // 128, tmp)

            def normT1(tj):
                kind_, t0_, T_, tok0_ = tiles[tj]
                norm_T(xt1, XT1, T_ // 128, (si if kind_ == "lat" else NB), 1, hTs[tj % 2], HTS[tj % 2], tmp)

            def load_norm(tj):
                load_x1(tj); stats1(tj); normT1(tj)

            for ti, (kind, t0, T, tok0) in enumerate(tiles):
                nb = T // 128
                s = si if kind == "lat" else NB
                xt, XT = xt1, XT1
                hT, HT = hTs[ti % 2], HTS[ti % 2]
                if ti == 0:
                    load_norm(0)
                if ti + 1 < len(tiles):
                    load_x1(ti + 1)
                cu = [ws.get() for _ in range(4)]
                for j in range(4):
                    ba, bg = ps_next(), ps_next()
                    for (bb, u) in ((ba, cu[j // 2]), (bg, cu[2 + j // 2])):
                        wv, wb = u
                        for k in range(8):
                            f = lambda e, bb=bb, wv=wv, k=k, j=j: e.matmul(
                                psf(bb)[:, 0:T], lhsT=wv[:, k, (j % 2) * 128:(j % 2) * 128 + 128], rhs=hT[:, k, 0:T],
                                start=(k == 0), stop=(k == 7))
                            (pe.op if k == 7 else pe.op_noinc)(f, reads=[wb, HT], writes=[PB[bb]])
                    sg, SG = sig[j % 2], SIG[j % 2]
                    act.op(lambda e, bg=bg, sg=sg: e.activation(out=sg[:, 0:T], in_=psf(bg)[:, 0:T], func=AF.Sigmoid),
                           reads=[PB[bg]], writes=[SG])
                    dve.op(lambda e, ba=ba, sg=sg, j=j: e.tensor_tensor(out=z_st[:, j, 0:T], in0=psf(ba)[:, 0:T],
                                                                        in1=sg[:, 0:T], op=ALU.mult),
                           reads=[PB[ba], SG], writes=[ZST])
                sp.dma(zT_d[l, si][:, :, tok0:tok0 + T].rearrange("j p t -> p j t"), z_st[:, :, 0:T], reads=[ZST], writes=[])
                for g in range(6):
                    wu = [ws.get() for _ in range(2)]
                    typ = g // 2
                    q0b = (quad_rr % 2) * 4
                    quad_rr += 1
                    QB = PB[q0b:q0b + nb]
                    for n in range(nb):
                        b = q0b + n
                        for k in range(8):
                            wv, wb = wu[k // 4]
                            f = lambda e, b=b, wv=wv, k=k, n=n: e.matmul(
                                psf(b), lhsT=hT[:, k, n * 128:(n + 1) * 128], rhs=wv[:, k % 4, :],
                                start=(k == 0), stop=(k == 7))
                            (pe.op if k == 7 else pe.op_noinc)(f, reads=[wb, HT], writes=[PB[b]])
                    pq = P[:, q0b:q0b + nb, :]
                    if len(pend) == 2 or (typ == 2 and pend):
                        pend.pop(0)(4 - q0b)
                    if typ == 2:
                        act.op(lambda e, pq=pq, g=g: e.activation(out=v_st[:, 0:nb, (g - 4) * 512:(g - 3) * 512], in_=pq, func=AF.Copy),
                               reads=QB, writes=[VST])
                        if g == 4 and ti + 1 < len(tiles):
                            normT1(ti + 1)
                        if g == 5:
                            sp.dma(v_d[l, si][tok0:tok0 + T, :].rearrange("(n p) c -> p n c", p=128), v_st[:, 0:nb, :],
                                   reads=[VST], writes=[])
                        continue
                    u = uc % 2
                    uc += 1
                    ty = typ
                    act.op(lambda e, pq=pq, u=u: e.activation(out=sq[u][:, 0:nb, :], in_=pq, func=AF.Square),
                           reads=QB, writes=[SQ[u]])
                    dve.op(lambda e, u=u: e.tensor_reduce(out=st8[u][:, 0:nb * 8],
                                                          in_=sq[u][:, 0:nb, :].rearrange("p n (a b) -> p (n a) b", a=8),
                                                          axis=AX.X, op=ALU.add), reads=[SQ[u]], writes=[ST8[u]])
                    act.op(lambda e, u=u: e.activation(out=st8[u][:, 32:32 + nb * 8], in_=st8[u][:, 0:nb * 8], func=AF.Sqrt,
                                                       bias=epsc[:, 0:1], scale=1.0 / 64), reads=[ST8[u], EPSB], writes=[ST8[u]])
                    dve.op(lambda e, u=u: e.reciprocal(out=st8[u][:, 64:64 + nb * 8], in_=st8[u][:, 32:32 + nb * 8]),
                           reads=[ST8[u]], writes=[ST8[u]])
                    dve.op(lambda e, u=u, pq=pq: e.tensor_tensor(
                        out=qn[u][:, 0:nb, :].rearrange("p n (a b) -> p (n a) b", a=8),
                        in0=pq.rearrange("p n (a b) -> p (n a) b", a=8),
                        in1=st8[u][:, 64:64 + nb * 8].unsqueeze(2).to_broadcast([128, nb * 8, 64]), op=ALU.mult),
                        reads=QB + [ST8[u]], writes=[QN[u]])
                    if kind == "lat":
                        blk0 = t0 // 128
                        q5 = qn[u][:, 0:nb, :].rearrange("p n (a h i) -> p n a h i", a=8, h=2)
                        t1, t2 = q5[:, :, :, 0, :], q5[:, :, :, 1, :]
                        tb = [tabs[ty][i][:, blk0:blk0 + nb, :].unsqueeze(2).to_broadcast([128, nb, 8, 32]) for i in range(4)]
                        o5 = qr[u][:, 0:nb, :].rearrange("p n (a h i) -> p n a h i", a=8, h=2)
                        ra, rb, rc, rd_ = [r[:, 0:nb, :, :] for r in rt]
                        dve.op(lambda e, ra=ra, t1=t1, tb=tb: e.tensor_tensor(out=ra, in0=t1, in1=tb[0], op=ALU.mult),
                               reads=[QN[u], TABS], writes=[RTA])
                        dve.op(lambda e, rb=rb, t2=t2, tb=tb: e.tensor_tensor(out=rb, in0=t2, in1=tb[1], op=ALU.mult),
                               reads=[QN[u], TABS], writes=[RTA])
                        dve.op(lambda e, o5=o5, ra=ra, rb=rb: e.tensor_tensor(out=o5[:, :, :, 0, :], in0=ra, in1=rb, op=ALU.subtract),
                               reads=[RTA], writes=[QR[u]])
                        pool.op(lambda e, rc=rc, t2=t2, tb=tb: e.tensor_tensor(out=rc, in0=t2, in1=tb[2], op=ALU.mult),
                                reads=[QN[u], TABS], writes=[RTB])
                        pool.op(lambda e, rd_=rd_, t1=t1, tb=tb: e.tensor_tensor(out=rd_, in0=t1, in1=tb[3], op=ALU.mult),
                                reads=[QN[u], TABS], writes=[RTB])
                        pool.op(lambda e, o5=o5, rc=rc, rd_=rd_: e.tensor_tensor(out=o5[:, :, :, 1, :], in0=rc, in1=rd_, op=ALU.add),
                                reads=[RTB], writes=[QR[u]])
                    else:
                        gbcast, GB = (gq_bc, GQ) if typ == 0 else (gk_bc, GK)
                        pool.op(lambda e, u=u, gbcast=gbcast: e.tensor_tensor(
                            out=qr[u][:, 0:nb, :].rearrange("p n (a b) -> p (n a) b", a=8),
                            in0=qn[u][:, 0:nb, :].rearrange("p n (a b) -> p (n a) b", a=8),
                            in1=gbcast.unsqueeze(1).to_broadcast([128, nb * 8, 64]), op=ALU.mult),
                            reads=[QN[u], GB], writes=[QR[u]])
                    def emit_T(tb, u=u, typ=typ, g=g, nb=nb, T=T, tok0=tok0):
                      dstT, DSTB = (qT_st, QTS) if typ == 0 else (kT_st, KTS)
                      h0 = (g % 2) * 4
                      for n0 in range(0, nb, 2):
                          bt = tb + n0 // 2
                          pv = psb(bt).rearrange("p (n h t) -> p n h t", n=2, h=4)
                          for nn in range(2):
                              for hh in range(4):
                                  f = lambda e, hh=hh, pv=pv, u=u, nn=nn, n0=n0: e.transpose(
                                      out=pv[:, nn, hh, :], in_=qr[u][:, n0 + nn, hh * 128:(hh + 1) * 128], identity=ident_b)
                                  (pe.op if (nn == 1 and hh == 3) else pe.op_noinc)(f, reads=[QR[u], IDB], writes=[PB[bt]])
                          act.op(lambda e, pv=pv, dstT=dstT, h0=h0, n0=n0: e.activation(
                              out=dstT[:, h0:h0 + 4, n0 * 128:(n0 + 2) * 128].rearrange("p h (n t) -> p h n t", n=2),
                              in_=pv.rearrange("p n h t -> p h n t"), func=AF.Copy),
                              reads=[PB[bt]], writes=[DSTB])
                      if g == 1:
                          sp.dma(qT_d[l, si][:, :, tok0:tok0 + T].rearrange("h p t -> p h t"), qT_st[:, :, 0:T], reads=[QTS], writes=[])
                      if g == 3:
                          sp.dma(kT_d[l, si][:, :, tok0:tok0 + T].rearrange("h p t -> p h t"), kT_st[:, :, 0:T], reads=[KTS], writes=[])

                    pend.append(emit_T)
                    if g == 1 and ti + 1 < len(tiles):
                        stats1(ti + 1)
            ws.done()
            fw.barrier()
            ar.reset(m)

        def pass2(l, si, last):
            m = ar.mark()
            kTh = [ar.alloc([128, TT], BF16) for _ in range(2)]; KTH = [Buf("kth0"), Buf("kth1")]
            qTh = [ar.alloc([128, TT], BF16) for _ in range(2)]; QTH = [Buf("qth0"), Buf("qth1")]
            vh = [ar.alloc([128, NKB, 128], BF16) for _ in range(2)]; VH = [Buf("vh0"), Buf("vh1")]
            E = [ar.alloc([128, 2, 512], BF16) for _ in range(3)]; EB = [Buf(f"E{i}") for i in range(3)]
            r0 = ar.alloc([128, 512], F32); R0 = Buf("r0")
            r1 = ar.alloc([128, 512], F32); R1 = Buf("r1")
            t0_ = ar.alloc([128, 512], F32); T0 = Buf("t0")
            t1_ = ar.alloc([128, 512], F32); T1 = Buf("t1")
            osq = ar.alloc([128, 512], F32); OSQ = Buf("osq")
            acc0 = ar.alloc([128, 512], F32); ACC0 = Buf("acc0")
            o0c = ar.alloc([128, 512], F32); O0C = Buf("o0c")
            o1c = ar.alloc([128, 512], F32); O1C = Buf("o1c")
            s1c = ar.alloc([128, 512], F32); S1C = Buf("s1c")
            o_all = ar.alloc([128, TT], F32); OALL = Buf("oall")
            ss_all = ar.alloc([128, TT], F32); SSALL = Buf("ssall")
            a_st = ar.alloc([128, TT], BF16); AST = Buf("ast")
            PS_S = [Buf("pss0"), Buf("pss1")]
            PO = [Buf("po0"), Buf("pso0"), Buf("po1"), Buf("pso1")]
            sslot = [0]
            qtiles = [(i * 512, 512, list(range(NKB))) for i in range(NLT)]
            if not last:
                qtiles.append((S, C, list(range(S // 128, NKB))))
            nq_tot = S + (0 if last else C)

            def load_head(h):
                sl = h % 2
                sp.dma(kTh[sl], kT_d[l, si, h], writes=[KTH[sl]])
                sp.dma(qTh[sl][:, 0:nq_tot], qT_d[l, si, h][:, 0:nq_tot], writes=[QTH[sl]])
                half = (NKB + 1) // 2
                for a, b in ((0, half), (half, NKB)):
                    sp.dma(vh[sl][:, a:b, :],
                           v_d[l, si][a * 128:b * 128, h * 128:(h + 1) * 128].rearrange("(kb p) c -> p kb c", p=128),
                           writes=[VH[sl]])

            load_head(0)
            for h in range(NH):
                sl = h % 2
                if h + 1 < NH:
                    load_head(h + 1)
                pending = None
                pending_a = None
                pending_s0 = None
                for (q0, N, kbs) in qtiles:
                    nk = len(kbs)

                    def emit_qk(j):
                        sb = sslot[0] % 2
                        sslot[0] += 1
                        kb = kbs[j]
                        pe.op_noinc(lambda e, sb=sb, kb=kb: e.matmul(
                            P[:, 2 * sb, 0:N], lhsT=kTh[sl][0:64, kb * 128:(kb + 1) * 128], rhs=qTh[sl][0:64, q0:q0 + N],
                            start=True, stop=True), reads=[KTH[sl], QTH[sl]], writes=[PS_S[sb]])
                        pe.op(lambda e, sb=sb, kb=kb: e.matmul(
                            P[:, 2 * sb + 1, 0:N], lhsT=kTh[sl][64:128, kb * 128:(kb + 1) * 128], rhs=qTh[sl][64:128, q0:q0 + N],
                            start=True, stop=True), reads=[KTH[sl], QTH[sl]], writes=[PS_S[sb]])
                        return sb

                    sbs = {0: emit_qk(0)}
                    for j in range(nk):
                        if j + 1 < nk:
                            sbs[j + 1] = emit_qk(j + 1)
                        if j == 0 and pending_s0 is not None:
                            pending_s0(); pending_s0 = None
                        sb = sbs[j]
                        ei = j % 3
                        act.op(lambda e, sb=sb, ei=ei: e.activation(out=E[ei][:, :, 0:N], in_=P[:, 2 * sb:2 * sb + 2, 0:N],
                                                                    func=AF.Exp, scale=0.125),
                               reads=[PS_S[sb]], writes=[EB[ei]])
                        kb = kbs[j]
                        st_, sp_ = (j == 0), (j == nk - 1)
                        if j == 0:
                            dve.op(lambda e, ei=ei: e.tensor_copy(out=acc0[:, 0:N], in_=E[ei][:, 0, 0:N]),
                                   reads=[EB[ei]], writes=[ACC0])
                        else:
                            dve.op(lambda e, ei=ei: e.tensor_tensor(out=acc0[:, 0:N], in0=acc0[:, 0:N], in1=E[ei][:, 0, 0:N], op=ALU.add),
                                   reads=[EB[ei], ACC0], writes=[ACC0])
                        for c in range(2):
                            pe.op_noinc(lambda e, c=c, ei=ei, kb=kb, st_=st_, sp_=sp_: e.matmul(
                                P[:, 4 + 2 * c, 0:N], lhsT=vh[sl][:, kb, :], rhs=E[ei][:, c, 0:N], start=st_, stop=sp_),
                                reads=[VH[sl], EB[ei]], writes=[PO[2 * c]])
                        pe.op(lambda e, ei=ei, st_=st_, sp_=sp_: e.matmul(
                            P[:, 7, 0:N], lhsT=ones_b, rhs=E[ei][:, 1, 0:N], start=st_, stop=sp_),
                            reads=[ONB, EB[ei]], writes=[PO[3]])
                        if j == min(1, nk - 1) and pending_a is not None:
                            pending_a(); pending_a = None
                        if j == min(4, nk - 1) and pending is not None:
                            pending(); pending = None
                    if pending_s0 is not None:
                        pending_s0(); pending_s0 = None
                    if pending_a is not None:
                        pending_a(); pending_a = None
                    if pending is not None:
                        pending(); pending = None
                    def part_s0(N=N):
                        pe.op(lambda e: e.matmul(P[:, 5, 0:N], lhsT=ones_f, rhs=acc0[:, 0:N], start=True, stop=True),
                              reads=[ONF, ACC0], writes=[PO[1]])
                    pending_s0 = part_s0
                    act.op(lambda e: e.activation(out=o0c[:, 0:N], in_=P[:, 4, 0:N], func=AF.Copy), reads=[PO[0]], writes=[O0C])
                    dve.op(lambda e: e.tensor_copy(out=s1c[:, 0:N], in_=P[:, 7, 0:N]), reads=[PO[3]], writes=[S1C])
                    dve.op(lambda e: e.tensor_copy(out=o1c[:, 0:N], in_=P[:, 6, 0:N]), reads=[PO[2]], writes=[O1C])

                    def part_a(q0=q0, N=N):
                        dve.op(lambda e: e.tensor_tensor(out=r0[:, 0:N], in0=P[:, 5, 0:N], in1=s1c[:, 0:N], op=ALU.mult),
                               reads=[PO[1], S1C], writes=[R0])
                        dve.op(lambda e: e.reciprocal(out=r1[:, 0:N], in_=r0[:, 0:N]), reads=[R0], writes=[R1])
                        dve.op(lambda e: e.tensor_tensor(out=t0_[:, 0:N], in0=o0c[:, 0:N], in1=s1c[:, 0:N], op=ALU.mult),
                               reads=[O0C, S1C], writes=[T0])
                        dve.op(lambda e: e.tensor_tensor(out=t1_[:, 0:N], in0=o1c[:, 0:N], in1=P[:, 5, 0:N], op=ALU.mult),
                               reads=[O1C, PO[1]], writes=[T1])
                        dve.op(lambda e: e.scalar_tensor_tensor(out=t0_[:, 0:N], in0=t1_[:, 0:N], scalar=lamc[:, 2:3],
                                                                in1=t0_[:, 0:N], op0=ALU.mult, op1=ALU.add),
                               reads=[T0, T1, LAMC], writes=[T0])
                        dve.op(lambda e: e.tensor_tensor(out=o_all[:, q0:q0 + N], in0=t0_[:, 0:N], in1=r1[:, 0:N], op=ALU.mult),
                               reads=[T0, R1], writes=[OALL])
                        dve.op(lambda e: e.tensor_tensor(out=osq[:, 0:N], in0=o_all[:, q0:q0 + N], in1=o_all[:, q0:q0 + N],
                                                         op=ALU.mult), reads=[OALL], writes=[OSQ])

                    def part_b(q0=q0, N=N):
                        pe.op(lambda e: e.matmul(P[:, 5, 0:N], lhsT=ones_f, rhs=osq[:, 0:N], start=True, stop=True),
                              reads=[ONF, OSQ], writes=[PO[1]])
                        dve.op(lambda e: e.tensor_copy(out=ss_all[:, q0:q0 + N], in_=P[:, 5, 0:N]),
                               reads=[PO[1]], writes=[SSALL])
                    pending_a = part_a
                    pending = part_b
                if pending_s0 is not None:
                    pending_s0(); pending_s0 = None
                if pending_a is not None:
                    pending_a(); pending_a = None
                pending(); pending = None
                act.op(lambda e: e.activation(out=ss_all[:, 0:nq_tot], in_=ss_all[:, 0:nq_tot], func=AF.Ln,
                                              bias=epsc[:, 0:1], scale=1.0 / 128), reads=[SSALL, EPSB], writes=[SSALL])
                act.op(lambda e: e.activation(out=ss_all[:, 0:nq_tot], in_=ss_all[:, 0:nq_tot], func=AF.Exp, scale=-0.5),
                       reads=[SSALL], writes=[SSALL])
                dve.op(lambda e: e.scalar_tensor_tensor(out=a_st[:, 0:nq_tot], in0=o_all[:, 0:nq_tot], scalar=lamc[:, 7:8],
                                                        in1=ss_all[:, 0:nq_tot], op0=ALU.mult, op1=ALU.mult),
                       reads=[OALL, SSALL, LAMC], writes=[AST])
                sp.dma(aT_d[l, si, h][:, 0:nq_tot], a_st[:, 0:nq_tot], reads=[AST], writes=[])
            fw.barrier()
            ar.reset(m)

        def pass3(l, si, last):
            m = ar.mark()
            xts = [ar.alloc([128, 4, 1024], F32) for _ in range(2)]; XTS = [Buf("xt0"), Buf("xt1")]
            hTs = [ar.alloc([128, 8, 512], BF16) for _ in range(2)]; HTS = [Buf("hT0"), Buf("hT1")]
            ssa = ar.alloc([128, 12], F32)
            tmp = (ssa[:, 0:4], ssa[:, 4:8], ssa[:, 8:12], ar.alloc([128, 1024], BF16), ar.alloc([128, 1024], BF16),
                   Buf("ss"), Buf("junk"), Buf("xh"))
            zin = ar.alloc([128, 4, 512 + 32], BF16); ZIN = Buf("zin")
            cacc = ar.alloc([128, 4, 512], F32); CACC = [Buf(f"cacc{j}") for j in range(4)]
            csq = ar.alloc([128, 4, 512], F32); CSQ = Buf("csq")
            mean = ar.alloc([128, 512], F32); MEAN = Buf("mean")
            var = ar.alloc([128, 512], F32); VAR = Buf("var")
            rstd = ar.alloc([128, 512], F32); RSTD = Buf("rstd")
            sconv = ar.alloc([128, 4, 512], BF16); SCONV = Buf("sconv")
            attn_t = ar.alloc([128, 8, 512], BF16); ATT = Buf("attn_t")
            gs = [ar.alloc([128, 512], F32) for _ in range(2)]; GS = [Buf("gs0"), Buf("gs1")]
            tt = [ar.alloc([128, 512], F32) for _ in range(2)]; TTB = [Buf("tt0"), Buf("tt1")]
            merged = ar.alloc([128, 8, 512], BF16); MERGED = Buf("merged")
            rtmp = [ar.alloc([128, 512], F32) for _ in range(2)]; RTMP = [Buf("rtmp0"), Buf("rtmp1")]
            aT = ar.alloc([128, 32, 512], BF16); AT = Buf("aT")
            rl = [ar.alloc([128, 512], BF16) for _ in range(2)]; RL = [Buf("rl0"), Buf("rl1")]
            o_st = [ar.alloc([128, 512], F32) for _ in range(2)]; OST = [Buf("ost0"), Buf("ost1")]
            tiles = [t for t in stream_tiles(si) if not (last and t[0] == "ctx")]
            units = []
            for _ in tiles:
                for dp in range(4):
                    units.append(wco_b[l][:, dp * 256:(dp + 1) * 256].rearrange("(k p) c -> p k c", p=128))
                    units.append(wao_b[l][:, dp * 256:(dp + 1) * 256].rearrange("(k p) c -> p k c", p=128))
                    units.append(win_b[l][:, GATE_OFF + dp * 256:GATE_OFF + (dp + 1) * 256].rearrange("(k p) c -> p k c", p=128))
                    units.append(win_b[l][:, GATE_OFF + D + dp * 256:GATE_OFF + D + (dp + 1) * 256].rearrange("(k p) c -> p k c", p=128))
                for hf in range(2):
                    for kh in range(2):
                        units.append(wo_b[l][kh * 512:(kh + 1) * 512, hf * 512:(hf + 1) * 512].rearrange("(k p) c -> p k c", p=128))
                for fu in range(16):
                    units.append(w1_b[l][:, fu * 256:(fu + 1) * 256].rearrange("(k p) c -> p k c", p=128))
                for hf in range(2):
                    for f4 in range(8):
                        units.append(w2_b[l][f4 * 512:(f4 + 1) * 512, hf * 512:(hf + 1) * 512].rearrange("(k p) c -> p k c", p=128))
            sp._wait(WCAST_IN[l].w)
            sp._wait(WCAST_REST[l].w)
            ws.extend(units)
            def conv_stage(tile):
                kind, t0, T, tok0 = tile
                lo, hi = (0, S) if kind == "lat" else (S, TT)
                a = max(lo, tok0 - 15)
                b = min(hi, tok0 + T + 15)
                pool.op(lambda e: e.memset(zin, 0.0), reads=[], writes=[ZIN])
                sp.dma(zin[:, :, a - (tok0 - 15):b - (tok0 - 15)], zT_d[l, si][:, :, a:b].rearrange("j p t -> p j t"),
                       reads=[], writes=[ZIN])
                for tap in range(CK):
                    for j in range(4):
                        wcol = PT2[:, tap * 4 + j:tap * 4 + j + 1]
                        if tap == 0:
                            dve.op(lambda e, j=j, wcol=wcol: e.tensor_scalar(out=cacc[:, j, 0:T], in0=zin[:, j, 0:T], scalar1=wcol,
                                                                             scalar2=PT1[:, 88 + j:89 + j], op0=ALU.mult, op1=ALU.add),
                                   reads=[ZIN, PT2B, PT1B], writes=[CACC[j]])
                        else:
                            dve.op(lambda e, j=j, wcol=wcol, tap=tap: e.scalar_tensor_tensor(
                                out=cacc[:, j, 0:T], in0=zin[:, j, tap:tap + T], scalar=wcol, in1=cacc[:, j, 0:T],
                                op0=ALU.mult, op1=ALU.add), reads=[ZIN, PT2B, CACC[j]], writes=[CACC[j]])

            def front_pre(ti):
                kind, t0, T, tok0 = tiles[ti]
                nb = T // 128
                norm_stats(xts[ti % 2], XTS[ti % 2], nb, tmp)
                for j in range(4):
                    act.op(lambda e, j=j: e.activation(out=csq[:, j, 0:T], in_=cacc[:, j, 0:T], func=AF.Square),
                           reads=[CACC[j]], writes=[CSQ])

            def front_stage(ti):
                kind, t0, T, tok0 = tiles[ti]
                nb = T // 128
                s = si if kind == "lat" else NB
                xt, XT = xts[ti % 2], XTS[ti % 2]
                hT, HT = hTs[0], HTS[0]
                sp.dma(attn_t[:, :, 0:T], aT_d[l, si][:, :, tok0:tok0 + T].rearrange("h p t -> p h t"), reads=[], writes=[ATT])
                norm_T(xt, XT, nb, s, 1, hT, HT, tmp)
                bm_, bq_ = ps_next(), ps_next()
                for j in range(4):
                    f = lambda e, j=j: e.matmul(psf(bm_)[:, 0:T], lhsT=ones_f, rhs=cacc[:, j, 0:T], start=(j == 0), stop=(j == 3))
                    (pe.op if j == 3 else pe.op_noinc)(f, reads=[ONF, CACC[j]], writes=[PB[bm_]])
                for j in range(4):
                    f = lambda e, j=j: e.matmul(psf(bq_)[:, 0:T], lhsT=ones_f, rhs=csq[:, j, 0:T], start=(j == 0), stop=(j == 3))
                    (pe.op if j == 3 else pe.op_noinc)(f, reads=[ONF, CSQ], writes=[PB[bq_]])
                dve.op(lambda e: e.tensor_scalar(out=mean[:, 0:T], in0=psf(bm_)[:, 0:T], scalar1=LN_SCALE, scalar2=None, op0=ALU.mult),
                       reads=[PB[bm_]], writes=[MEAN])
                dve.op(lambda e: e.tensor_tensor(out=var[:, 0:T], in0=mean[:, 0:T], in1=mean[:, 0:T], op=ALU.mult),
                       reads=[MEAN], writes=[VAR])
                dve.op(lambda e: e.scalar_tensor_tensor(out=var[:, 0:T], in0=psf(bq_)[:, 0:T], scalar=LN_SCALE, in1=var[:, 0:T],
                                                        op0=ALU.mult, op1=ALU.subtract), reads=[PB[bq_], VAR], writes=[VAR])
                act.op(lambda e: e.activation(out=rstd[:, 0:T], in_=var[:, 0:T], func=AF.Sqrt, bias=epsc[:, 0:1], scale=1.0),
                       reads=[VAR, EPSB], writes=[RSTD])
                dve.op(lambda e: e.reciprocal(out=rstd[:, 0:T], in_=rstd[:, 0:T]), reads=[RSTD], writes=[RSTD])
                for j in range(4):
                    dve.op(lambda e, j=j: e.tensor_tensor(out=cacc[:, j, 0:T], in0=cacc[:, j, 0:T], in1=mean[:, 0:T], op=ALU.subtract),
                           reads=[CACC[j], MEAN], writes=[CACC[j]])
                    dve.op(lambda e, j=j: e.tensor_tensor(out=cacc[:, j, 0:T], in0=cacc[:, j, 0:T], in1=rstd[:, 0:T], op=ALU.mult),
                           reads=[CACC[j], RSTD], writes=[CACC[j]])
                    act.op(lambda e, j=j: e.activation(out=sconv[:, j, 0:T], in_=cacc[:, j, 0:T], func=AF.Silu,
                                                       scale=PT1[:, 80 + j:81 + j], bias=PT1[:, 84 + j:85 + j]),
                           reads=[CACC[j], PT1B], writes=[SCONV])

            def load_x(ti):
                kind, t0, T, tok0 = tiles[ti]
                rd = []
                if l > 0:
                    rd.append(XRES[si] if kind == "lat" else HCTX[si])
                sp.dma(xts[ti % 2][:, 0:T // 128, :], x_src(l, si, kind, t0, T), reads=rd, writes=[XTS[ti % 2]])

            load_x(0)
            conv_stage(tiles[0])
            front_pre(0)
            front_stage(0)
            cur_stream = None
            for ti, (kind, t0, T, tok0) in enumerate(tiles):
                nb = T // 128
                s = si if kind == "lat" else NB
                if cur_stream != s:
                    build_gbc(s)
                    cur_stream = s
                xt, XT = xts[ti % 2], XTS[ti % 2]
                hT, HT = hTs[0], HTS[0]
                h2T, H2T = hTs[1], HTS[1]
                if ti + 1 < len(tiles):
                    load_x(ti + 1)
                for dp in range(4):
                    wco_u = ws.get()
                    wao_u = ws.get()
                    wgc_u = ws.get()
                    wga_u = ws.get()
                    for d2 in range(2):
                        dc = dp * 2 + d2
                        byc, bya, bgc, bga = ps_next(), ps_next(), ps_next(), ps_next()
                        cc = d2 * 128
                        for k in range(4):
                            f = lambda e, k=k, cc=cc, wv=wco_u[0]: e.matmul(psf(byc)[:, 0:T], lhsT=wv[:, k, cc:cc + 128],
                                                                            rhs=sconv[:, k, 0:T], start=(k == 0), stop=(k == 3))
                            (pe.op if k == 3 else pe.op_noinc)(f, reads=[wco_u[1], SCONV], writes=[PB[byc]])
                        for (bb, wu, rhs_t, RB) in ((bya, wao_u, attn_t, ATT), (bgc, wgc_u, hT, HT), (bga, wga_u, hT, HT)):
                            for k in range(8):
                                f = lambda e, k=k, bb=bb, wv=wu[0], rhs_t=rhs_t, d2=d2: e.matmul(
                                    psf(bb)[:, 0:T], lhsT=wv[:, k, d2 * 128:(d2 + 1) * 128], rhs=rhs_t[:, k, 0:T],
                                    start=(k == 0), stop=(k == 7))
                                (pe.op if k == 7 else pe.op_noinc)(f, reads=[wu[1], RB], writes=[PB[bb]])
                        act.op(lambda e, dc=dc, bgc=bgc: e.activation(out=gs[0][:, 0:T], in_=psf(bgc)[:, 0:T], func=AF.Sigmoid,
                                                                      bias=PT1[:, 64 + dc:65 + dc], scale=1.0),
                               reads=[PB[bgc], PT1B], writes=[GS[0]])
                        act.op(lambda e, dc=dc, bga=bga: e.activation(out=gs[1][:, 0:T], in_=psf(bga)[:, 0:T], func=AF.Sigmoid,
                                                                      bias=PT1[:, 72 + dc:73 + dc], scale=1.0),
                               reads=[PB[bga], PT1B], writes=[GS[1]])
                        dve.op(lambda e, byc=byc: e.tensor_tensor(out=tt[0][:, 0:T], in0=psf(byc)[:, 0:T], in1=gs[0][:, 0:T], op=ALU.mult),
                               reads=[PB[byc], GS[0]], writes=[TTB[0]])
                        dve.op(lambda e, bya=bya: e.tensor_tensor(out=tt[1][:, 0:T], in0=psf(bya)[:, 0:T], in1=gs[1][:, 0:T], op=ALU.mult),
                               reads=[PB[bya], GS[1]], writes=[TTB[1]])
                        dve.op(lambda e, dc=dc: e.tensor_tensor(out=merged[:, dc, 0:T], in0=tt[0][:, 0:T], in1=tt[1][:, 0:T], op=ALU.add),
                               reads=[TTB[0], TTB[1]], writes=[MERGED])
                for hf in range(2):
                    wu = [ws.get() for _ in range(2)]
                    for n in range(nb):
                        b = ps_next()
                        for k in range(8):
                            wv, wb = wu[k // 4]
                            f = lambda e, b=b, k=k, n=n, wv=wv: e.matmul(psf(b), lhsT=merged[:, k, n * 128:(n + 1) * 128],
                                                                         rhs=wv[:, k % 4, :], start=(k == 0), stop=(k == 7))
                            (pe.op if k == 7 else pe.op_noinc)(f, reads=[wb, MERGED], writes=[PB[b]])
                        ri = n % 2
                        dve.op(lambda e, b=b, hf=hf, ri=ri: e.tensor_tensor(out=rtmp[ri], in0=psf(b), in1=gbc[0][:, hf * 512:(hf + 1) * 512],
                                                                            op=ALU.mult), reads=[PB[b], GBC[0]], writes=[RTMP[ri]])
                        dve.op(lambda e, n=n, hf=hf, ri=ri: e.tensor_tensor(out=xt[:, n, hf * 512:(hf + 1) * 512],
                                                                            in0=xt[:, n, hf * 512:(hf + 1) * 512], in1=rtmp[ri], op=ALU.add),
                               reads=[XT, RTMP[ri]], writes=[XT])
                norm_to_hT(xt, XT, nb, s, 2, h2T, H2T, tmp)
                if ti + 1 < len(tiles):
                    conv_stage(tiles[ti + 1])
                for fu in range(16):
                    wv, wb = ws.get()
                    for fc in range(2):
                        fi = fu * 2 + fc
                        b = ps_next()
                        for k in range(8):
                            f = lambda e, b=b, k=k, fc=fc, wv=wv: e.matmul(psf(b)[:, 0:T], lhsT=wv[:, k, fc * 128:(fc + 1) * 128],
                                                                           rhs=h2T[:, k, 0:T], start=(k == 0), stop=(k == 7))
                            (pe.op if k == 7 else pe.op_noinc)(f, reads=[wb, H2T], writes=[PB[b]])
                        ri = fi % 2
                        act.op(lambda e, b=b, ri=ri: e.activation(out=rl[ri][:, 0:T], in_=psf(b)[:, 0:T], func=AF.Relu),
                               reads=[PB[b]], writes=[RL[ri]])
                        act.op(lambda e, fi=fi, ri=ri: e.activation(out=aT[:, fi, 0:T], in_=rl[ri][:, 0:T], func=AF.Square),
                               reads=[RL[ri]], writes=[AT])
                dst_base, DSTB = (out_d, XRES[si]) if kind == "lat" else (hctx_d, HCTX[si])
                if ti + 1 < len(tiles):
                    front_pre(ti + 1)
                for hf in range(2):
                    banks = [ps_next() for _ in range(nb)]
                    for f4 in range(8):
                        wv, wb = ws.get()
                        for n in range(nb):
                            for f_ in range(4):
                                fi = f4 * 4 + f_
                                f = lambda e, n=n, fi=fi, f_=f_, wv=wv: e.matmul(psf(banks[n]), lhsT=aT[:, fi, n * 128:(n + 1) * 128],
                                                                                 rhs=wv[:, f_, :], start=(fi == 0), stop=(fi == 31))
                                (pe.op if (fi == 31 or f_ == 3) else pe.op_noinc)(f, reads=[wb, AT], writes=[PB[banks[n]]])
                    for n in range(nb):
                        ri = n % 2
                        dve.op(lambda e, n=n, hf=hf, ri=ri: e.tensor_tensor(out=rtmp[ri], in0=psf(banks[n]),
                                                                            in1=gbc[1][:, hf * 512:(hf + 1) * 512], op=ALU.mult),
                               reads=[PB[banks[n]], GBC[1]], writes=[RTMP[ri]])
                        dve.op(lambda e, n=n, hf=hf, ri=ri: e.tensor_tensor(out=o_st[ri], in0=xt[:, n, hf * 512:(hf + 1) * 512],
                                                                            in1=rtmp[ri], op=ALU.add),
                               reads=[XT, RTMP[ri]], writes=[OST[ri]])
                        sp.dma(dst_base[si, t0 + n * 128:t0 + (n + 1) * 128, hf * 512:(hf + 1) * 512], o_st[ri],
                               reads=[OST[ri]], writes=[DSTB])
                    if hf == 0 and ti + 1 < len(tiles):
                        front_stage(ti + 1)
            ws.done()
            fw.barrier()
            ar.reset(m)

        for l in range(DEPTH):
            last = (l == DEPTH - 1)
            layer_setup(l)
            for si in range(NB):
                pass1(l, si)
                pass2(l, si, last)
                pass3(l, si, last)
        fw.finish()
    return nc


def rope_tables(S):
    f = np.float32
    rows = S // 64
    row = np.repeat(np.arange(rows, dtype=f), 64)
    col = np.tile(np.arange(64, dtype=f), rows)
    inv = np.power(f(10000.0), -np.arange(16, dtype=f) / f(16)).astype(f)
    ang = np.concatenate([row[:, None] * inv, col[:, None] * inv], axis=-1).astype(f)
    return np.cos(ang).astype(f), np.sin(ang).astype(f)


def make_in_maps(inputs, n_cores, NB):
    S = inputs["x"].shape[1]
    cos, sin = rope_tables(S)
    ident = np.eye(128, dtype=np.float32)
    maps = []
    for ci in range(n_cores):
        sl = slice(ci * NB, (ci + 1) * NB)
        m = {}
        for k, v in inputs.items():
            v = np.asarray(v)
            if k in ("x", "c", "ctx"):
                m[k] = np.ascontiguousarray(v[sl])
            elif k in ("lam_q", "lam_k"):
                m[k] = np.ascontiguousarray(v.reshape(v.shape[0], 128))
            else:
                m[k] = np.ascontiguousarray(v)
        m["ident"] = ident
        m["rope_cos"] = cos
        m["rope_sin"] = sin
        maps.append(m)
    return maps


def kernel(**inputs):
    x = np.asarray(inputs["x"])
    B, S, _ = x.shape
    C = np.asarray(inputs["ctx"]).shape[1]
    DEPTH = np.asarray(inputs["w_in"]).shape[0]
    n_cores = N_CORES if B % N_CORES == 0 else 1
    NB = B // n_cores
    nc = build_program(S, C, NB, DEPTH)
    maps = make_in_maps(inputs, n_cores, NB)
    res = run_bass_kernel_spmd(nc, maps, core_ids=list(range(n_cores)))
    out = np.concatenate([np.asarray(r["out"]) for r in res.results], axis=0)
    return out.astype(np.float32)
```

```python
import math
from contextlib import ExitStack

import numpy as np
import ml_dtypes
import concourse.bass as bass
import concourse.mybir as mybir
from concourse.bass_utils import run_bass_kernel_spmd

F32 = mybir.dt.float32
BF16 = mybir.dt.bfloat16
ALU = mybir.AluOpType
AF = mybir.ActivationFunctionType
AX = mybir.AxisListType

D = 1024
KD = 8
NH = 8
IN_COLS = 6144
DFF = 4096
CW = 512
CK = 31
Q_OFF, K_OFF, V_OFF, GATE_OFF = 1024, 2048, 3072, 4096
EPS = 1e-6
N_CORES = 8


class Buf:
    def __init__(self, name, acc=False):
        self.name = name
        self.w = {}
        self.r = {}
        self.acc = acc


def _merge(dst, src):
    for k, (s, v) in src.items():
        if k not in dst or dst[k][1] < v:
            dst[k] = (s, v)


class _Rec:
    def __init__(self):
        self.call = None

    def __getattr__(self, name):
        def _f(*a, **k):
            assert self.call is None
            self.call = (name, a, k)
            return self
        return _f


def _record(fn):
    r = _Rec()
    fn(r)
    assert r.call is not None
    return r.call


class Queue:
    def __init__(self, name, sem, is_pe=False):
        self.name, self.sem, self.is_pe = name, sem, is_pe
        self.cnt = 0
        self.seen = {}
        self.ops = []
        self.dma_sems = []
        self.dma_cnt = []
        self.dma_rr = 0
        self.pending_noinc = False

    def _wait(self, tok):
        for sid, (sem, v) in tok.items():
            if self.is_pe and sem is self.sem:
                continue
            if self.seen.get(sid, 0) < v:
                self.seen[sid] = v
                self.ops.append(lambda e, sem=sem, v=v: e.wait_ge(sem, v))

    def _deps(self, reads, writes):
        for b in reads:
            self._wait(b.w)
        for b in writes:
            if not b.acc:
                self._wait(b.w)
            self._wait(b.r)

    def _commit(self, tok, reads, writes):
        for b in reads:
            _merge(b.r, tok)
        for b in writes:
            if b.acc:
                _merge(b.w, tok)
            else:
                b.w = dict(tok)
                b.r = {}

    def op(self, fn, reads=(), writes=()):
        self._deps(reads, writes)
        self.cnt += 1
        sem = self.sem
        name, a, k = _record(fn)
        self.ops.append(lambda e, name=name, a=a, k=k, sem=sem: getattr(e, name)(*a, **k).then_inc(sem, 1))
        tok = {id(sem): (sem, self.cnt)}
        self._commit(tok, reads, writes)
        self.pending_noinc = False
        return tok

    def op_noinc(self, fn, reads=(), writes=()):
        assert self.is_pe
        self._deps(reads, writes)
        name, a, k = _record(fn)
        self.ops.append(lambda e, name=name, a=a, k=k: getattr(e, name)(*a, **k))
        tok = {id(self.sem): (self.sem, self.cnt + 1)}
        self._commit(tok, reads, writes)
        self.pending_noinc = True
        return tok

    def dma(self, out, in_, reads=(), writes=()):
        self._deps(reads, writes)
        i = self.dma_rr
        self.dma_rr = (self.dma_rr + 1) % len(self.dma_sems)
        sem = self.dma_sems[i]
        prev = self.dma_cnt[i]
        if prev:
            self._wait({id(sem): (sem, prev)})
        self.dma_cnt[i] = prev + 16
        self.ops.append(lambda e, out=out, in_=in_, sem=sem:
                        e.dma_start(out=out, in_=in_).then_inc(sem, 16))
        tok = {id(sem): (sem, prev + 16)}
        self._commit(tok, reads, writes)
        return tok

    def all_tokens(self):
        t = {}
        if self.cnt:
            t[id(self.sem)] = (self.sem, self.cnt)
        for s, c in zip(self.dma_sems, self.dma_cnt):
            if c:
                t[id(s)] = (s, c)
        return t


class FW:
    def __init__(self, nc, stack, n_sync_dma=16, n_pool_dma=6):
        self.nc = nc
        mk = lambda n: stack.enter_context(nc.semaphore(n))
        self.pe = Queue("pe", mk("s_pe"), is_pe=True)
        self.act = Queue("act", mk("s_act"))
        self.dve = Queue("dve", mk("s_dve"))
        self.pool = Queue("pool", mk("s_pool"))
        self.sp = Queue("sp", mk("s_sp"))
        self.queues = [self.pe, self.act, self.dve, self.pool, self.sp]
        for q, n in ((self.sp, n_sync_dma), (self.pool, n_pool_dma)):
            q.dma_sems = [mk(f"d_{q.name}{i}") for i in range(n)]
            q.dma_cnt = [0] * n

    def barrier(self):
        assert not self.pe.pending_noinc
        tok = {}
        for q in self.queues:
            _merge(tok, q.all_tokens())
        for q in self.queues:
            q._wait(tok)

    def finish(self):
        self.barrier()
        nc = self.nc
        with nc.Block() as block:
            @block.tensor
            def _(e):
                for f in self.pe.ops:
                    f(e)

            @block.scalar
            def _(e):
                for f in self.act.ops:
                    f(e)

            @block.vector
            def _(e):
                for f in self.dve.ops:
                    f(e)

            @block.gpsimd
            def _(e):
                for f in self.pool.ops:
                    f(e)

            @block.sync
            def _(e):
                for f in self.sp.ops:
                    f(e)


class Arena:
    def __init__(self, U, nbytes):
        self.U, self.nbytes, self.off = U, nbytes, 0

    def mark(self):
        return self.off

    def reset(self, m):
        self.off = m

    def alloc(self, shape, dtype):
        n = 1
        for s in shape[1:]:
            n *= s
        esz = 4 if dtype == F32 else 2
        nb = (n * esz + 63) // 64 * 64
        assert self.off + nb <= self.nbytes, f"SBUF arena overflow {self.off}+{nb}>{self.nbytes}"
        ap = self.U[0:shape[0], self.off // 2:(self.off + n * esz) // 2]
        self.off += nb
        if dtype == F32:
            ap = ap.bitcast(F32)
        if len(shape) == 3:
            ap = ap.rearrange("p (a b) -> p a b", a=shape[1])
        elif len(shape) == 4:
            ap = ap.rearrange("p (a b c) -> p a b c", a=shape[1], b=shape[2])
        return ap


def build_program(S, C, NB, DEPTH):
    TT = S + C
    NKB = TT // 128
    NLT = S // 512
    assert S % 512 == 0 and C % 128 == 0 and C <= 512
    nc = bass.Bass("TRN2", target_bir_lowering=False)

    def din(name, shape, dt=F32):
        return nc.dram_tensor(name, list(shape), dt, kind="ExternalInput").ap()

    def dint(name, shape, dt):
        return nc.dram_tensor(name, list(shape), dt, kind="Internal").ap()

    x_d = din("x", [NB, S, D])
    c_d = din("c", [NB, D])
    ctx_d = din("ctx", [NB, C, D])
    cctx_d = din("c_ctx", [D])
    w_ada_d = din("w_ada", [DEPTH, D, 6 * D])
    b_ada_d = din("b_ada", [DEPTH, 6 * D])
    n1g_d = din("norm1_g", [DEPTH, D])
    w_in_d = din("w_in", [DEPTH, D, IN_COLS])
    bgate_d = din("b_gate", [DEPTH, 2 * D])
    qng_d = din("q_norm_g", [DEPTH, 64])
    kng_d = din("k_norm_g", [DEPTH, 64])
    lamq_d = din("lam_q", [DEPTH, 128])
    lamk_d = din("lam_k", [DEPTH, 128])
    ang_d = din("attn_norm_g", [DEPTH, 128])
    wdw_d = din("w_dw", [DEPTH, CK, CW])
    bdw_d = din("b_dw", [DEPTH, CW])
    lng_d = din("conv_ln_g", [DEPTH, CW])
    lnb_d = din("conv_ln_b", [DEPTH, CW])
    wco_d = din("w_conv_out", [DEPTH, CW, D])
    wao_d = din("w_attn_out", [DEPTH, D, D])
    wo_d = din("w_out", [DEPTH, D, D])
    n2g_d = din("norm2_g", [DEPTH, D])
    w1_d = din("w_mlp1", [DEPTH, D, DFF])
    w2_d = din("w_mlp2", [DEPTH, DFF, D])
    ident_d = din("ident", [128, 128])
    cos_d = din("rope_cos", [S, 32])
    sin_d = din("rope_sin", [S, 32])
    out_d = nc.dram_tensor("out", [NB, S, D], F32, kind="ExternalOutput").ap()

    win_b = dint("win_b", [DEPTH, D, IN_COLS], BF16)
    wco_b = dint("wco_b", [DEPTH, CW, D], BF16)
    wao_b = dint("wao_b", [DEPTH, D, D], BF16)
    wo_b = dint("wo_b", [DEPTH, D, D], BF16)
    w1_b = dint("w1_b", [DEPTH, D, DFF], BF16)
    w2_b = dint("w2_b", [DEPTH, DFF, D], BF16)
    hctx_d = dint("hctx", [NB, C, D], F32)
    qT_d = dint("qT_s", [DEPTH, NB, NH, 128, TT], BF16)
    kT_d = dint("kT_s", [DEPTH, NB, NH, 128, TT], BF16)
    v_d = dint("v_s", [DEPTH, NB, TT, D], BF16)
    zT_d = dint("zT_s", [DEPTH, NB, 4, 128, TT], BF16)
    aT_d = dint("aT_s", [DEPTH, NB, NH, 128, TT], BF16)

    with ExitStack() as st:
        fw = FW(nc, st)
        pe, act, dve, pool, sp = fw.pe, fw.act, fw.dve, fw.pool, fw.sp
        SB_BYTES = 206 * 1024
        U = st.enter_context(nc.sbuf_tensor("U", [128, SB_BYTES // 2], BF16))
        ar = Arena(U, SB_BYTES)
        P = st.enter_context(nc.psum_tensor("P", [128, 8, 512], F32))
        PB = [Buf(f"ps{i}") for i in range(8)]
        ps_rr = [0]

        def ps_next():
            i = ps_rr[0]
            ps_rr[0] = (i + 1) % 8
            return i

        def psf(i):
            return P[:, i, :]

        def psb(i):
            return P[:, i, :].bitcast(BF16)

        ident_f = ar.alloc([128, 128], F32); IDF = Buf("identf")
        ident_b = ar.alloc([128, 128], BF16); IDB = Buf("identb")
        ones_f = ar.alloc([128, 128], F32); ONF = Buf("onesf")
        ones_b = ar.alloc([128, 128], BF16); ONB = Buf("onesb")
        epsc = ar.alloc([128, 1], F32); EPSB = Buf("eps")
        pstage = ar.alloc([128, 128], F32); PST = Buf("pstage")
        PT1 = ar.alloc([128, 92], F32); PT1B = Buf("pt1")
        PT2 = ar.alloc([128, 124], F32); PT2B = Buf("pt2")
        NS3 = NB + 1
        scT = ar.alloc([128, 8, 4], F32); SCT = Buf("scT")
        modT = ar.alloc([128, 48, NS3], F32); MODT = Buf("modT")
        AB = ar.alloc([128, NS3, 4, 8], F32); ABB = Buf("AB")
        gq_bc = ar.alloc([128, 64], F32); GQ = Buf("gq")
        gk_bc = ar.alloc([128, 64], F32); GK = Buf("gk")
        lamw = ar.alloc([128, 4, 128], F32); LAMW = Buf("lamw")
        lamc = ar.alloc([128, 8], F32); LAMC = Buf("lamc")
        Gt = [ar.alloc([128, 128], F32) for _ in range(2)]; GTB = [Buf("gt0"), Buf("gt1")]
        gbc = [ar.alloc([128, 1024], F32) for _ in range(2)]; GBC = [Buf("gbc0"), Buf("gbc1")]
        NSLOT = 8
        wslots = [(ar.alloc([128, 2048], BF16), Buf(f"wslot{i}")) for i in range(NSLOT)]
        common_mark = ar.mark()

        class WStream:
            PF = 4

            def __init__(self):
                self.units = []
                self.issued = 0
                self.nxt = 0

            def extend(self, srcs):
                self.units += srcs

            def _view(self, j):
                slot, buf = wslots[j % NSLOT]
                src = self.units[j]
                a, b = src.shape[1], src.shape[2]
                if src.dtype == F32:
                    v = slot[:, 0:2 * a * b].bitcast(F32)
                else:
                    v = slot[:, 0:a * b]
                return v.rearrange("p (a b) -> p a b", a=a), buf

            def _issue(self, n):
                while self.issued < min(n, len(self.units)):
                    j = self.issued
                    v, buf = self._view(j)
                    sp.dma(v, self.units[j], writes=[buf])
                    self.issued += 1

            def get(self):
                j = self.nxt
                self.nxt += 1
                assert j < len(self.units)
                self._issue(j + 1 + self.PF)
                return self._view(j)

            def done(self):
                assert self.nxt == len(self.units) == self.issued, (self.nxt, len(self.units), self.issued)

        ws = WStream()

        sp.dma(ident_f, ident_d, writes=[IDF])
        dve.op(lambda e: e.tensor_copy(out=ident_b, in_=ident_f), reads=[IDF], writes=[IDB])
        dve.op(lambda e: e.memset(ones_f, 1.0), writes=[ONF])
        dve.op(lambda e: e.memset(ones_b, 1.0), writes=[ONB])
        dve.op(lambda e: e.memset(epsc, EPS), writes=[EPSB])

        WCAST_IN = [Buf(f"wcast_in{l}", acc=True) for l in range(DEPTH)]
        WCAST_REST = [Buf(f"wcast_rest{l}", acc=True) for l in range(DEPTH)]
        for l in range(DEPTH):
            for (src, dst, rows) in ((w_in_d, win_b, D), (wco_d, wco_b, CW), (wao_d, wao_b, D),
                                     (wo_d, wo_b, D), (w1_d, w1_b, D), (w2_d, w2_b, DFF)):
                wbuf = WCAST_IN[l] if src is w_in_d else WCAST_REST[l]
                for r0 in range(0, rows, 128):
                    pool.dma(dst[l, r0:r0 + 128, :], src[l, r0:r0 + 128, :], writes=[wbuf])

        m0 = ar.mark()
        crow = ar.alloc([NS3, 1024], F32); CROW = Buf("crow")
        srow = ar.alloc([NS3, 1024], F32); SROW = Buf("srow")
        sp.dma(crow[0:NB, :], c_d, writes=[CROW])
        sp.dma(crow[NB:NB + 1, :], cctx_d.rearrange("(o d) -> o d", o=1), writes=[CROW])
        act.op(lambda e: e.activation(out=srow, in_=crow, func=AF.Silu), reads=[CROW], writes=[SROW])
        bi = ps_next()
        for k in range(8):
            pe.op(lambda e, k=k: e.transpose(out=psf(bi)[:, k * 4:k * 4 + NS3], in_=srow[0:NS3, k * 128:(k + 1) * 128],
                                             identity=ident_f[0:NS3, 0:NS3]),
                  reads=[SROW, IDF], writes=[PB[bi]])
        dve.op(lambda e: e.tensor_copy(out=scT[:, :, 0:NS3],
                                       in_=psf(bi)[:, 0:32].rearrange("p (k s) -> p k s", s=4)[:, :, 0:NS3]),
               reads=[PB[bi]], writes=[SCT])
        fw.barrier()
        ar.reset(m0)

        LN_SCALE = 1.0 / CW

        def layer_setup(l):
            lam_init = 0.8 - 0.6 * math.exp(-0.3 * l)
            sp.dma(pstage[0:48, :], b_ada_d[l].rearrange("(r p) -> r p", p=128), writes=[PST])
            sp.dma(pstage[48:56, :], n1g_d[l].rearrange("(r p) -> r p", p=128), writes=[PST])
            sp.dma(pstage[56:64, :], n2g_d[l].rearrange("(r p) -> r p", p=128), writes=[PST])
            sp.dma(pstage[64:80, :], bgate_d[l].rearrange("(r p) -> r p", p=128), writes=[PST])
            sp.dma(pstage[80:84, :], lng_d[l].rearrange("(r p) -> r p", p=128), writes=[PST])
            sp.dma(pstage[84:88, :], lnb_d[l].rearrange("(r p) -> r p", p=128), writes=[PST])
            sp.dma(pstage[88:92, :], bdw_d[l].rearrange("(r p) -> r p", p=128), writes=[PST])
            b1 = ps_next()
            pe.op(lambda e: e.transpose(out=psf(b1)[:, 0:92], in_=pstage[0:92, :], identity=ident_f[0:92, 0:92]),
                  reads=[PST, IDF], writes=[PB[b1]])
            dve.op(lambda e: e.tensor_copy(out=PT1, in_=psf(b1)[:, 0:92]), reads=[PB[b1]], writes=[PT1B])
            sp.dma(pstage[0:124, :], wdw_d[l].rearrange("t (c p) -> (t c) p", p=128), reads=[], writes=[PST])
            b2 = ps_next()
            pe.op(lambda e: e.transpose(out=psf(b2)[:, 0:124], in_=pstage[0:124, :], identity=ident_f[0:124, 0:124]),
                  reads=[PST, IDF], writes=[PB[b2]])
            dve.op(lambda e: e.tensor_copy(out=PT2, in_=psf(b2)[:, 0:124]), reads=[PB[b2]], writes=[PT2B])
            sp.dma(gq_bc, qng_d[l].partition_broadcast(128), writes=[GQ])
            sp.dma(gk_bc, kng_d[l].partition_broadcast(128), writes=[GK])
            sp.dma(lamw[:, 0, :], lamq_d[l].partition_broadcast(128), writes=[LAMW])
            sp.dma(lamw[:, 1, :], lamk_d[l].partition_broadcast(128), writes=[LAMW])
            sp.dma(lamc[:, 3:4], ang_d[l].rearrange("(p o) -> p o", o=1), writes=[LAMC])
            dve.op(lambda e: e.tensor_tensor(out=lamw[:, 2, :], in0=lamw[:, 0, :], in1=lamw[:, 1, :], op=ALU.mult),
                   reads=[LAMW], writes=[LAMW])
            dve.op(lambda e: e.tensor_reduce(out=lamc[:, 0:2], in_=lamw[:, 2, :].rearrange("p (a b) -> p a b", a=2),
                                             axis=AX.X, op=ALU.add), reads=[LAMW, LAMC], writes=[LAMC])
            act.op(lambda e: e.activation(out=lamc[:, 4:6], in_=lamc[:, 0:2], func=AF.Exp), reads=[LAMC], writes=[LAMC])
            dve.op(lambda e: e.tensor_tensor(out=lamc[:, 6:7], in0=lamc[:, 4:5], in1=lamc[:, 5:6], op=ALU.subtract),
                   reads=[LAMC], writes=[LAMC])
            dve.op(lambda e: e.tensor_scalar(out=lamc[:, 2:3], in0=lamc[:, 6:7], scalar1=float(lam_init), scalar2=-1.0,
                                             op0=ALU.add, op1=ALU.mult), reads=[LAMC], writes=[LAMC])
            dve.op(lambda e: e.tensor_scalar(out=lamc[:, 7:8], in0=lamc[:, 3:4], scalar1=float(1.0 - lam_init),
                                             scalar2=None, op0=ALU.mult), reads=[LAMC], writes=[LAMC])
            ws.extend([w_ada_d[l][:, j * 128:(j + 1) * 128].rearrange("(k p) c -> p k c", p=128) for j in range(48)])
            bm = ps_next()
            mps = psf(bm)[:, 0:192].rearrange("p (j s) -> p j s", s=4)
            for j in range(48):
                wv, wb = ws.get()
                for k in range(8):
                    f = (lambda e, j=j, k=k, wv=wv: e.matmul(mps[:, j, 0:NS3], lhsT=wv[:, k, :], rhs=scT[:, k, 0:NS3],
                                                              start=(k == 0), stop=(k == 7)))
                    (pe.op if k == 7 else pe.op_noinc)(f, reads=[wb, SCT], writes=[PB[bm]])
            ws.done()
            dve.op(lambda e: e.tensor_tensor(out=modT, in0=mps[:, :, 0:NS3],
                                             in1=PT1[:, 0:48].unsqueeze(2).to_broadcast([128, 48, NS3]), op=ALU.add),
                   reads=[PB[bm], PT1B], writes=[MODT])
            for s in range(NS3):
                dve.op(lambda e, s=s: e.scalar_tensor_tensor(out=AB[:, s, 0, :], in0=modT[:, 8:16, s], scalar=1.0,
                                                             in1=PT1[:, 48:56], op0=ALU.add, op1=ALU.mult),
                       reads=[MODT, PT1B], writes=[ABB])
                dve.op(lambda e, s=s: e.tensor_copy(out=AB[:, s, 1, :], in_=modT[:, 0:8, s]), reads=[MODT], writes=[ABB])
                dve.op(lambda e, s=s: e.scalar_tensor_tensor(out=AB[:, s, 2, :], in0=modT[:, 32:40, s], scalar=1.0,
                                                             in1=PT1[:, 56:64], op0=ALU.add, op1=ALU.mult),
                       reads=[MODT, PT1B], writes=[ABB])
                dve.op(lambda e, s=s: e.tensor_copy(out=AB[:, s, 3, :], in_=modT[:, 24:32, s]), reads=[MODT], writes=[ABB])
            fw.barrier()

        def build_gbc(s):
            for gi, j0 in ((0, 16), (1, 40)):
                for half in range(2):
                    b = ps_next()
                    for kk in range(4):
                        k = half * 4 + kk
                        g = Gt[k % 2]
                        dve.op(lambda e, g=g, k=k: e.tensor_scalar(out=g, in0=ones_f, scalar1=modT[:, j0 + k, s:s + 1],
                                                                   scalar2=None, op0=ALU.mult),
                               reads=[ONF, MODT], writes=[GTB[k % 2]])
                        pe.op(lambda e, g=g, kk=kk, b=b: e.matmul(psf(b)[:, kk * 128:(kk + 1) * 128], lhsT=g, rhs=ident_f,
                                                                  start=True, stop=True),
                              reads=[GTB[k % 2], IDF], writes=[PB[b]])
                    dve.op(lambda e, b=b, gi=gi, half=half: e.tensor_copy(out=gbc[gi][:, half * 512:(half + 1) * 512],
                                                                          in_=psf(b)),
                           reads=[PB[b]], writes=[GBC[gi]])

        def norm_stats(xt, XT, nb, tmp):
            ss, sd, rstd, junk, xh, SSB, JB, XHB = tmp
            dve.op(lambda e: e.memset(ss, 0.0), reads=[], writes=[SSB])
            for n in range(nb):
                act.op(lambda e, n=n: e.activation(out=junk, in_=xt[:, n, :], func=AF.Square, accum_out=ss[:, n:n + 1]),
                       reads=[XT, SSB], writes=[JB, SSB])
            act.op(lambda e: e.activation(out=sd[:, 0:nb], in_=ss[:, 0:nb], func=AF.Sqrt, bias=epsc[:, 0:1], scale=1.0 / D),
                   reads=[SSB, EPSB], writes=[SSB])
            dve.op(lambda e: e.reciprocal(out=rstd[:, 0:nb], in_=sd[:, 0:nb]), reads=[SSB], writes=[SSB])

        def norm_T(xt, XT, nb, s, which, hT, HT, tmp):
            ss, sd, rstd, junk, xh, SSB, JB, XHB = tmp
            ia, ib = (0, 1) if which == 1 else (2, 3)
            xhs = [(xh, XHB), (junk, JB)]
            base = ((ps_rr[0] + 3) // 4 * 4) % 8
            ps_rr[0] = (base + 4) % 8
            for n in range(nb):
                xb, XB = xhs[n % 2]
                dve.op(lambda e, n=n, xb=xb: e.tensor_scalar(out=xb, in0=xt[:, n, :], scalar1=rstd[:, n:n + 1], scalar2=None,
                                                             op0=ALU.mult), reads=[XT, SSB], writes=[XB])
                b = base + n
                pv = psb(b).rearrange("p (k t) -> p k t", k=8)
                for k in range(8):
                    f = lambda e, k=k, pv=pv, xb=xb: e.transpose(out=pv[:, k, :], in_=xb[:, k * 128:(k + 1) * 128], identity=ident_b)
                    (pe.op if k == 7 else pe.op_noinc)(f, reads=[XB, IDB], writes=[PB[b]])
            pq = P[:, base:base + nb, :].bitcast(BF16).rearrange("p n (k t) -> p n k t", k=8)
            for k in range(8):
                act.op(lambda e, k=k: e.activation(out=hT[:, k, 0:nb * 128].rearrange("p (n t) -> p n t", n=nb), in_=pq[:, :, k, :],
                                                   func=AF.Identity, scale=AB[:, s, ia, k:k + 1], bias=AB[:, s, ib, k:k + 1]),
                       reads=PB[base:base + nb] + [ABB], writes=[HT])

        def norm_to_hT(xt, XT, nb, s, which, hT, HT, tmp):
            norm_stats(xt, XT, nb, tmp)
            norm_T(xt, XT, nb, s, which, hT, HT, tmp)

        def stream_tiles(si):
            tiles = [("lat", i * 512, 512, i * 512) for i in range(NLT)]
            for c0 in range(0, C, 512):
                tiles.append(("ctx", c0, min(512, C - c0), S + c0))
            return tiles

        def x_src(l, si, kind, t0, T):
            if kind == "lat":
                base = x_d if l == 0 else out_d
            else:
                base = ctx_d if l == 0 else hctx_d
            return base[si, t0:t0 + T, :].rearrange("(n p) d -> p n d", p=128)

        XRES = [Buf(f"xres{si}", acc=True) for si in range(NB)]
        HCTX = [Buf(f"hctx{si}", acc=True) for si in range(NB)]

        def pass1(l, si):
            m = ar.mark()
            xt1 = ar.alloc([128, 4, 1024], F32); XT1 = Buf("xt")
            hTs = [ar.alloc([128, 8, 512], BF16) for _ in range(2)]; HTS = [Buf("hT0"), Buf("hT1")]
            ssa = ar.alloc([128, 12], F32)
            tmp = (ssa[:, 0:4], ssa[:, 4:8], ssa[:, 8:12], ar.alloc([128, 1024], BF16), ar.alloc([128, 1024], BF16),
                   Buf("ss"), Buf("junk"), Buf("xh"))
            NBLK = S // 128
            tabs = [[ar.alloc([128, NBLK, 32], F32) for _ in range(4)] for _ in range(2)]; TABS = Buf("tabs")
            sig = [ar.alloc([128, 512], F32) for _ in range(2)]; SIG = [Buf("sig0"), Buf("sig1")]
            z_st = ar.alloc([128, 4, 512], BF16); ZST = Buf("zst")
            sq = [ar.alloc([128, 4, 512], BF16) for _ in range(2)]; SQ = [Buf("sq0"), Buf("sq1")]
            qn = [ar.alloc([128, 4, 512], F32) for _ in range(2)]; QN = [Buf("qn0"), Buf("qn1")]
            st8 = [ar.alloc([128, 96], F32) for _ in range(2)]; ST8 = [Buf("st80"), Buf("st81")]
            rt = [ar.alloc([128, 4, 8, 32], F32) for _ in range(4)]; RTA = Buf("rta"); RTB = Buf("rtb")
            qr = [ar.alloc([128, 4, 512], BF16) for _ in range(2)]; QR = [Buf("qr0"), Buf("qr1")]
            qT_st = ar.alloc([128, 8, 512], BF16); QTS = Buf("qTst")
            kT_st = ar.alloc([128, 8, 512], BF16); KTS = Buf("kTst")
            v_st = ar.alloc([128, 4, 1024], BF16); VST = Buf("vst")
            cos_t = qn[0].rearrange("p a b -> p (a b)")[:, 0:NBLK * 32].rearrange("p (n i) -> p n i", i=32)
            sin_t = qn[1].rearrange("p a b -> p (a b)")[:, 0:NBLK * 32].rearrange("p (n i) -> p n i", i=32)
            sp.dma(cos_t, cos_d.rearrange("(n p) i -> p n i", p=128), writes=[QN[0]])
            sp.dma(sin_t, sin_d.rearrange("(n p) i -> p n i", p=128), writes=[QN[1]])
            for ty, (gb, GB) in enumerate(((gq_bc, GQ), (gk_bc, GK))):
                g1 = gb[:, 0:32].unsqueeze(1).to_broadcast([128, NBLK, 32])
                g2 = gb[:, 32:64].unsqueeze(1).to_broadcast([128, NBLK, 32])
                for ti_, (src, SB_, gg) in enumerate(((cos_t, QN[0], g1), (sin_t, QN[1], g2), (cos_t, QN[0], g2), (sin_t, QN[1], g1))):
                    dve.op(lambda e, ty=ty, ti_=ti_, src=src, gg=gg: e.tensor_tensor(out=tabs[ty][ti_], in0=src, in1=gg, op=ALU.mult),
                           reads=[SB_, GB], writes=[TABS])
            tiles = stream_tiles(si)
            units = []
            for _ in tiles:
                units += [win_b[l][:, u * 256:(u + 1) * 256].rearrange("(k p) c -> p k c", p=128) for u in range(4)]
                for g in range(6):
                    c0 = Q_OFF + g * 512
                    units += [win_b[l][kh * 512:(kh + 1) * 512, c0:c0 + 512].rearrange("(k p) c -> p k c", p=128)
                              for kh in range(2)]
            sp._wait(WCAST_IN[l].w)
            ws.extend(units)
            src_buf = lambda kind: (XRES[si] if kind == "lat" else HCTX[si])
            uc = 0
            quad_rr = 0
            pend = []

            def load_x1(tj):
                kind_, t0_, T_, tok0_ = tiles[tj]
                rd = []
                if l > 0:
                    rd.append(src_buf(kind_))
                sp.dma(xt1[:, 0:T_ // 128, :], x_src(l, si, kind_, t0_, T_), reads=rd, writes=[XT1])

            def stats1(tj):
                norm_stats(xt1, XT1, tiles[tj][2] // 128, tmp)

            def normT1(tj):
                kind_, t0_, T_, tok0_ = tiles[tj]
                norm_T(xt1, XT1, T_ // 128, (si if kind_ == "lat" else NB), 1, hTs[tj % 2], HTS[tj % 2], tmp)

            def load_norm(tj):
                load_x1(tj); stats1(tj); normT1(tj)

            for ti, (kind, t0, T, tok0) in enumerate(tiles):
                nb = T // 128
                s = si if kind == "lat" else NB
                xt, XT = xt1, XT1
                hT, HT = hTs[ti % 2], HTS[ti % 2]
                if ti == 0:
                    load_norm(0)
                if ti + 1 < len(tiles):
                    load_x1(ti + 1)
                cu = [ws.get() for _ in range(4)]
                for j in range(4):
                    ba, bg = ps_next(), ps_next()
                    for (bb, u) in ((ba, cu[j // 2]), (bg, cu[2 + j // 2])):
                        wv, wb = u
                        for k in range(8):
                            f = lambda e, bb=bb, wv=wv, k=k, j=j: e.matmul(
                                psf(bb)[:, 0:T], lhsT=wv[:, k, (j % 2) * 128:(j % 2) * 128 + 128], rhs=hT[:, k, 0:T],
                                start=(k == 0), stop=(k == 7))
                            (pe.op if k == 7 else pe.op_noinc)(f, reads=[wb, HT], writes=[PB[bb]])
                    sg, SG = sig[j % 2], SIG[j % 2]
                    act.op(lambda e, bg=bg, sg=sg: e.activation(out=sg[:, 0:T], in_=psf(bg)[:, 0:T], func=AF.Sigmoid),
                           reads=[PB[bg]], writes=[SG])
                    dve.op(lambda e, ba=ba, sg=sg, j=j: e.tensor_tensor(out=z_st[:, j, 0:T], in0=psf(ba)[:, 0:T],
                                                                        in1=sg[:, 0:T], op=ALU.mult),
                           reads=[PB[ba], SG], writes=[ZST])
                sp.dma(zT_d[l, si][:, :, tok0:tok0 + T].rearrange("j p t -> p j t"), z_st[:, :, 0:T], reads=[ZST], writes=[])
                for g in range(6):
                    wu = [ws.get() for _ in range(2)]
                    typ = g // 2
                    q0b = (quad_rr % 2) * 4
                    quad_rr += 1
                    QB = PB[q0b:q0b + nb]
                    for n in range(nb):
                        b = q0b + n
                        for k in range(8):
                            wv, wb = wu[k // 4]
                            f = lambda e, b=b, wv=wv, k=k, n=n: e.matmul(
                                psf(b), lhsT=hT[:, k, n * 128:(n + 1) * 128], rhs=wv[:, k % 4, :],
                                start=(k == 0), stop=(k == 7))
                            (pe.op if k == 7 else pe.op_noinc)(f, reads=[wb, HT], writes=[PB[b]])
                    pq = P[:, q0b:q0b + nb, :]
                    if len(pend) == 2 or (typ == 2 and pend):
                        pend.pop(0)(4 - q0b)
                    if typ == 2:
                        act.op(lambda e, pq=pq, g=g: e.activation(out=v_st[:, 0:nb, (g - 4) * 512:(g - 3) * 512], in_=pq, func=AF.Copy),
                               reads=QB, writes=[VST])
                        if g == 4 and ti + 1 < len(tiles):
                            normT1(ti + 1)
                        if g == 5:
                            sp.dma(v_d[l, si][tok0:tok0 + T, :].rearrange("(n p) c -> p n c", p=128), v_st[:, 0:nb, :],
                                   reads=[VST], writes=[])
                        continue
                    u = uc % 2
                    uc += 1
                    ty = typ
                    act.op(lambda e, pq=pq, u=u: e.activation(out=sq[u][:, 0:nb, :], in_=pq, func=AF.Square),
                           reads=QB, writes=[SQ[u]])
                    dve.op(lambda e, u=u: e.tensor_reduce(out=st8[u][:, 0:nb * 8],
                                                          in_=sq[u][:, 0:nb, :].rearrange("p n (a b) -> p (n a) b", a=8),
                                                          axis=AX.X, op=ALU.add), reads=[SQ[u]], writes=[ST8[u]])
                    act.op(lambda e, u=u: e.activation(out=st8[u][:, 32:32 + nb * 8], in_=st8[u][:, 0:nb * 8], func=AF.Sqrt,
                                                       bias=epsc[:, 0:1], scale=1.0 / 64), reads=[ST8[u], EPSB], writes=[ST8[u]])
                    dve.op(lambda e, u=u: e.reciprocal(out=st8[u][:, 64:64 + nb * 8], in_=st8[u][:, 32:32 + nb * 8]),
                           reads=[ST8[u]], writes=[ST8[u]])
                    dve.op(lambda e, u=u, pq=pq: e.tensor_tensor(
                        out=qn[u][:, 0:nb, :].rearrange("p n (a b) -> p (n a) b", a=8),
                        in0=pq.rearrange("p n (a b) -> p (n a) b", a=8),
                        in1=st8[u][:, 64:64 + nb * 8].unsqueeze(2).to_broadcast([128, nb * 8, 64]), op=ALU.mult),
                        reads=QB + [ST8[u]], writes=[QN[u]])
                    if kind == "lat":
                        blk0 = t0 // 128
                        q5 = qn[u][:, 0:nb, :].rearrange("p n (a h i) -> p n a h i", a=8, h=2)
                        t1, t2 = q5[:, :, :, 0, :], q5[:, :, :, 1, :]
                        tb = [tabs[ty][i][:, blk0:blk0 + nb, :].unsqueeze(2).to_broadcast([128, nb, 8, 32]) for i in range(4)]
                        o5 = qr[u][:, 0:nb, :].rearrange("p n (a h i) -> p n a h i", a=8, h=2)
                        ra, rb, rc, rd_ = [r[:, 0:nb, :, :] for r in rt]
                        dve.op(lambda e, ra=ra, t1=t1, tb=tb: e.tensor_tensor(out=ra, in0=t1, in1=tb[0], op=ALU.mult),
                               reads=[QN[u], TABS], writes=[RTA])
                        dve.op(lambda e, rb=rb, t2=t2, tb=tb: e.tensor_tensor(out=rb, in0=t2, in1=tb[1], op=ALU.mult),
                               reads=[QN[u], TABS], writes=[RTA])
                        dve.op(lambda e, o5=o5, ra=ra, rb=rb: e.tensor_tensor(out=o5[:, :, :, 0, :], in0=ra, in1=rb, op=ALU.subtract),
                               reads=[RTA], writes=[QR[u]])
                        pool.op(lambda e, rc=rc, t2=t2, tb=tb: e.tensor_tensor(out=rc, in0=t2, in1=tb[2], op=ALU.mult),
                                reads=[QN[u], TABS], writes=[RTB])
                        pool.op(lambda e, rd_=rd_, t1=t1, tb=tb: e.tensor_tensor(out=rd_, in0=t1, in1=tb[3], op=ALU.mult),
                                reads=[QN[u], TABS], writes=[RTB])
                        pool.op(lambda e, o5=o5, rc=rc, rd_=rd_: e.tensor_tensor(out=o5[:, :, :, 1, :], in0=rc, in1=rd_, op=ALU.add),
                                reads=[RTB], writes=[QR[u]])
                    else:
                        gbcast, GB = (gq_bc, GQ) if typ == 0 else (gk_bc, GK)
                        pool.op(lambda e, u=u, gbcast=gbcast: e.tensor_tensor(
                            out=qr[u][:, 0:nb, :].rearrange("p n (a b) -> p (n a) b", a=8),
                            in0=qn[u][:, 0:nb, :].rearrange("p n (a b) -> p (n a) b", a=8),
                            in1=gbcast.unsqueeze(1).to_broadcast([128, nb * 8, 64]), op=ALU.mult),
                            reads=[QN[u], GB], writes=[QR[u]])
                    def emit_T(tb, u=u, typ=typ, g=g, nb=nb, T=T, tok0=tok0):
                      dstT, DSTB = (qT_st, QTS) if typ == 0 else (kT_st, KTS)
                      h0 = (g % 2) * 4
                      for n0 in range(0, nb, 2):
                          bt = tb + n0 // 2
                          pv = psb(bt).rearrange("p (n h t) -> p n h t", n=2, h=4)
                          for nn in range(2):
                              for hh in range(4):
                                  f = lambda e, hh=hh, pv=pv, u=u, nn=nn, n0=n0: e.transpose(
                                      out=pv[:, nn, hh, :], in_=qr[u][:, n0 + nn, hh * 128:(hh + 1) * 128], identity=ident_b)
                                  (pe.op if (nn == 1 and hh == 3) else pe.op_noinc)(f, reads=[QR[u], IDB], writes=[PB[bt]])
                          act.op(lambda e, pv=pv, dstT=dstT, h0=h0, n0=n0: e.activation(
                              out=dstT[:, h0:h0 + 4, n0 * 128:(n0 + 2) * 128].rearrange("p h (n t) -> p h n t", n=2),
                              in_=pv.rearrange("p n h t -> p h n t"), func=AF.Copy),
                              reads=[PB[bt]], writes=[DSTB])
                      if g == 1:
                          sp.dma(qT_d[l, si][:, :, tok0:tok0 + T].rearrange("h p t -> p h t"), qT_st[:, :, 0:T], reads=[QTS], writes=[])
                      if g == 3:
                          sp.dma(kT_d[l, si][:, :, tok0:tok0 + T].rearrange("h p t -> p h t"), kT_st[:, :, 0:T], reads=[KTS], writes=[])

                    pend.append(emit_T)
                    if g == 1 and ti + 1 < len(tiles):
                        stats1(ti + 1)
            ws.done()
            fw.barrier()
            ar.reset(m)

        def pass2(l, si, last):
            m = ar.mark()
            kTh = [ar.alloc([128, TT], BF16) for _ in range(2)]; KTH = [Buf("kth0"), Buf("kth1")]
            qTh = [ar.alloc([128, TT], BF16) for _ in range(2)]; QTH = [Buf("qth0"), Buf("qth1")]
            vh = [ar.alloc([128, NKB, 128], BF16) for _ in range(2)]; VH = [Buf("vh0"), Buf("vh1")]
            E = [ar.alloc([128, 2, 512], BF16) for _ in range(3)]; EB = [Buf(f"E{i}") for i in range(3)]
            r0 = ar.alloc([128, 512], F32); R0 = Buf("r0")
            r1 = ar.alloc([128, 512], F32); R1 = Buf("r1")
            t0_ = ar.alloc([128, 512], F32); T0 = Buf("t0")
            t1_ = ar.alloc([128, 512], F32); T1 = Buf("t1")
            osq = ar.alloc([128, 512], F32); OSQ = Buf("osq")
            acc0 = ar.alloc([128, 512], F32); ACC0 = Buf("acc0")
            o0c = ar.alloc([128, 512], F32); O0C = Buf("o0c")
            o1c = ar.alloc([128, 512], F32); O1C = Buf("o1c")
            s1c = ar.alloc([128, 512], F32); S1C = Buf("s1c")
            o_all = ar.alloc([128, TT], F32); OALL = Buf("oall")
            ss_all = ar.alloc([128, TT], F32); SSALL = Buf("ssall")
            a_st = ar.alloc([128, TT], BF16); AST = Buf("ast")
            PS_S = [Buf("pss0"), Buf("pss1")]
            PO = [Buf("po0"), Buf("pso0"), Buf("po1"), Buf("pso1")]
            sslot = [0]
            qtiles = [(i * 512, 512, list(range(NKB))) for i in range(NLT)]
            if not last:
                qtiles.append((S, C, list(range(S // 128, NKB))))
            nq_tot = S + (0 if last else C)

            def load_head(h):
                sl = h % 2
                sp.dma(kTh[sl], kT_d[l, si, h], writes=[KTH[sl]])
                sp.dma(qTh[sl][:, 0:nq_tot], qT_d[l, si, h][:, 0:nq_tot], writes=[QTH[sl]])
                half = (NKB + 1) // 2
                for a, b in ((0, half), (half, NKB)):
                    sp.dma(vh[sl][:, a:b, :],
                           v_d[l, si][a * 128:b * 128, h * 128:(h + 1) * 128].rearrange("(kb p) c -> p kb c", p=128),
                           writes=[VH[sl]])

            load_head(0)
            for h in range(NH):
                sl = h % 2
                if h + 1 < NH:
                    load_head(h + 1)
                pending = None
                pending_a = None
                pending_s0 = None
                for (q0, N, kbs) in qtiles:
                    nk = len(kbs)

                    def emit_qk(j):
                        sb = sslot[0] % 2
                        sslot[0] += 1
                        kb = kbs[j]
                        pe.op_noinc(lambda e, sb=sb, kb=kb: e.matmul(
                            P[:, 2 * sb, 0:N], lhsT=kTh[sl][0:64, kb * 128:(kb + 1) * 128], rhs=qTh[sl][0:64, q0:q0 + N],
                            start=True, stop=True), reads=[KTH[sl], QTH[sl]], writes=[PS_S[sb]])
                        pe.op(lambda e, sb=sb, kb=kb: e.matmul(
                            P[:, 2 * sb + 1, 0:N], lhsT=kTh[sl][64:128, kb * 128:(kb + 1) * 128], rhs=qTh[sl][64:128, q0:q0 + N],
                            start=True, stop=True), reads=[KTH[sl], QTH[sl]], writes=[PS_S[sb]])
                        return sb

                    sbs = {0: emit_qk(0)}
                    if nk > 1:
                        sbs[1] = emit_qk(1)
                    for j in range(nk):
                        sb = sbs[j]
                        ei = j % 3
                        act.op(lambda e, sb=sb, ei=ei: e.activation(out=E[ei][:, :, 0:N], in_=P[:, 2 * sb:2 * sb + 2, 0:N],
                                                                    func=AF.Exp, scale=0.125),
                               reads=[PS_S[sb]], writes=[EB[ei]])
                        if j + 2 < nk:
                            sbs[j + 2] = emit_qk(j + 2)
                        if j == 0 and pending_s0 is not None:
                            pending_s0(); pending_s0 = None
                        kb = kbs[j]
                        st_, sp_ = (j == 0), (j == nk - 1)
                        if j == 0:
                            dve.op(lambda e, ei=ei: e.tensor_copy(out=acc0[:, 0:N], in_=E[ei][:, 0, 0:N]),
                                   reads=[EB[ei]], writes=[ACC0])
                        else:
                            dve.op(lambda e, ei=ei: e.tensor_tensor(out=acc0[:, 0:N], in0=acc0[:, 0:N], in1=E[ei][:, 0, 0:N], op=ALU.add),
                                   reads=[EB[ei], ACC0], writes=[ACC0])
                        for c in range(2):
                            pe.op_noinc(lambda e, c=c, ei=ei, kb=kb, st_=st_, sp_=sp_: e.matmul(
                                P[:, 4 + 2 * c, 0:N], lhsT=vh[sl][:, kb, :], rhs=E[ei][:, c, 0:N], start=st_, stop=sp_),
                                reads=[VH[sl], EB[ei]], writes=[PO[2 * c]])
                        pe.op(lambda e, ei=ei, st_=st_, sp_=sp_: e.matmul(
                            P[:, 7, 0:N], lhsT=ones_b, rhs=E[ei][:, 1, 0:N], start=st_, stop=sp_),
                            reads=[ONB, EB[ei]], writes=[PO[3]])
                        if j == min(1, nk - 1) and pending_a is not None:
                            pending_a(); pending_a = None
                        if j == min(4, nk - 1) and pending is not None:
                            pending(); pending = None
                    if pending_s0 is not None:
                        pending_s0(); pending_s0 = None
                    if pending_a is not None:
                        pending_a(); pending_a = None
                    if pending is not None:
                        pending(); pending = None
                    def part_s0(N=N):
                        pe.op(lambda e: e.matmul(P[:, 5, 0:N], lhsT=ones_f, rhs=acc0[:, 0:N], start=True, stop=True),
                              reads=[ONF, ACC0], writes=[PO[1]])
                    pending_s0 = part_s0
                    act.op(lambda e: e.activation(out=o0c[:, 0:N], in_=P[:, 4, 0:N], func=AF.Copy), reads=[PO[0]], writes=[O0C])
                    dve.op(lambda e: e.tensor_copy(out=s1c[:, 0:N], in_=P[:, 7, 0:N]), reads=[PO[3]], writes=[S1C])
                    dve.op(lambda e: e.tensor_copy(out=o1c[:, 0:N], in_=P[:, 6, 0:N]), reads=[PO[2]], writes=[O1C])

                    def part_a(q0=q0, N=N):
                        dve.op(lambda e: e.tensor_tensor(out=r0[:, 0:N], in0=P[:, 5, 0:N], in1=s1c[:, 0:N], op=ALU.mult),
                               reads=[PO[1], S1C], writes=[R0])
                        dve.op(lambda e: e.reciprocal(out=r1[:, 0:N], in_=r0[:, 0:N]), reads=[R0], writes=[R1])
                        dve.op(lambda e: e.tensor_tensor(out=t0_[:, 0:N], in0=o0c[:, 0:N], in1=s1c[:, 0:N], op=ALU.mult),
                               reads=[O0C, S1C], writes=[T0])
                        dve.op(lambda e: e.tensor_tensor(out=t1_[:, 0:N], in0=o1c[:, 0:N], in1=P[:, 5, 0:N], op=ALU.mult),
                               reads=[O1C, PO[1]], writes=[T1])
                        dve.op(lambda e: e.scalar_tensor_tensor(out=t0_[:, 0:N], in0=t1_[:, 0:N], scalar=lamc[:, 2:3],
                                                                in1=t0_[:, 0:N], op0=ALU.mult, op1=ALU.add),
                               reads=[T0, T1, LAMC], writes=[T0])
                        dve.op(lambda e: e.tensor_tensor(out=o_all[:, q0:q0 + N], in0=t0_[:, 0:N], in1=r1[:, 0:N], op=ALU.mult),
                               reads=[T0, R1], writes=[OALL])
                        dve.op(lambda e: e.tensor_tensor(out=osq[:, 0:N], in0=o_all[:, q0:q0 + N], in1=o_all[:, q0:q0 + N],
                                                         op=ALU.mult), reads=[OALL], writes=[OSQ])

                    def part_b(q0=q0, N=N):
                        pe.op(lambda e: e.matmul(P[:, 5, 0:N], lhsT=ones_f, rhs=osq[:, 0:N], start=True, stop=True),
                              reads=[ONF, OSQ], writes=[PO[1]])
                        dve.op(lambda e: e.tensor_copy(out=ss_all[:, q0:q0 + N], in_=P[:, 5, 0:N]),
                               reads=[PO[1]], writes=[SSALL])
                    pending_a = part_a
                    pending = part_b
                if pending_s0 is not None:
                    pending_s0(); pending_s0 = None
                if pending_a is not None:
                    pending_a(); pending_a = None
                pending(); pending = None
                act.op(lambda e: e.activation(out=ss_all[:, 0:nq_tot], in_=ss_all[:, 0:nq_tot], func=AF.Ln,
                                              bias=epsc[:, 0:1], scale=1.0 / 128), reads=[SSALL, EPSB], writes=[SSALL])
                act.op(lambda e: e.activation(out=ss_all[:, 0:nq_tot], in_=ss_all[:, 0:nq_tot], func=AF.Exp, scale=-0.5),
                       reads=[SSALL], writes=[SSALL])
                dve.op(lambda e: e.scalar_tensor_tensor(out=a_st[:, 0:nq_tot], in0=o_all[:, 0:nq_tot], scalar=lamc[:, 7:8],
                                                        in1=ss_all[:, 0:nq_tot], op0=ALU.mult, op1=ALU.mult),
                       reads=[OALL, SSALL, LAMC], writes=[AST])
                sp.dma(aT_d[l, si, h][:, 0:nq_tot], a_st[:, 0:nq_tot], reads=[AST], writes=[])
            fw.barrier()
            ar.reset(m)

        def pass3(l, si, last):
            m = ar.mark()
            xts = [ar.alloc([128, 4, 1024], F32) for _ in range(2)]; XTS = [Buf("xt0"), Buf("xt1")]
            hTs = [ar.alloc([128, 8, 512], BF16) for _ in range(2)]; HTS = [Buf("hT0"), Buf("hT1")]
            ssa = ar.alloc([128, 12], F32)
            tmp = (ssa[:, 0:4], ssa[:, 4:8], ssa[:, 8:12], ar.alloc([128, 1024], BF16), ar.alloc([128, 1024], BF16),
                   Buf("ss"), Buf("junk"), Buf("xh"))
            zin = ar.alloc([128, 4, 512 + 32], BF16); ZIN = Buf("zin")
            cacc = ar.alloc([128, 4, 512], F32); CACC = [Buf(f"cacc{j}") for j in range(4)]
            csq = ar.alloc([128, 4, 512], F32); CSQ = Buf("csq")
            mean = ar.alloc([128, 512], F32); MEAN = Buf("mean")
            var = ar.alloc([128, 512], F32); VAR = Buf("var")
            rstd = ar.alloc([128, 512], F32); RSTD = Buf("rstd")
            sconv = ar.alloc([128, 4, 512], BF16); SCONV = Buf("sconv")
            attn_t = ar.alloc([128, 8, 512], BF16); ATT = Buf("attn_t")
            gs = [ar.alloc([128, 512], F32) for _ in range(2)]; GS = [Buf("gs0"), Buf("gs1")]
            tt = [ar.alloc([128, 512], F32) for _ in range(2)]; TTB = [Buf("tt0"), Buf("tt1")]
            merged = ar.alloc([128, 8, 512], BF16); MERGED = Buf("merged")
            rtmp = [ar.alloc([128, 512], F32) for _ in range(2)]; RTMP = [Buf("rtmp0"), Buf("rtmp1")]
            aT = ar.alloc([128, 32, 512], BF16); AT = Buf("aT")
            rl = [ar.alloc([128, 512], BF16) for _ in range(2)]; RL = [Buf("rl0"), Buf("rl1")]
            o_st = [ar.alloc([128, 512], F32) for _ in range(2)]; OST = [Buf("ost0"), Buf("ost1")]
            tiles = [t for t in stream_tiles(si) if not (last and t[0] == "ctx")]
            units = []
            for _ in tiles:
                for dp in range(4):
                    units.append(wco_b[l][:, dp * 256:(dp + 1) * 256].rearrange("(k p) c -> p k c", p=128))
                    units.append(wao_b[l][:, dp * 256:(dp + 1) * 256].rearrange("(k p) c -> p k c", p=128))
                    units.append(win_b[l][:, GATE_OFF + dp * 256:GATE_OFF + (dp + 1) * 256].rearrange("(k p) c -> p k c", p=128))
                    units.append(win_b[l][:, GATE_OFF + D + dp * 256:GATE_OFF + D + (dp + 1) * 256].rearrange("(k p) c -> p k c", p=128))
                for hf in range(2):
                    for kh in range(2):
                        units.append(wo_b[l][kh * 512:(kh + 1) * 512, hf * 512:(hf + 1) * 512].rearrange("(k p) c -> p k c", p=128))
                for fu in range(16):
                    units.append(w1_b[l][:, fu * 256:(fu + 1) * 256].rearrange("(k p) c -> p k c", p=128))
                for hf in range(2):
                    for f4 in range(8):
                        units.append(w2_b[l][f4 * 512:(f4 + 1) * 512, hf * 512:(hf + 1) * 512].rearrange("(k p) c -> p k c", p=128))
            sp._wait(WCAST_IN[l].w)
            sp._wait(WCAST_REST[l].w)
            ws.extend(units)
            def conv_stage(tile):
                kind, t0, T, tok0 = tile
                lo, hi = (0, S) if kind == "lat" else (S, TT)
                a = max(lo, tok0 - 15)
                b = min(hi, tok0 + T + 15)
                pool.op(lambda e: e.memset(zin, 0.0), reads=[], writes=[ZIN])
                sp.dma(zin[:, :, a - (tok0 - 15):b - (tok0 - 15)], zT_d[l, si][:, :, a:b].rearrange("j p t -> p j t"),
                       reads=[], writes=[ZIN])
                for tap in range(CK):
                    for j in range(4):
                        wcol = PT2[:, tap * 4 + j:tap * 4 + j + 1]
                        if tap == 0:
                            dve.op(lambda e, j=j, wcol=wcol: e.tensor_scalar(out=cacc[:, j, 0:T], in0=zin[:, j, 0:T], scalar1=wcol,
                                                                             scalar2=PT1[:, 88 + j:89 + j], op0=ALU.mult, op1=ALU.add),
                                   reads=[ZIN, PT2B, PT1B], writes=[CACC[j]])
                        else:
                            dve.op(lambda e, j=j, wcol=wcol, tap=tap: e.scalar_tensor_tensor(
                                out=cacc[:, j, 0:T], in0=zin[:, j, tap:tap + T], scalar=wcol, in1=cacc[:, j, 0:T],
                                op0=ALU.mult, op1=ALU.add), reads=[ZIN, PT2B, CACC[j]], writes=[CACC[j]])

            def front_pre(ti):
                kind, t0, T, tok0 = tiles[ti]
                nb = T // 128
                norm_stats(xts[ti % 2], XTS[ti % 2], nb, tmp)
                for j in range(4):
                    act.op(lambda e, j=j: e.activation(out=csq[:, j, 0:T], in_=cacc[:, j, 0:T], func=AF.Square),
                           reads=[CACC[j]], writes=[CSQ])

            def front_stage(ti):
                kind, t0, T, tok0 = tiles[ti]
                nb = T // 128
                s = si if kind == "lat" else NB
                xt, XT = xts[ti % 2], XTS[ti % 2]
                hT, HT = hTs[0], HTS[0]
                sp.dma(attn_t[:, :, 0:T], aT_d[l, si][:, :, tok0:tok0 + T].rearrange("h p t -> p h t"), reads=[], writes=[ATT])
                norm_T(xt, XT, nb, s, 1, hT, HT, tmp)
                bm_, bq_ = ps_next(), ps_next()
                for j in range(4):
                    f = lambda e, j=j: e.matmul(psf(bm_)[:, 0:T], lhsT=ones_f, rhs=cacc[:, j, 0:T], start=(j == 0), stop=(j == 3))
                    (pe.op if j == 3 else pe.op_noinc)(f, reads=[ONF, CACC[j]], writes=[PB[bm_]])
                for j in range(4):
                    f = lambda e, j=j: e.matmul(psf(bq_)[:, 0:T], lhsT=ones_f, rhs=csq[:, j, 0:T], start=(j == 0), stop=(j == 3))
                    (pe.op if j == 3 else pe.op_noinc)(f, reads=[ONF, CSQ], writes=[PB[bq_]])
                dve.op(lambda e: e.tensor_scalar(out=mean[:, 0:T], in0=psf(bm_)[:, 0:T], scalar1=LN_SCALE, scalar2=None, op0=ALU.mult),
                       reads=[PB[bm_]], writes=[MEAN])
                dve.op(lambda e: e.tensor_tensor(out=var[:, 0:T], in0=mean[:, 0:T], in1=mean[:, 0:T], op=ALU.mult),
                       reads=[MEAN], writes=[VAR])
                dve.op(lambda e: e.scalar_tensor_tensor(out=var[:, 0:T], in0=psf(bq_)[:, 0:T], scalar=LN_SCALE, in1=var[:, 0:T],
                                                        op0=ALU.mult, op1=ALU.subtract), reads=[PB[bq_], VAR], writes=[VAR])
                act.op(lambda e: e.activation(out=rstd[:, 0:T], in_=var[:, 0:T], func=AF.Sqrt, bias=epsc[:, 0:1], scale=1.0),
                       reads=[VAR, EPSB], writes=[RSTD])
                dve.op(lambda e: e.reciprocal(out=rstd[:, 0:T], in_=rstd[:, 0:T]), reads=[RSTD], writes=[RSTD])
                for j in range(4):
                    dve.op(lambda e, j=j: e.tensor_tensor(out=cacc[:, j, 0:T], in0=cacc[:, j, 0:T], in1=mean[:, 0:T], op=ALU.subtract),
                           reads=[CACC[j], MEAN], writes=[CACC[j]])
                    dve.op(lambda e, j=j: e.tensor_tensor(out=cacc[:, j, 0:T], in0=cacc[:, j, 0:T], in1=rstd[:, 0:T], op=ALU.mult),
                           reads=[CACC[j], RSTD], writes=[CACC[j]])
                    act.op(lambda e, j=j: e.activation(out=sconv[:, j, 0:T], in_=cacc[:, j, 0:T], func=AF.Silu,
                                                       scale=PT1[:, 80 + j:81 + j], bias=PT1[:, 84 + j:85 + j]),
                           reads=[CACC[j], PT1B], writes=[SCONV])

            def load_x(ti):
                kind, t0, T, tok0 = tiles[ti]
                rd = []
                if l > 0:
                    rd.append(XRES[si] if kind == "lat" else HCTX[si])
                sp.dma(xts[ti % 2][:, 0:T // 128, :], x_src(l, si, kind, t0, T), reads=rd, writes=[XTS[ti % 2]])

            load_x(0)
            conv_stage(tiles[0])
            front_pre(0)
            front_stage(0)
            cur_stream = None
            for ti, (kind, t0, T, tok0) in enumerate(tiles):
                nb = T // 128
                s = si if kind == "lat" else NB
                if cur_stream != s:
                    build_gbc(s)
                    cur_stream = s
                xt, XT = xts[ti % 2], XTS[ti % 2]
                hT, HT = hTs[0], HTS[0]
                h2T, H2T = hTs[1], HTS[1]
                if ti + 1 < len(tiles):
                    load_x(ti + 1)
                for dp in range(4):
                    wco_u = ws.get()
                    wao_u = ws.get()
                    wgc_u = ws.get()
                    wga_u = ws.get()
                    for d2 in range(2):
                        dc = dp * 2 + d2
                        byc, bya, bgc, bga = ps_next(), ps_next(), ps_next(), ps_next()
                        cc = d2 * 128
                        for k in range(4):
                            f = lambda e, k=k, cc=cc, wv=wco_u[0]: e.matmul(psf(byc)[:, 0:T], lhsT=wv[:, k, cc:cc + 128],
                                                                            rhs=sconv[:, k, 0:T], start=(k == 0), stop=(k == 3))
                            (pe.op if k == 3 else pe.op_noinc)(f, reads=[wco_u[1], SCONV], writes=[PB[byc]])
                        for (bb, wu, rhs_t, RB) in ((bya, wao_u, attn_t, ATT), (bgc, wgc_u, hT, HT), (bga, wga_u, hT, HT)):
                            for k in range(8):
                                f = lambda e, k=k, bb=bb, wv=wu[0], rhs_t=rhs_t, d2=d2: e.matmul(
                                    psf(bb)[:, 0:T], lhsT=wv[:, k, d2 * 128:(d2 + 1) * 128], rhs=rhs_t[:, k, 0:T],
                                    start=(k == 0), stop=(k == 7))
                                (pe.op if k == 7 else pe.op_noinc)(f, reads=[wu[1], RB], writes=[PB[bb]])
                        act.op(lambda e, dc=dc, bgc=bgc: e.activation(out=gs[0][:, 0:T], in_=psf(bgc)[:, 0:T], func=AF.Sigmoid,
                                                                      bias=PT1[:, 64 + dc:65 + dc], scale=1.0),
                               reads=[PB[bgc], PT1B], writes=[GS[0]])
                        act.op(lambda e, dc=dc, bga=bga: e.activation(out=gs[1][:, 0:T], in_=psf(bga)[:, 0:T], func=AF.Sigmoid,
                                                                      bias=PT1[:, 72 + dc:73 + dc], scale=1.0),
                               reads=[PB[bga], PT1B], writes=[GS[1]])
                        dve.op(lambda e, byc=byc: e.tensor_tensor(out=tt[0][:, 0:T], in0=psf(byc)[:, 0:T], in1=gs[0][:, 0:T], op=ALU.mult),
                               reads=[PB[byc], GS[0]], writes=[TTB[0]])
                        dve.op(lambda e, bya=bya: e.tensor_tensor(out=tt[1][:, 0:T], in0=psf(bya)[:, 0:T], in1=gs[1][:, 0:T], op=ALU.mult),
                               reads=[PB[bya], GS[1]], writes=[TTB[1]])
                        dve.op(lambda e, dc=dc: e.tensor_tensor(out=merged[:, dc, 0:T], in0=tt[0][:, 0:T], in1=tt[1][:, 0:T], op=ALU.add),
                               reads=[TTB[0], TTB[1]], writes=[MERGED])
                for hf in range(2):
                    wu = [ws.get() for _ in range(2)]
                    for n in range(nb):
                        b = ps_next()
                        for k in range(8):
                            wv, wb = wu[k // 4]
                            f = lambda e, b=b, k=k, n=n, wv=wv: e.matmul(psf(b), lhsT=merged[:, k, n * 128:(n + 1) * 128],
                                                                         rhs=wv[:, k % 4, :], start=(k == 0), stop=(k == 7))
                            (pe.op if k == 7 else pe.op_noinc)(f, reads=[wb, MERGED], writes=[PB[b]])
                        ri = n % 2
                        dve.op(lambda e, b=b, hf=hf, ri=ri: e.tensor_tensor(out=rtmp[ri], in0=psf(b), in1=gbc[0][:, hf * 512:(hf + 1) * 512],
                                                                            op=ALU.mult), reads=[PB[b], GBC[0]], writes=[RTMP[ri]])
                        dve.op(lambda e, n=n, hf=hf, ri=ri: e.tensor_tensor(out=xt[:, n, hf * 512:(hf + 1) * 512],
                                                                            in0=xt[:, n, hf * 512:(hf + 1) * 512], in1=rtmp[ri], op=ALU.add),
                               reads=[XT, RTMP[ri]], writes=[XT])
                norm_to_hT(xt, XT, nb, s, 2, h2T, H2T, tmp)
                if ti + 1 < len(tiles):
                    conv_stage(tiles[ti + 1])
                for fu in range(16):
                    wv, wb = ws.get()
                    for fc in range(2):
                        fi = fu * 2 + fc
                        b = ps_next()
                        for k in range(8):
                            f = lambda e, b=b, k=k, fc=fc, wv=wv: e.matmul(psf(b)[:, 0:T], lhsT=wv[:, k, fc * 128:(fc + 1) * 128],
                                                                           rhs=h2T[:, k, 0:T], start=(k == 0), stop=(k == 7))
                            (pe.op if k == 7 else pe.op_noinc)(f, reads=[wb, H2T], writes=[PB[b]])
                        ri = fi % 2
                        act.op(lambda e, b=b, ri=ri: e.activation(out=rl[ri][:, 0:T], in_=psf(b)[:, 0:T], func=AF.Relu),
                               reads=[PB[b]], writes=[RL[ri]])
                        act.op(lambda e, fi=fi, ri=ri: e.activation(out=aT[:, fi, 0:T], in_=rl[ri][:, 0:T], func=AF.Square),
                               reads=[RL[ri]], writes=[AT])
                dst_base, DSTB = (out_d, XRES[si]) if kind == "lat" else (hctx_d, HCTX[si])
                if ti + 1 < len(tiles):
                    front_pre(ti + 1)
                for hf in range(2):
                    banks = [ps_next() for _ in range(nb)]
                    for f4 in range(8):
                        wv, wb = ws.get()
                        for n in range(nb):
                            for f_ in range(4):
                                fi = f4 * 4 + f_
                                f = lambda e, n=n, fi=fi, f_=f_, wv=wv: e.matmul(psf(banks[n]), lhsT=aT[:, fi, n * 128:(n + 1) * 128],
                                                                                 rhs=wv[:, f_, :], start=(fi == 0), stop=(fi == 31))
                                (pe.op if (fi == 31 or f_ == 3) else pe.op_noinc)(f, reads=[wb, AT], writes=[PB[banks[n]]])
                    for n in range(nb):
                        ri = n % 2
                        dve.op(lambda e, n=n, hf=hf, ri=ri: e.tensor_tensor(out=rtmp[ri], in0=psf(banks[n]),
                                                                            in1=gbc[1][:, hf * 512:(hf + 1) * 512], op=ALU.mult),
                               reads=[PB[banks[n]], GBC[1]], writes=[RTMP[ri]])
                        dve.op(lambda e, n=n, hf=hf, ri=ri: e.tensor_tensor(out=o_st[ri], in0=xt[:, n, hf * 512:(hf + 1) * 512],
                                                                            in1=rtmp[ri], op=ALU.add),
                               reads=[XT, RTMP[ri]], writes=[OST[ri]])
                        sp.dma(dst_base[si, t0 + n * 128:t0 + (n + 1) * 128, hf * 512:(hf + 1) * 512], o_st[ri],
                               reads=[OST[ri]], writes=[DSTB])
                    if hf == 0 and ti + 1 < len(tiles):
                        front_stage(ti + 1)
            ws.done()
            fw.barrier()
            ar.reset(m)

        for l in range(DEPTH):
            last = (l == DEPTH - 1)
            layer_setup(l)
            for si in range(NB):
                pass1(l, si)
                pass2(l, si, last)
                pass3(l, si, last)
        fw.finish()
    return nc


def rope_tables(S):
    f = np.float32
    rows = S // 64
    row = np.repeat(np.arange(rows, dtype=f), 64)
    col = np.tile(np.arange(64, dtype=f), rows)
    inv = np.power(f(10000.0), -np.arange(16, dtype=f) / f(16)).astype(f)
    ang = np.concatenate([row[:, None] * inv, col[:, None] * inv], axis=-1).astype(f)
    return np.cos(ang).astype(f), np.sin(ang).astype(f)


def make_in_maps(inputs, n_cores, NB):
    S = inputs["x"].shape[1]
    cos, sin = rope_tables(S)
    ident = np.eye(128, dtype=np.float32)
    maps = []
    for ci in range(n_cores):
        sl = slice(ci * NB, (ci + 1) * NB)
        m = {}
        for k, v in inputs.items():
            v = np.asarray(v)
            if k in ("x", "c", "ctx"):
                m[k] = np.ascontiguousarray(v[sl])
            elif k in ("lam_q", "lam_k"):
                m[k] = np.ascontiguousarray(v.reshape(v.shape[0], 128))
            else:
                m[k] = np.ascontiguousarray(v)
        m["ident"] = ident
        m["rope_cos"] = cos
        m["rope_sin"] = sin
        maps.append(m)
    return maps


def kernel(**inputs):
    x = np.asarray(inputs["x"])
    B, S, _ = x.shape
    C = np.asarray(inputs["ctx"]).shape[1]
    DEPTH = np.asarray(inputs["w_in"]).shape[0]
    n_cores = N_CORES if B % N_CORES == 0 else 1
    NB = B // n_cores
    nc = build_program(S, C, NB, DEPTH)
    maps = make_in_maps(inputs, n_cores, NB)
    res = run_bass_kernel_spmd(nc, maps, core_ids=list(range(n_cores)))
    out = np.concatenate([np.asarray(r["out"]) for r in res.results], axis=0)
    return out.astype(np.float32)
```
